# Optimizing a Trainium2 kernel written in Bass

```python
import jax, jax.numpy as jnp
from jax import lax
import numpy as np

D_MODEL = 2048
BATCH = 1
SEQ = 8192
DEPTH = 2

P_DIM = 256
EPS = 1e-6
NEG = -1e30
BRANCH_WIDTH = 2048
N_BRANCH = 3

A_INNER = BRANCH_WIDTH
A_HEAD_DIM = 64
A_HEADS = A_INNER // A_HEAD_DIM
A_GROUPS = 8
A_STATE = 128
A_CONV = 4
A_CONV_CH = A_INNER + 2 * A_GROUPS * A_STATE
A_CHUNK = 128

B_HEADS = 16
B_KV_HEADS = 4
B_HEAD_DIM = BRANCH_WIDTH // B_HEADS
B_KV_DIM = B_KV_HEADS * B_HEAD_DIM
CMP_LEN = 32
CMP_STRIDE = 16
CMP_HIDDEN = 256
SEL_LEN = 64
SEL_TOPK = 16
SEL_LOCAL = 2
WINDOW = 512
Q_BLOCK = 128

C_HEADS = 8
C_HEAD_DIM = BRANCH_WIDTH // C_HEADS
C_CHUNK = 128
ROPE_BASE = 10000.0

IN_SIZES = (
    A_INNER, A_INNER, A_GROUPS * A_STATE, A_GROUPS * A_STATE, A_HEADS,
    BRANCH_WIDTH, B_KV_DIM, B_KV_DIM, B_KV_DIM, B_KV_DIM, B_KV_DIM, B_KV_DIM, 3 * B_HEADS, BRANCH_WIDTH,
    BRANCH_WIDTH, BRANCH_WIDTH, BRANCH_WIDTH, BRANCH_WIDTH,
    N_BRANCH * D_MODEL,
)
IN_DIM = sum(IN_SIZES)

kernel_name = 'hybrid_ssd_nsa_retention_block'


def rms_norm(x, gain=None):
    xf = x.astype(jnp.float32)
    y = xf * lax.rsqrt(jnp.mean(xf * xf, axis=-1, keepdims=True) + EPS)
    if gain is not None:
        y = y * gain.astype(jnp.float32)
    return y.astype(x.dtype)


def masked_softmax(s, mask):
    s = jnp.where(mask, s.astype(jnp.float32), NEG)
    return jax.nn.softmax(s, axis=-1) * mask


def causal_dwconv(u, w, b):
    k, c = w.shape
    y = lax.conv_general_dilated(u, w[:, None, :].astype(u.dtype), window_strides=(1,),
                                 padding=[(k - 1, 0)], dimension_numbers=('NWC', 'WIO', 'NWC'),
                                 feature_group_count=c)
    return y + b.astype(u.dtype)


def ssd_mixer(xs, z, bm, cm, dt, conv_w, conv_b, dt_bias, a_log, d_skip, norm_w):
    b, L, _ = xs.shape
    xbc = jax.nn.silu(causal_dwconv(jnp.concatenate([xs, bm, cm], axis=-1), conv_w, conv_b))
    xs, bm, cm = jnp.split(xbc, [A_INNER, A_INNER + A_GROUPS * A_STATE], axis=-1)
    dt = jax.nn.softplus((dt + dt_bias).astype(jnp.float32))
    a = -jnp.exp(a_log.astype(jnp.float32)) * dt
    nc, r = L // A_CHUNK, A_HEADS // A_GROUPS
    xh = xs.reshape(b, nc, A_CHUNK, A_GROUPS, r, A_HEAD_DIM)
    xdt = xh * dt.reshape(b, nc, A_CHUNK, A_GROUPS, r, 1).astype(xh.dtype)
    bm = bm.reshape(b, nc, A_CHUNK, A_GROUPS, A_STATE)
    cm = cm.reshape(b, nc, A_CHUNK, A_GROUPS, A_STATE)
    a_cs = jnp.cumsum(a.reshape(b, nc, A_CHUNK, A_GROUPS, r), axis=2)
    causal = jnp.tril(jnp.ones((A_CHUNK, A_CHUNK), dtype=bool))[None, None, :, :, None, None]
    seg = a_cs[:, :, :, None] - a_cs[:, :, None, :]
    decay = jnp.exp(jnp.where(causal, seg, -jnp.inf))
    cb = jnp.einsum('bclgn,bcsgn->bclsg', cm, bm)
    y_diag = jnp.einsum('bclsg,bclsgr,bcsgrp->bclgrp', cb, decay, xdt)
    decay_end = jnp.exp(a_cs[:, :, -1:] - a_cs)
    states = jnp.einsum('bclgn,bclgr,bclgrp->bcgrpn', bm, decay_end, xdt)
    chunk_decay = jnp.exp(a_cs[:, :, -1])

    def step(hs, inp):
        s_c, d_c = inp
        return hs * d_c[..., None, None] + s_c, hs

    h0 = jnp.zeros((b, A_GROUPS, r, A_HEAD_DIM, A_STATE), jnp.float32)
    _, prev = lax.scan(step, h0, (states.astype(jnp.float32).swapaxes(0, 1), chunk_decay.swapaxes(0, 1)))
    prev = prev.swapaxes(0, 1)
    y_off = jnp.einsum('bclgn,bcgrpn,bclgr->bclgrp', cm, prev, jnp.exp(a_cs))
    y = y_diag + y_off + xh * d_skip.reshape(A_GROUPS, r, 1)
    y = y.reshape(b, L, A_INNER).astype(xs.dtype)
    return rms_norm(y * jax.nn.silu(z), norm_w)


def compress(kv, pe, w1, w2):
    b, L, hk, d = kv.shape
    n_cmp = (L - CMP_LEN) // CMP_STRIDE + 1
    idx = np.arange(n_cmp)[:, None] * CMP_STRIDE + np.arange(CMP_LEN)[None, :]
    blocks = kv[:, idx] + pe[:, None, :]
    blocks = blocks.transpose(0, 1, 3, 2, 4).reshape(b, n_cmp, hk, CMP_LEN * d)
    return jax.nn.silu(blocks @ w1) @ w2


def nsa_mixer(q, kc, vc, ks, vs, kw, vw, gates, z, cmp_pe, cmp_w1, cmp_w2):
    b, L, _ = q.shape
    r = B_HEADS // B_KV_HEADS
    scale = B_HEAD_DIM ** -0.5
    q = q.reshape(b, L, B_KV_HEADS, r, B_HEAD_DIM)
    kvs = (b, L, B_KV_HEADS, B_HEAD_DIM)
    kc, vc, ks, vs, kw, vw = [t.reshape(kvs) for t in (kc, vc, ks, vs, kw, vw)]
    kc = compress(kc, cmp_pe[0], cmp_w1[0], cmp_w2[0])
    vc = compress(vc, cmp_pe[1], cmp_w1[1], cmp_w2[1])
    n_cmp, n_sel = kc.shape[1], L // SEL_LEN
    top_k = min(SEL_TOPK, n_sel)
    cmp_end = np.arange(n_cmp) * CMP_STRIDE + CMP_LEN - 1
    ci, sj = np.arange(n_cmp)[:, None], np.arange(n_sel)[None, :]
    overlap = jnp.asarray(((ci * CMP_STRIDE < (sj + 1) * SEL_LEN) &
                           (ci * CMP_STRIDE + CMP_LEN > sj * SEL_LEN)).astype(np.float32))
    pad = jnp.zeros((b, WINDOW, B_KV_HEADS, B_HEAD_DIM), kw.dtype)
    kw_p = jnp.concatenate([pad, kw], axis=1)
    vw_p = jnp.concatenate([pad, vw], axis=1)
    gates = jax.nn.sigmoid(gates.astype(jnp.float32)).reshape(b, L, B_KV_HEADS, r, 3)
    bidx = jnp.arange(b)[:, None, None, None]
    hidx = jnp.arange(B_KV_HEADS)[None, None, :, None]
    blk = jnp.arange(n_sel)

    def block(i):
        s0 = i * Q_BLOCK
        qb = lax.dynamic_slice_in_dim(q, s0, Q_BLOCK, axis=1)
        t = s0 + jnp.arange(Q_BLOCK)
        s_c = jnp.einsum('btgrd,bngd->bgrtn', qb, kc) * scale
        p_c = masked_softmax(s_c, cmp_end[None, :] <= t[:, None])
        o_c = jnp.einsum('bgrtn,bngd->btgrd', p_c.astype(vc.dtype), vc)
        imp = jnp.einsum('bgrtn,nj->btgj', p_c, overlap)
        cur = t // SEL_LEN
        eligible = blk[None, :] * SEL_LEN <= t[:, None]
        dist = cur[:, None] - blk[None, :]
        forced = (blk[None, :] == 0) | ((dist >= 0) & (dist < SEL_LOCAL))
        imp = jnp.where(forced[None, :, None, :], jnp.inf,
                        jnp.where(eligible[None, :, None, :], imp, -jnp.inf))
        top_val, top_idx = lax.top_k(imp, top_k)
        tok = (top_idx[..., None] * SEL_LEN + jnp.arange(SEL_LEN)).reshape(b, Q_BLOCK, B_KV_HEADS, top_k * SEL_LEN)
        tok_ok = jnp.repeat(top_val > -jnp.inf, SEL_LEN, axis=-1) & (tok <= t[None, :, None, None])
        k_sel = ks[bidx, tok, hidx]
        v_sel = vs[bidx, tok, hidx]
        s_s = jnp.einsum('btgrd,btgsd->btgrs', qb, k_sel) * scale
        p_s = masked_softmax(s_s, tok_ok[:, :, :, None, :])
        o_s = jnp.einsum('btgrs,btgsd->btgrd', p_s.astype(v_sel.dtype), v_sel)
        kwb = lax.dynamic_slice_in_dim(kw_p, s0, WINDOW + Q_BLOCK, axis=1)
        vwb = lax.dynamic_slice_in_dim(vw_p, s0, WINDOW + Q_BLOCK, axis=1)
        kpos = s0 - WINDOW + jnp.arange(WINDOW + Q_BLOCK)
        m_w = (kpos[None, :] <= t[:, None]) & (kpos[None, :] > t[:, None] - WINDOW) & (kpos[None, :] >= 0)
        s_w = jnp.einsum('btgrd,bsgd->bgrts', qb, kwb) * scale
        p_w = masked_softmax(s_w, m_w)
        o_w = jnp.einsum('bgrts,bsgd->btgrd', p_w.astype(vwb.dtype), vwb)
        gb = lax.dynamic_slice_in_dim(gates, s0, Q_BLOCK, axis=1)
        o = gb[..., 0:1] * o_c + gb[..., 1:2] * o_s + gb[..., 2:3] * o_w
        return o.astype(q.dtype)

    out = lax.map(block, jnp.arange(L // Q_BLOCK))
    out = out.transpose(1, 0, 2, 3, 4, 5).reshape(b, L, BRANCH_WIDTH)
    return out * jax.nn.silu(z)


def rotary(x, pos):
    d = x.shape[-1]
    inv = ROPE_BASE ** (-jnp.arange(0, d, 2, dtype=jnp.float32) / d)
    ang = pos.astype(jnp.float32)[:, None] * inv[None, :]
    cos = jnp.cos(ang)[None, :, None, :]
    sin = jnp.sin(ang)[None, :, None, :]
    xf = x.astype(jnp.float32)
    x1, x2 = xf[..., 0::2], xf[..., 1::2]
    out = jnp.stack([x1 * cos - x2 * sin, x1 * sin + x2 * cos], axis=-1).reshape(x.shape)
    return out.astype(x.dtype)


def retention_mixer(q, k, v, z, gn_w):
    b, L, _ = q.shape
    H, d, T = C_HEADS, C_HEAD_DIM, C_CHUNK
    nc = L // T
    pos = jnp.arange(L)
    q = rotary(q.reshape(b, L, H, d), pos).reshape(b, nc, T, H, d)
    k = (rotary(k.reshape(b, L, H, d), pos) * (d ** -0.5)).reshape(b, nc, T, H, d)
    v = v.reshape(b, nc, T, H, d)
    log_g = jnp.log1p(-jnp.exp2(-5.0 - jnp.arange(H, dtype=jnp.float32)))
    i = jnp.arange(T, dtype=jnp.float32)
    diff = i[:, None] - i[None, :]
    dmat = jnp.where(diff >= 0, jnp.exp(log_g[:, None, None] * jnp.maximum(diff, 0.0)), 0.0)
    s = jnp.einsum('bcihd,bcjhd->bchij', q, k) * dmat
    inner = jnp.einsum('bchij,bcjhe->bcihe', s.astype(v.dtype), v)
    zeta = jnp.exp(log_g[:, None] * (T - 1 - i)[None, :])
    kv = jnp.einsum('bcjhd,hj,bcjhe->bchde', k, zeta, v).astype(jnp.float32)
    chunk_decay = jnp.exp(log_g * T)

    def step(R, kv_c):
        return R * chunk_decay[:, None, None] + kv_c, R

    _, r_prev = lax.scan(step, jnp.zeros((b, H, d, d), jnp.float32), kv.swapaxes(0, 1))
    r_prev = r_prev.swapaxes(0, 1)
    xi = jnp.exp(log_g[:, None] * (i + 1.0)[None, :])
    cross = jnp.einsum('bcihd,hi,bchde->bcihe', q, xi, r_prev)
    o = (inner + cross).reshape(b, L, H, d).astype(jnp.float32)
    mu = jnp.mean(o, axis=-1, keepdims=True)
    var = jnp.mean(jnp.square(o - mu), axis=-1, keepdims=True)
    o = ((o - mu) * lax.rsqrt(var + EPS)).reshape(b, L, H * d) * gn_w.astype(jnp.float32)
    return (jax.nn.silu(z.astype(jnp.float32)) * o).astype(z.dtype)


def setup_inputs(seed: int = 0) -> dict:
    key = jax.random.key(seed)
    kk = jax.random.split(key, 21)
    f32 = jnp.float32

    def nrm(k, shape, fan_in):
        return jax.random.normal(k, shape, f32) * (fan_in ** -0.5)

    def gain(k, shape):
        return 1.0 + 0.02 * jax.random.normal(k, shape, f32)

    dt0 = jnp.exp(jax.random.uniform(kk[7], (DEPTH, A_HEADS), f32, np.log(1e-3), np.log(1e-1)))
    return {
        'x': jax.random.normal(kk[0], (BATCH, SEQ, D_MODEL), f32),
        'p': jax.random.normal(kk[1], (DEPTH, BATCH, SEQ, P_DIM), f32),
        'norm_pre': gain(kk[2], (DEPTH, D_MODEL)),
        'norm_post': gain(kk[3], (DEPTH, D_MODEL)),
        'w_in': nrm(kk[4], (DEPTH, D_MODEL, IN_DIM), D_MODEL),
        'conv_w': nrm(kk[5], (DEPTH, A_CONV, A_CONV_CH), A_CONV),
        'conv_b': 0.01 * jax.random.normal(kk[6], (DEPTH, A_CONV_CH), f32),
        'dt_bias': dt0 + jnp.log(-jnp.expm1(-dt0)),
        'a_log': jnp.log(jax.random.uniform(kk[8], (DEPTH, A_HEADS), f32, 1.0, 16.0)),
        'd_skip': gain(kk[9], (DEPTH, A_HEADS)),
        'ssm_norm': gain(kk[10], (DEPTH, A_INNER)),
        'cmp_pe': 0.02 * jax.random.normal(kk[11], (DEPTH, 2, CMP_LEN, B_HEAD_DIM), f32),
        'cmp_w1': nrm(kk[12], (DEPTH, 2, CMP_LEN * B_HEAD_DIM, CMP_HIDDEN), CMP_LEN * B_HEAD_DIM),
        'cmp_w2': nrm(kk[13], (DEPTH, 2, CMP_HIDDEN, B_HEAD_DIM), CMP_HIDDEN),
        'ret_norm': gain(kk[14], (DEPTH, BRANCH_WIDTH)),
        'w_branch': nrm(kk[15], (DEPTH, N_BRANCH, BRANCH_WIDTH, D_MODEL), BRANCH_WIDTH),
        'w_out': nrm(kk[16], (DEPTH, D_MODEL, D_MODEL), D_MODEL),
        'ple_proj': nrm(kk[17], (DEPTH, P_DIM, D_MODEL), P_DIM),
        'ple_gate': nrm(kk[18], (DEPTH, D_MODEL, D_MODEL), D_MODEL),
        'ple_norm': gain(kk[19], (DEPTH, D_MODEL)),
    }


def reference(x, p, norm_pre, norm_post, w_in, conv_w, conv_b, dt_bias, a_log, d_skip,
              ssm_norm, cmp_pe, cmp_w1, cmp_w2, ret_norm, w_branch, w_out,
              ple_proj, ple_gate, ple_norm):
    b, L, _ = x.shape
    offsets = np.cumsum(IN_SIZES)[:-1].tolist()
    for i in range(DEPTH):
        h = rms_norm(x, norm_pre[i])
        (xa, za, ba, ca, dta, qb, kcb, vcb, ksb, vsb, kwb, vwb, gb, zb,
         qc, kc, vc, zc, gm) = jnp.split(h @ w_in[i], offsets, axis=-1)
        ya = ssd_mixer(xa, za, ba, ca, dta, conv_w[i], conv_b[i], dt_bias[i], a_log[i], d_skip[i], ssm_norm[i])
        yb = nsa_mixer(qb, kcb, vcb, ksb, vsb, kwb, vwb, gb, zb, cmp_pe[i], cmp_w1[i], cmp_w2[i])
        yc = retention_mixer(qc, kc, vc, zc, ret_norm[i])
        u = jnp.einsum('bsjw,jwd->bsjd', jnp.stack([ya, yb, yc], axis=2), w_branch[i])
        gates = jax.nn.sigmoid(gm.astype(jnp.float32)).reshape(b, L, N_BRANCH, D_MODEL)
        merged = jnp.sum(gates * u, axis=2).astype(x.dtype)
        x = x + rms_norm(merged @ w_out[i], norm_post[i])
        e = p[i] @ ple_proj[i]
        g = jax.nn.sigmoid((rms_norm(x) @ ple_gate[i]).astype(jnp.float32)).astype(x.dtype)
        x = x + rms_norm(g * e, ple_norm[i])
    return x
```

```python
import numpy as np
import ml_dtypes
from contextlib import ExitStack, contextmanager
import concourse.bass as bass
import concourse.mybir as mybir
from concourse.bass_utils import run_bass_kernel_spmd

F32 = mybir.dt.float32
BF16 = mybir.dt.bfloat16
AF = mybir.ActivationFunctionType
ALU = mybir.AluOpType
NPBF = ml_dtypes.bfloat16

NDMA_SEM = 8
OUTQ = "sp"
SAME_ENGINE_SYNC = True

D = 2048
T = 8192
NCORE = 8
TS = T // NCORE
DEPTH = 2
EPS = 1e-6
P_DIM = 256
IN_SIZES = (2048, 2048, 1024, 1024, 32, 2048, 512, 512, 512, 512, 512, 512, 48, 2048, 2048, 2048, 2048, 2048, 6144)
OFF = np.concatenate([[0], np.cumsum(IN_SIZES)]).tolist()
(O_XA, O_ZA, O_BA, O_CA, O_DT, O_QB, O_KCB, O_VCB, O_KSB, O_VSB, O_KWB, O_VWB, O_GB, O_ZB, O_QC, O_KC, O_VC, O_ZC,
 O_GM) = OFF[:19]
NFM = 1536
NTM = 1792
NSM = 10
NEG = -30000.0
SCALE_B = 128 ** -0.5


class Buf:
    def __init__(self, name, t, psum=False):
        self.name = name
        self.t = t
        self.psum = psum

    def __getitem__(self, k):
        return self.t[k]

    def k(self, *idx):
        if self.psum:
            return (self.name,)
        return (self.name,) + tuple(idx)


def _key(x):
    if isinstance(x, Buf):
        return (x.name,)
    if isinstance(x, str):
        return (x,)
    return tuple(x)


class Sched:
    ENGS = ("pe", "act", "dve", "pool", "sp")
    CENGS = ("pe", "act", "dve", "pool")

    def __init__(self, nc):
        self.nc = nc
        self.ops = []
        self.es = ExitStack()
        self.pes = None
        self.sems = {e: self.es.enter_context(nc.semaphore("s_" + e)) for e in self.ENGS}
        self.dsems = {e: [self.es.enter_context(nc.semaphore("d_%s%d" % (e, i))) for i in range(NDMA_SEM)]
                      for e in ("sp", "pool", "act")}
        self.cc_sem = self.es.enter_context(nc.semaphore("s_cc"))
        self.cc_cnt = 0
        self.scopes = []
        self.cnt = {e: 0 for e in self.ENGS}
        self.dcnt = {e: 0 for e in self.dsems}
        self.waited = {e: {} for e in self.ENGS}
        self.last_w = {}
        self.readers = {}
        self.nphase = 0
        self.total_ops = 0
        self.eng = {"pe": nc.tensor, "act": nc.scalar, "dve": nc.vector, "pool": nc.gpsimd, "sp": nc.sync}

    def _stack(self, persist):
        if persist or self.pes is None:
            return self.scopes[-1] if self.scopes else self.es
        return self.pes

    @contextmanager
    def scope(self):
        assert self.pes is None
        st = ExitStack()
        self.scopes.append(st)
        try:
            yield
        finally:
            self.scopes.pop()
            st.close()

    def sb(self, name, shape, dt, persist=False):
        self.nalloc = getattr(self, "nalloc", 0) + 1
        name = "sb%d_%s" % (self.nalloc, name)
        t = self._stack(persist).enter_context(self.nc.sbuf_tensor(name, list(shape), dt))
        return Buf(name, t)

    def ps(self, name, shape, dt, persist=False):
        self.nalloc = getattr(self, "nalloc", 0) + 1
        name = "ps%d_%s" % (self.nalloc, name)
        t = self._stack(persist).enter_context(self.nc.psum_tensor(name, list(shape), dt))
        if not hasattr(self, "psum_names"):
            self.psum_names = set()
        self.psum_names.add(name)
        return Buf(name, t, psum=True)

    @contextmanager
    def phase(self):
        self.pes = ExitStack()
        try:
            yield
            self.flush()
        finally:
            self.pes.close()
            self.pes = None

    def add(self, eng, fn, r=(), w=(), dma=False):
        self.ops.append(dict(eng=eng, fn=fn, r=[_key(x) for x in r], w=[_key(x) for x in w], dma=dma))

    def pe(self, fn, r=(), w=()): self.add("pe", fn, r, w)
    def act(self, fn, r=(), w=()): self.add("act", fn, r, w)
    def dve(self, fn, r=(), w=()): self.add("dve", fn, r, w)
    def pool(self, fn, r=(), w=()): self.add("pool", fn, r, w)
    def dma(self, fn, r=(), w=(), q="sp"): self.add(q, fn, r, w, dma=True)

    def cc(self, fn, r=(), w=()):
        self.add("pool", fn, r, w, dma=False)
        self.ops[-1]["cc"] = True

    @staticmethod
    def _ov(a, b):
        n = min(len(a), len(b))
        return a[:n] == b[:n]

    def flush(self):
        nc = self.nc
        ops = self.ops
        self.ops = []
        base = self.total_ops
        allops = getattr(self, "_all", None)
        if allops is None:
            allops = self._all = {}
        last_w, readers = self.last_w, self.readers
        n = len(ops)
        for li, op in enumerate(ops):
            i = base + li
            allops[i] = op
            deps = set()
            psn = getattr(self, "psum_names", ())
            for k in op["r"]:
                for (k2, j) in last_w.get(k[0], ()):
                    if self._ov(k, k2):
                        deps.add(j)
                if k[0] in psn:
                    for (k2, j) in readers.get(k[0], ()):
                        if j >= base and allops[j]["eng"] != op["eng"]:
                            deps.add(j)
            for k in op["w"]:
                for (k2, j) in last_w.get(k[0], ()):
                    if self._ov(k, k2):
                        deps.add(j)
                for (k2, j) in readers.get(k[0], ()):
                    if self._ov(k, k2):
                        deps.add(j)
            deps.discard(i)
            fdeps = []
            for j in deps:
                if j < base:
                    continue
                oj = allops[j]
                if oj["eng"] == op["eng"] and not oj["dma"] and not op["dma"] and not oj.get("cc") and not op.get("cc"):
                    if op["eng"] == "pe" or not SAME_ENGINE_SYNC:
                        continue
                fdeps.append(j)
            op["deps"] = fdeps
            for k in op["w"]:
                lw = last_w.setdefault(k[0], [])
                lw[:] = [(k2, j) for (k2, j) in lw if not (len(k) <= len(k2) and k2[:len(k)] == k)]
                lw.append((k, i))
                rd = readers.setdefault(k[0], [])
                rd[:] = [(k2, j) for (k2, j) in rd if not (len(k) <= len(k2) and k2[:len(k)] == k)]
            for k in op["r"]:
                rd = readers.setdefault(k[0], [])
                rd[:] = [(k2, j) for (k2, j) in rd
                         if not (k2 == k and j >= base and allops[j]["eng"] == op["eng"]
                                 and not allops[j]["dma"] and not op["dma"])]
                rd.append((k, i))
        needed = set()
        for op in ops:
            for j in op["deps"]:
                needed.add(j)
        lastc = {}
        for li, op in enumerate(ops):
            if not op["dma"] and not op.get("cc"):
                lastc[op["eng"]] = base + li
        for e, j in lastc.items():
            needed.add(j)
        bar = []
        if self.nphase > 0:
            for e in self.CENGS:
                if self.cnt[e] > 0:
                    bar.append((self.sems[e], self.cnt[e]))
            if self.cc_cnt > 0:
                bar.append((self.cc_sem, self.cc_cnt))
            for e, lst in self.dsems.items():
                m = self.dcnt[e]
                for si in range(NDMA_SEM):
                    k = (m - si + NDMA_SEM - 1) // NDMA_SEM if m > si else 0
                    if k > 0:
                        bar.append((lst[si], 16 * k))
        for li, op in enumerate(ops):
            i = base + li
            e = op["eng"]
            if op["dma"]:
                m = self.dcnt[e]; self.dcnt[e] += 1
                op["sig"] = (self.dsems[e][m % NDMA_SEM], 16 * (m // NDMA_SEM + 1))
                op["dprev"] = (self.dsems[e][m % NDMA_SEM], 16 * (m // NDMA_SEM)) if m >= NDMA_SEM else None
            elif op.get("cc"):
                self.cc_cnt += 1
                op["sig"] = (self.cc_sem, self.cc_cnt)
            elif i in needed:
                self.cnt[e] += 1
                op["sig"] = (self.sems[e], self.cnt[e])
            else:
                op["sig"] = None
        self.total_ops += n
        self.nphase += 1
        sems_id = {}

        with nc.Block() as block:
            def emit(e, eng):
                waited = self.waited[e]

                def wait(s, v):
                    if waited.get(id(s), 0) >= v:
                        return
                    waited[id(s)] = v
                    eng.wait_ge(s, v)
                for (s, v) in bar:
                    wait(s, v)
                for op in ops:
                    if op["eng"] != e:
                        continue
                    if op["dma"] and op["dprev"] is not None:
                        wait(*op["dprev"])
                    for j in op["deps"]:
                        wait(*allops[j]["sig"])
                    ins = op["fn"](eng)
                    if op["sig"] is not None:
                        if op.get("cc"):
                            ins.then_inc(op["sig"][0])
                        else:
                            ins.then_inc(op["sig"][0], 16 if op["dma"] else 1)

            @block.tensor
            def _(eng): emit("pe", eng)

            @block.scalar
            def _(eng): emit("act", eng)

            @block.vector
            def _(eng): emit("dve", eng)

            @block.gpsimd
            def _(eng): emit("pool", eng)

            @block.sync
            def _(eng): emit("sp", eng)
        for li in range(n):
            op = allops[base + li]
            op["fn"] = None

    def finish(self):
        nc = self.nc
        assert not self.ops
        with nc.Block() as block:
            @block.sync
            def _(eng):
                waited = self.waited["sp"]
                for e in self.CENGS:
                    if self.cnt[e] > 0 and waited.get(id(self.sems[e]), 0) < self.cnt[e]:
                        eng.wait_ge(self.sems[e], self.cnt[e])
                if self.cc_cnt > 0 and waited.get(id(self.cc_sem), 0) < self.cc_cnt:
                    eng.wait_ge(self.cc_sem, self.cc_cnt)
                for e, lst in self.dsems.items():
                    m = self.dcnt[e]
                    for si in range(NDMA_SEM):
                        k = (m - si + NDMA_SEM - 1) // NDMA_SEM if m > si else 0
                        if k > 0 and waited.get(id(lst[si]), 0) < 16 * k:
                            eng.wait_ge(lst[si], 16 * k)
        self.es.close()


def core_cols(c):
    gk = c // 2
    own = [2 * (c % 2), 2 * (c % 2) + 1]
    oth = [r for r in range(4) if r not in own]
    qorder = own + oth
    fm = []
    fm += list(range(O_XA + 256 * c, O_XA + 256 * c + 256))
    fm += list(range(O_BA + 128 * c, O_BA + 128 * c + 128))
    fm += list(range(O_CA + 128 * c, O_CA + 128 * c + 128))
    for r in qorder:
        hh = 4 * gk + r
        fm += list(range(O_QB + 128 * hh, O_QB + 128 * hh + 128))
    for o in (O_KCB, O_VCB, O_KSB, O_KWB):
        fm += list(range(o + 128 * gk, o + 128 * gk + 128))
    tm = []
    tm += list(range(O_ZA + 256 * c, O_ZA + 256 * c + 256))
    for r in own:
        hh = 4 * gk + r
        tm += list(range(O_ZB + 128 * hh, O_ZB + 128 * hh + 128))
    for o in (O_ZC, O_QC, O_KC, O_VC):
        tm += list(range(o + 256 * c, o + 256 * c + 256))
    for o in (O_VSB, O_VWB):
        tm += list(range(o + 128 * gk, o + 128 * gk + 128))
    sm = list(range(O_DT + 4 * c, O_DT + 4 * c + 4))
    for r in own:
        hh = 4 * gk + r
        sm += [O_GB + 3 * hh + b for b in range(3)]
    assert len(fm) == NFM and len(tm) == NTM and len(sm) == NSM
    return np.array(fm), np.array(tm), np.array(sm)


def emit_p2(S, hT_d, wfm_d, wtm_d, FM_d, TM_d, SM_d):
    NW = NTM + 16
    with S.phase():
        Wfm = S.sb("Wfm", [128, 16, NFM], BF16)
        Wtm = S.sb("Wtm", [128, 16, NW], BF16)
        stg = [S.sb("wstg%d" % i, [128, NW], F32) for i in range(3)]
        ns = 0
        for k in range(16):
            for (wd, Wb, ncol) in ((wfm_d, Wfm, NFM), (wtm_d, Wtm, NW)):
                st = stg[ns % 3]; ns += 1
                S.dma(lambda e, st=st, wd=wd, k=k, ncol=ncol: e.dma_start(out=st[:, :ncol], in_=wd[k * 128:(k + 1) * 128, :]),
                      w=[st])
                if ns % 2:
                    S.pool(lambda e, st=st, Wb=Wb, k=k, ncol=ncol: e.tensor_copy(out=Wb[:, k, :], in_=st[:, :ncol]),
                           r=[st], w=[Wb.k(k)])
                else:
                    S.dve(lambda e, st=st, Wb=Wb, k=k, ncol=ncol: e.tensor_copy(out=Wb[:, k, :], in_=st[:, :ncol]),
                          r=[st], w=[Wb.k(k)])
        hb = [S.sb("p2_h%d" % i, [128, 16, 512], BF16) for i in range(2)]
        fms = [S.sb("p2_fm%d" % i, [128, 12, 512], BF16) for i in range(2)]
        tms = [S.sb("p2_tm%d" % i, [128, NTM], BF16) for i in range(2)]
        sms = [S.sb("p2_sm%d" % i, [128, 16], F32) for i in range(2)]
        banks = [S.ps("p2_ps%d" % i, [128, 512], F32) for i in range(8)]
        nb = 0
        nev = 0
        if callable(hT_d):
            hsrc = hT_d
        else:
            hT_v = hT_d.rearrange("(k p) t -> p k t", p=128)
            hsrc = lambda tb, q: hT_v[:, q * 4:(q + 1) * 4, tb * 512:(tb + 1) * 512]
        FM_v = FM_d.rearrange("(f p) t -> p f t", p=128)
        ntm = 0
        import os as _os
        _ntb = int(_os.environ.get('P2_NTB', '16')); _fm = int(_os.environ.get('P2_FM', '1')); _tm = int(_os.environ.get('P2_TM', '1'))
        for tb in range(_ntb):
            h = hb[tb % 2]
            for q in range(4):
                S.dma(lambda e, h=h, q=q, tb=tb: e.dma_start(out=h[:, q * 4:(q + 1) * 4, :], in_=hsrc(tb, q)),
                      w=[h.k(q)])
            fmst = fms[tb % 2]
            for ft in range(12 if _fm else 0):
                ps = banks[nb % 8]; nb += 1
                for k in range(16):
                    S.pe(lambda e, ps=ps, k=k, ft=ft, h=h: e.matmul(out=ps[:], lhsT=Wfm[:, k, ft * 128:(ft + 1) * 128],
                                                                    rhs=h[:, k, :], start=(k == 0), stop=(k == 15)),
                         r=[Wfm.k(k), h.k(k // 4)], w=[ps])
                if nev % 2:
                    S.act(lambda e, ps=ps, fmst=fmst, ft=ft: e.copy(out=fmst[:, ft, :], in_=ps[:]), r=[ps], w=[fmst.k(ft // 4)])
                else:
                    S.dve(lambda e, ps=ps, fmst=fmst, ft=ft: e.tensor_copy(out=fmst[:, ft, :], in_=ps[:]), r=[ps], w=[fmst.k(ft // 4)])
                nev += 1
                if ft % 4 == 3:
                    f0 = ft - 3
                    S.dma(lambda e, fmst=fmst, f0=f0, tb=tb: e.dma_start(out=FM_v[:, f0:f0 + 4, tb * 512:(tb + 1) * 512],
                                                                         in_=fmst[:, f0:f0 + 4, :]),
                          r=[fmst.k(f0 // 4)], w=[("FM_d", tb, f0)], q=OUTQ)
            for tt in range(4 if _tm else 0):
                tmst = tms[ntm % 2]; smst = sms[ntm % 2]; ntm += 1
                ct = tb * 4 + tt
                for cg in range(int(_os.environ.get('P2_NCG', '4'))):
                    c0 = cg * 512
                    c1 = min(c0 + 512, NTM + 16)
                    ps = banks[nb % 8]; nb += 1
                    for k in range(16):
                        S.pe(lambda e, ps=ps, k=k, h=h, tt=tt, c0=c0, c1=c1: e.matmul(
                            out=ps[:, :c1 - c0], lhsT=h[:, k, tt * 128:(tt + 1) * 128], rhs=Wtm[:, k, c0:c1],
                            start=(k == 0), stop=(k == 15)), r=[Wtm.k(k), h.k(k // 4)], w=[ps])
                    cb = min(c1, NTM)
                    if nev % 2:
                        S.act(lambda e, ps=ps, tmst=tmst, c0=c0, cb=cb: e.copy(out=tmst[:, c0:cb], in_=ps[:, :cb - c0]),
                              r=[ps], w=[tmst.k(cg)])
                    else:
                        S.dve(lambda e, ps=ps, tmst=tmst, c0=c0, cb=cb: e.tensor_copy(out=tmst[:, c0:cb], in_=ps[:, :cb - c0]),
                              r=[ps], w=[tmst.k(cg)])
                    nev += 1
                    if cg == 3 and int(_os.environ.get('P2_SMC', '1')):
                        S.act(lambda e, ps=ps, smst=smst, c0=c0: e.copy(out=smst[:, :16], in_=ps[:, NTM - c0:NTM - c0 + 16]),
                              r=[ps], w=[smst])
                S.dma(lambda e, tmst=tmst, ct=ct: e.dma_start(out=TM_d[ct * 128:(ct + 1) * 128, :], in_=tmst[:]),
                      r=[tmst], w=[("TM_d", ct)], q=OUTQ)
                if int(_os.environ.get('P2_SM', '1')):
                    S.dma(lambda e, smst=smst, ct=ct: e.dma_start(out=SM_d[:, ct, :], in_=smst[:]),
                          r=[smst], w=[("SM_d", ct)], q=OUTQ)


def make_consts():
    c = {}
    c["identb"] = np.eye(128, dtype=np.float32).astype(NPBF)
    c["identf"] = np.eye(128, dtype=np.float32)
    j = np.arange(128)
    c["U"] = (j[:, None] <= j[None, :]).astype(np.float32)
    m = np.where(j[None, :] < j[:, None], NEG, 0.0).astype(np.float32)
    c["mneg4"] = np.tile(m, (1, 4))
    c["tri01"] = (j[None, :] >= j[:, None]).astype(np.float32).astype(NPBF)
    c["rot"] = rot_table()
    c.update(nsa_consts())
    return c


def load_const(S, name, ap_d, shape, dt, persist=False):
    b = S.sb(name, shape, dt, persist=persist)
    S.dma(lambda e: e.dma_start(out=b[:], in_=ap_d), w=[b])
    return b


def emit_ssd(S, FM_d, TM_d, SM_d, CD, yT_d, nchunk=64):
    FM_v = FM_d.rearrange("(f p) t -> p f t", p=128)
    yT_v = yT_d.rearrange("(f p) t -> p f t", p=128)
    with S.phase():
        identb = load_const(S, "identb", CD["identb"], [128, 128], BF16)
        identf = load_const(S, "identf", CD["identf"], [128, 128], F32)
        U = load_const(S, "Utri", CD["U"], [128, 128], F32)
        mneg4 = load_const(S, "mneg4", CD["mneg4"], [128, 512], F32)
        ssdp = load_const(S, "ssdp", CD["ssdp"], [128, 16], F32)
        convp = load_const(S, "convp", CD["convp"], [128, 4, 5], F32)
        sm = load_const(S, "small", SM_d, [128, 64, 16], F32)
        dt = S.sb("dt", [128, 64, 4], F32)
        a = S.sb("a_t", [128, 64, 4], F32)
        negA = S.sb("negA", [128, 4], F32)
        S.dve(lambda e: e.tensor_tensor(out=dt[:], in0=sm[:, :, 0:4], in1=ssdp[:, 0:4].unsqueeze(1).to_broadcast([128, 64, 4]),
                                        op=ALU.add), r=[sm, ssdp], w=[dt])
        S.act(lambda e: e.activation(out=dt[:], in_=dt[:], func=AF.Exp), r=[dt], w=[dt])
        S.act(lambda e: e.activation(out=dt[:], in_=dt[:], func=AF.Ln, bias=1.0), r=[dt], w=[dt])
        S.act(lambda e: e.activation(out=negA[:], in_=ssdp[:, 4:8], func=AF.Exp), r=[ssdp], w=[negA])
        S.dve(lambda e: e.tensor_scalar(out=negA[:], in0=negA[:], scalar1=-1.0, scalar2=None, op0=ALU.mult), r=[negA], w=[negA])
        S.dve(lambda e: e.tensor_tensor(out=a[:], in0=dt[:], in1=negA[:].unsqueeze(1).to_broadcast([128, 64, 4]), op=ALU.mult),
              r=[dt, negA], w=[a])
        Dg = S.sb("Dg", [128, 16, 128], BF16)
        for tl in range(4):
            for tap in range(4):
                S.dve(lambda e, tl=tl, tap=tap: e.tensor_scalar(out=Dg[:, tl * 4 + tap, :], in0=identf[:], scalar1=convp[:, tl, tap:tap + 1],
                                                                scalar2=None, op0=ALU.mult), r=[identf, convp], w=[Dg.k(tl * 4 + tap)])
        prev = S.sb("prev", [128, 256], F32)
        prevb = S.sb("prevb", [128, 256], BF16)
        S.dve(lambda e: e.memset(prev[:], 0.0), w=[prev])
        S.dve(lambda e: e.memset(prevb[:], 0.0), w=[prevb])
        xin = [S.sb("xin%d" % i, [128, 4, 515], BF16) for i in range(2)]
        xcs = [S.sb("xc%d" % i, [128, 4, 512], BF16) for i in range(2)]
        yst = [S.sb("yst%d" % i, [128, 2, 512], BF16) for i in range(2)]
        xdts = [S.sb("xdt%d" % i, [128, 256], BF16) for i in range(2)]
        xdtes = [S.sb("xdte%d" % i, [128, 256], BF16) for i in range(2)]
        xsk = [S.sb("xsk%d" % i, [128, 256], F32) for i in range(2)]
        bmt = [S.sb("bmt%d" % i, [128, 128], BF16) for i in range(2)]
        abcs = [S.sb("abc%d" % i, [128, 4, 128], F32) for i in range(2)]
        negCs = [S.sb("negC%d" % i, [128, 4], F32) for i in range(2)]
        eacss = [S.sb("eacs%d" % i, [128, 4], F32) for i in range(2)]
        cdecs = [S.sb("cdec%d" % i, [128, 4], F32) for i in range(2)]
        decTs = [S.sb("decT%d" % i, [128, 4, 128], F32) for i in range(2)]
        LTs = [S.sb("LT%d" % i, [128, 4, 128], BF16) for i in range(2)]
        yds = [S.sb("yd%d" % i, [128, 256], F32) for i in range(2)]
        y1s = [S.sb("y1_%d" % i, [128, 256], F32) for i in range(2)]
        y2s = [S.sb("y2_%d" % i, [128, 256], F32) for i in range(2)]
        zts = [S.sb("zt%d" % i, [128, 256], BF16) for i in range(2)]
        szs = [S.sb("sz%d" % i, [128, 256], F32) for i in range(2)]
        ygs = [S.sb("yg%d" % i, [128, 256], BF16) for i in range(2)]
        p_tr = [S.ps("ps_tr%d" % i, [128, 512], BF16) for i in range(2)]
        p_seg = [S.ps("ps_seg%d" % i, [128, 512], F32) for i in range(2)]
        p_misc = [S.ps("ps_misc%d" % i, [128, 512], F32) for i in range(2)]
        p_y = S.ps("ps_y", [128, 512], F32)
        p_to = S.ps("ps_to", [128, 2, 128], BF16)
        p_cv = p_seg
        ncv = 0
        for c in range(nchunk):
            tb, q = c // 4, c % 4
            off = q * 128
            par = c % 2
            if q == 0:
                xi = xin[tb % 2]; xc = xcs[tb % 2]
                if tb == 0:
                    S.pool(lambda e, xi=xi: e.memset(xi[:, :, 0:3], 0.0), w=[xi])
                    S.dma(lambda e, xi=xi: e.dma_start(out=xi[:, :, 3:515], in_=FM_v[:, 0:4, 0:512]), w=[xi])
                else:
                    S.dma(lambda e, xi=xi, tb=tb: e.dma_start(out=xi[:, :, :], in_=FM_v[:, 0:4, tb * 512 - 3:tb * 512 + 512]), w=[xi])
                for tl in range(4):
                    pc = p_cv[ncv % 2]; ncv += 1
                    for tap in range(4):
                        S.pe(lambda e, pc=pc, tl=tl, tap=tap, xi=xi: e.matmul(out=pc[:, :], lhsT=Dg[:, tl * 4 + tap, :], rhs=xi[:, tl, tap:tap + 512],
                                                                             start=(tap == 0), stop=(tap == 3)), r=[Dg, xi], w=[pc])
                    S.act(lambda e, pc=pc, tl=tl, xc=xc: e.activation(out=xc[:, tl, :], in_=pc[:, :], func=AF.Silu, bias=convp[:, tl, 4:5]),
                          r=[pc, convp], w=[xc.k(tl)])
            xc = xcs[tb % 2]
            ptr = p_tr[par]; seg = p_seg[par]; misc = p_misc[par]
            xdt = xdts[par]; xdte = xdtes[par]; xs_ = xsk[par]; bm = bmt[par]; abc = abcs[par]
            negC = negCs[par]; eacs = eacss[par]; cdec = cdecs[par]; decT = decTs[par]; LT = LTs[par]
            yd = yds[par]; y1 = y1s[par]; y2 = y2s[par]; zt = zts[par]; sz = szs[par]; yg = ygs[par]
            for jx in range(3):
                S.pe(lambda e, ptr=ptr, jx=jx, xc=xc, off=off: e.transpose(out=ptr[:, jx * 128:(jx + 1) * 128], in_=xc[:, jx, off:off + 128],
                                                                           identity=identb[:]), r=[xc.k(jx), identb], w=[ptr.k(jx)])
            S.dve(lambda e, ptr=ptr, xdt=xdt, c=c: e.tensor_tensor(out=xdt[:].rearrange("p (h e) -> p h e", h=4),
                                                                    in0=ptr[:, 0:256].rearrange("p (h e) -> p h e", h=4),
                                                                    in1=dt[:, c, :].unsqueeze(2).to_broadcast([128, 4, 64]), op=ALU.mult),
                  r=[ptr.k(0), ptr.k(1), dt], w=[xdt])
            S.dve(lambda e, ptr=ptr, xs_=xs_: e.tensor_tensor(out=xs_[:].rearrange("p (h e) -> p h e", h=4),
                                                              in0=ptr[:, 0:256].rearrange("p (h e) -> p h e", h=4),
                                                              in1=ssdp[:, 8:12].unsqueeze(2).to_broadcast([128, 4, 64]), op=ALU.mult),
                  r=[ptr.k(0), ptr.k(1), ssdp], w=[xs_])
            S.act(lambda e, ptr=ptr, bm=bm: e.copy(out=bm[:], in_=ptr[:, 256:384]), r=[ptr.k(2)], w=[bm])
            S.pool(lambda e, abc=abc, c=c: e.tensor_copy(out=abc[:], in_=a[:, c, :].unsqueeze(2).to_broadcast([128, 4, 128])),
                   r=[a], w=[abc])
            S.pe(lambda e, misc=misc, c=c: e.matmul(out=misc[:, 128:132], lhsT=U[:], rhs=a[:, c, :], start=True, stop=True),
                 r=[U, a], w=[misc.k("C")])
            S.pe(lambda e, seg=seg: e.matmul(out=seg[:, :], lhsT=identf[:], rhs=mneg4[:], start=True, stop=False),
                 r=[identf, mneg4], w=[seg])
            for h in range(4):
                S.pe(lambda e, seg=seg, h=h, abc=abc: e.matmul(out=seg[:, h * 128:(h + 1) * 128], lhsT=abc[:, h, :], rhs=U[:],
                                                                start=False, stop=(h == 3)), r=[abc, U], w=[seg])
            S.dve(lambda e, misc=misc, negC=negC: e.tensor_scalar(out=negC[:], in0=misc[:, 128:132], scalar1=-1.0, scalar2=None, op0=ALU.mult),
                  r=[misc.k("C")], w=[negC])
            S.act(lambda e, misc=misc, eacs=eacs: e.activation(out=eacs[:], in_=misc[:, 128:132], func=AF.Exp), r=[misc.k("C")], w=[eacs])
            for h in range(4):
                S.act(lambda e, seg=seg, h=h, decT=decT, negC=negC: e.activation(out=decT[:, h, :], in_=seg[:, h * 128:(h + 1) * 128],
                                                                                 func=AF.Exp, bias=negC[:, h:h + 1]),
                      r=[seg, negC], w=[decT.k(h)])
            S.act(lambda e, seg=seg, cdec=cdec: e.activation(out=cdec[:], in_=seg[:, :].rearrange("p (h e) -> p h e", h=4)[:, :, 127],
                                                             func=AF.Exp), r=[seg], w=[cdec])
            S.pe(lambda e, misc=misc, xc=xc, off=off: e.matmul(out=misc[:, 0:128], lhsT=xc[:, 2, off:off + 128], rhs=xc[:, 3, off:off + 128],
                                                               start=True, stop=True), r=[xc.k(2), xc.k(3)], w=[misc.k("cb")])
            S.dve(lambda e, misc=misc, LT=LT, decT=decT: e.tensor_tensor(out=LT[:], in0=misc[:, 0:128].unsqueeze(1).to_broadcast([128, 4, 128]),
                                                                         in1=decT[:], op=ALU.mult), r=[misc.k("cb"), decT], w=[LT])
            S.dve(lambda e, xdt=xdt, xdte=xdte, decT=decT: e.tensor_tensor(out=xdte[:].rearrange("p (h e) -> p h e", h=4),
                                                                           in0=xdt[:].rearrange("p (h e) -> p h e", h=4),
                                                                           in1=decT[:, :, 127:128].to_broadcast([128, 4, 64]), op=ALU.mult),
                  r=[xdt, decT], w=[xdte])
            for h in range(4):
                S.pe(lambda e, h=h, LT=LT, xdt=xdt: e.matmul(out=p_y[:, h * 64:(h + 1) * 64], lhsT=LT[:, h, :], rhs=xdt[:, h * 64:(h + 1) * 64],
                                                             start=True, stop=True), r=[LT, xdt], w=[p_y.k("d")])
            S.pe(lambda e, xc=xc, off=off: e.matmul(out=p_y[:, 256:512], lhsT=xc[:, 3, off:off + 128], rhs=prevb[:], start=True, stop=True),
                 r=[xc.k(3), prevb], w=[p_y.k("o")])
            S.pe(lambda e, misc=misc, bm=bm, xdte=xdte: e.matmul(out=misc[:, 256:512], lhsT=bm[:], rhs=xdte[:], start=True, stop=True),
                 r=[bm, xdte], w=[misc.k("st")])
            S.act(lambda e, yd=yd: e.copy(out=yd[:], in_=p_y[:, 0:256]), r=[p_y.k("d")], w=[yd])
            S.dve(lambda e, y1=y1, eacs=eacs: e.tensor_tensor(out=y1[:].rearrange("p (h e) -> p h e", h=4),
                                                              in0=p_y[:, 256:512].rearrange("p (h e) -> p h e", h=4),
                                                              in1=eacs[:].unsqueeze(2).to_broadcast([128, 4, 64]), op=ALU.mult),
                  r=[p_y.k("o"), eacs], w=[y1])
            S.pool(lambda e, y2=y2, yd=yd, xs_=xs_: e.tensor_tensor(out=y2[:], in0=yd[:], in1=xs_[:], op=ALU.add), r=[yd, xs_], w=[y2])
            S.pool(lambda e, y2=y2, y1=y1: e.tensor_tensor(out=y2[:], in0=y2[:], in1=y1[:], op=ALU.add), r=[y2, y1], w=[y2])
            S.dma(lambda e, zt=zt, c=c: e.dma_start(out=zt[:], in_=TM_d[c * 128:(c + 1) * 128, 0:256]), w=[zt])
            S.act(lambda e, zt=zt, sz=sz: e.activation(out=sz[:], in_=zt[:], func=AF.Silu), r=[zt], w=[sz])
            S.dve(lambda e, yg=yg, y2=y2, sz=sz: e.tensor_tensor(out=yg[:], in0=y2[:], in1=sz[:], op=ALU.mult), r=[y2, sz], w=[yg])
            ys = yst[tb % 2]
            for jx in range(2):
                S.pe(lambda e, jx=jx, yg=yg: e.transpose(out=p_to[:, jx, :], in_=yg[:, jx * 128:(jx + 1) * 128], identity=identb[:]),
                     r=[yg, identb], w=[p_to])
            S.act(lambda e, ys=ys, off=off: e.copy(out=ys[:, :, off:off + 128], in_=p_to[:, :, :]), r=[p_to], w=[ys])
            if q == 3 or c == nchunk - 1:
                S.dma(lambda e, ys=ys, tb=tb: e.dma_start(out=yT_v[:, 0:2, tb * 512:(tb + 1) * 512], in_=ys[:]), r=[ys], w=[("yT_d", "a", tb)],
                      q=OUTQ)
            S.dve(lambda e, cdec=cdec: e.tensor_tensor(out=prev[:].rearrange("p (h e) -> p h e", h=4),
                                                       in0=prev[:].rearrange("p (h e) -> p h e", h=4),
                                                       in1=cdec[:].unsqueeze(2).to_broadcast([128, 4, 64]), op=ALU.mult),
                  r=[prev, cdec], w=[prev])
            S.dve(lambda e, misc=misc: e.tensor_tensor(out=prev[:], in0=prev[:], in1=misc[:, 256:512], op=ALU.add),
                  r=[prev, misc.k("st")], w=[prev])
            S.pool(lambda e: e.tensor_copy(out=prevb[:], in_=prev[:]), r=[prev], w=[prevb])


def ssd_params(c, conv_w, conv_b, dt_bias, a_log, d_skip):
    ssdp = np.zeros((128, 16), np.float32)
    ssdp[:, 0:4] = dt_bias[4 * c:4 * c + 4][None, :]
    ssdp[:, 4:8] = a_log[4 * c:4 * c + 4][None, :]
    ssdp[:, 8:12] = d_skip[4 * c:4 * c + 4][None, :]
    chans = [np.arange(256 * c, 256 * c + 128), np.arange(256 * c + 128, 256 * c + 256),
             2048 + np.arange(128 * c, 128 * c + 128), 3072 + np.arange(128 * c, 128 * c + 128)]
    convp = np.zeros((128, 4, 5), np.float32)
    for tl, ch in enumerate(chans):
        convp[:, tl, 0:4] = conv_w[:, ch].T
        convp[:, tl, 4] = conv_b[ch]
    return ssdp, convp


def core_consts(c, layer, inp):
    d = {}
    ssdp, convp = ssd_params(c, inp["conv_w"][layer], inp["conv_b"][layer], inp["dt_bias"][layer], inp["a_log"][layer],
                             inp["d_skip"][layer])
    d["ssdp"] = ssdp
    d["convp"] = convp
    d["retp"], d["gnw"] = ret_params(c, inp["ret_norm"][layer])
    return d


def emit_ret(S, FM_d, TM_d, SM_d, CD, yT_d, nchunk=64):
    yT_v = yT_d.rearrange("(f p) t -> p f t", p=128)
    with S.phase():
        identb = load_const(S, "identb", CD["identb"], [128, 128], BF16)
        tri01 = load_const(S, "tri01", CD["tri01"], [128, 128], BF16)
        retp = load_const(S, "retp", CD["retp"], [128, 8], F32)
        gnw = load_const(S, "gnw", CD["gnw"], [128, 256], F32)
        R = S.sb("R", [128, 2, 256], F32)
        Rg = S.sb("Rg", [128, 2, 256], BF16)
        Rt = S.sb("Rt", [128, 2, 256], F32)
        S.dve(lambda e: e.memset(R[:], 0.0), w=[R])
        S.dve(lambda e: e.memset(Rg[:], 0.0), w=[Rg])
        tmcs = [S.sb("tmc%d" % i, [128, 1024], BF16) for i in range(2)]
        rts = [S.sb("rt%d" % i, [128, 256], F32) for i in range(2)]
        m1s = [S.sb("m1_%d" % i, [128, 128, 2], F32) for i in range(2)]
        m2s = [S.sb("m2_%d" % i, [128, 128, 2], F32) for i in range(2)]
        qrs = [S.sb("qr%d" % i, [128, 128, 2], BF16) for i in range(2)]
        krs = [S.sb("kr%d" % i, [128, 128, 2], BF16) for i in range(2)]
        qkTs = [S.sb("qkT%d" % i, [128, 4, 128], BF16) for i in range(2)]
        sTms = [S.sb("sTm%d" % i, [128, 128], BF16) for i in range(2)]
        st6 = [S.sb("st6_%d" % i, [128, 6], F32) for i in range(2)]
        mvs = [S.sb("mv%d" % i, [128, 2], F32) for i in range(2)]
        rss = [S.sb("rs%d" % i, [128, 2], F32) for i in range(2)]
        ons = [S.sb("on%d" % i, [128, 256], F32) for i in range(2)]
        szs = [S.sb("rsz%d" % i, [128, 256], F32) for i in range(2)]
        ycs = [S.sb("yc%d" % i, [128, 256], BF16) for i in range(2)]
        yst = [S.sb("ryst%d" % i, [128, 2, 512], BF16) for i in range(2)]
        p_tr = [S.ps("rp_tr%d" % i, [128, 4, 128], BF16) for i in range(2)]
        p_s = S.ps("rp_s", [128, 512], F32)
        p_o = [S.ps("rp_o%d" % i, [128, 512], F32) for i in range(2)]
        p_kvs = [S.ps("rp_kv%d" % i, [128, 512], F32) for i in range(2)]
        p_to = S.ps("rp_to", [128, 2, 128], BF16)
        def chunk_stages(c):
          st_ = {}

          def stageA():
            tb, q4 = c // 4, c % 4
            off = q4 * 128
            par = c % 2
            tmc = tmcs[par]; rt = rts[par]; qkT = qkTs[par]; sTm = sTms[par]
            ptr = p_tr[par]; po = p_o[par]; p_kv = p_kvs[par]
            S.dma(lambda e, tmc=tmc, c=c: e.dma_start(out=tmc[:], in_=TM_d[c * 128:(c + 1) * 128, 512:1536]), w=[tmc])
            S.dma(lambda e, rt=rt, c=c: e.dma_start(out=rt[:], in_=CD["rot"][c * 128:(c + 1) * 128, :]), w=[rt])
            xr = {}
            for nm, c0, gi, outs in (("q", 256, 0, qrs), ("k", 512, 1, krs)):
                m1 = m1s[0 if nm == "q" else 1]; m2 = m2s[0 if nm == "q" else 1]
                xo = outs[par]
                xr[nm] = xo
                for mm_, t0 in ((m1, 0), (m2, 128)):
                    S.dve(lambda e, mm_=mm_, t0=t0, tmc=tmc, c0=c0, gi=gi, rt=rt: e.scalar_tensor_tensor(
                        out=mm_[:], in0=tmc[:, c0:c0 + 256].rearrange("p (i two) -> p i two", two=2), scalar=retp[:, gi:gi + 1],
                        in1=rt[:, t0:t0 + 128].unsqueeze(2).to_broadcast([128, 128, 2]), op0=ALU.mult, op1=ALU.mult),
                        r=[tmc, retp, rt], w=[mm_])
                S.pool(lambda e, xo=xo, m1=m1, m2=m2: e.tensor_tensor(out=xo[:, :, 0], in0=m1[:, :, 0], in1=m2[:, :, 1], op=ALU.subtract),
                       r=[m1, m2], w=[xo.k(0)])
                S.pool(lambda e, xo=xo, m1=m1, m2=m2: e.tensor_tensor(out=xo[:, :, 1], in0=m2[:, :, 0], in1=m1[:, :, 1], op=ALU.add),
                       r=[m1, m2], w=[xo.k(1)])
            qr, kr = xr["q"], xr["k"]
            for jx in range(2):
                S.pe(lambda e, ptr=ptr, jx=jx, qr=qr: e.transpose(out=ptr[:, jx, :], in_=qr[:].rearrange("p i two -> p (i two)")[:, jx * 128:(jx + 1) * 128],
                                                                  identity=identb[:]), r=[qr, identb], w=[ptr])
            for jx in range(2):
                S.pe(lambda e, ptr=ptr, jx=jx, kr=kr: e.transpose(out=ptr[:, 2 + jx, :], in_=kr[:].rearrange("p i two -> p (i two)")[:, jx * 128:(jx + 1) * 128],
                                                                  identity=identb[:]), r=[kr, identb], w=[ptr])
            S.act(lambda e, ptr=ptr, qkT=qkT: e.copy(out=qkT[:], in_=ptr[:]), r=[ptr], w=[qkT])
            for dc in range(2):
                S.pe(lambda e, dc=dc, qkT=qkT: e.matmul(out=p_s[:, 0:128], lhsT=qkT[:, 2 + dc, :], rhs=qkT[:, dc, :], start=(dc == 0), stop=(dc == 1)),
                     r=[qkT], w=[p_s])
            S.dve(lambda e, sTm=sTm: e.tensor_tensor(out=sTm[:], in0=p_s[:, 0:128], in1=tri01[:], op=ALU.mult), r=[p_s, tri01], w=[sTm])
            for dc in range(2):
                S.pe(lambda e, dc=dc, kr=kr, tmc=tmc, p_kv=p_kv: e.matmul(out=p_kv[:, dc * 256:(dc + 1) * 256],
                                                               lhsT=kr[:].rearrange("p i two -> p (i two)")[:, dc * 128:(dc + 1) * 128],
                                                               rhs=tmc[:, 768:1024], start=True, stop=True), r=[kr, tmc], w=[p_kv])
            sz = szs[par]
            S.act(lambda e, sz=sz, tmc=tmc: e.activation(out=sz[:], in_=tmc[:, 0:256], func=AF.Silu), r=[tmc], w=[sz])
            st_.update(dict(tb=tb, q4=q4, off=off, par=par, tmc=tmc, qkT=qkT, sTm=sTm, po=po, p_kv=p_kv, sz=sz))

          def stageB():
            tb, q4, off, par = st_['tb'], st_['q4'], st_['off'], st_['par']
            tmc, qkT, sTm, po, p_kv = st_['tmc'], st_['qkT'], st_['sTm'], st_['po'], st_['p_kv']
            S.pe(lambda e, po=po, sTm=sTm, tmc=tmc: e.matmul(out=po[:, 0:256], lhsT=sTm[:], rhs=tmc[:, 768:1024], start=True, stop=False),
                 r=[sTm, tmc], w=[po])
            for dc in range(2):
                S.pe(lambda e, po=po, dc=dc, qkT=qkT: e.matmul(out=po[:, 0:256], lhsT=qkT[:, dc, :], rhs=Rg[:, dc, :], start=False, stop=(dc == 1)),
                     r=[qkT, Rg], w=[po])
            S.pool(lambda e: e.tensor_scalar(out=Rt[:], in0=R[:], scalar1=retp[:, 2:3], scalar2=0.0, op0=ALU.mult, op1=ALU.add),
                   r=[R, retp], w=[Rt])
            S.dve(lambda e, p_kv=p_kv: e.scalar_tensor_tensor(out=R[:].rearrange("p a b -> p (a b)"), in0=p_kv[:, :], scalar=retp[:, 3:4],
                                                   in1=Rt[:].rearrange("p a b -> p (a b)"), op0=ALU.mult, op1=ALU.add),
                  r=[p_kv, retp, Rt], w=[R])
            S.pool(lambda e: e.tensor_scalar(out=Rg[:], in0=R[:], scalar1=retp[:, 4:5], scalar2=0.0, op0=ALU.mult, op1=ALU.add),
                   r=[R, retp], w=[Rg])
            s6 = st6[par]; mv = mvs[par]; rs = rss[par]; on = ons[par]; sz = szs[par]; yc = ycs[par]
            S.dve(lambda e, s6=s6, po=po: e.bn_stats(out=s6[:], in_=po[:, 0:256]), r=[po], w=[s6])
            S.dve(lambda e, s6=s6, mv=mv: e.bn_aggr(out=mv[:], in_=s6[:]), r=[s6], w=[mv])
            S.dve(lambda e, mv=mv, rs=rs: e.tensor_scalar(out=rs[:, 0:1], in0=mv[:, 1:2], scalar1=EPS, scalar2=None, op0=ALU.add), r=[mv], w=[rs])
            S.act(lambda e, rs=rs: e.activation(out=rs[:, 0:1], in_=rs[:, 0:1], func=AF.Sqrt), r=[rs], w=[rs])
            S.dve(lambda e, rs=rs: e.reciprocal(out=rs[:, 0:1], in_=rs[:, 0:1]), r=[rs], w=[rs])
            S.dve(lambda e, rs=rs, mv=mv: e.tensor_scalar(out=rs[:, 1:2], in0=mv[:, 0:1], scalar1=rs[:, 0:1], scalar2=-1.0, op0=ALU.mult, op1=ALU.mult),
                  r=[rs, mv], w=[rs])
            S.act(lambda e, on=on, po=po, rs=rs: e.activation(out=on[:], in_=po[:, 0:256], func=AF.Identity, bias=rs[:, 1:2], scale=rs[:, 0:1]),
                  r=[po, rs], w=[on])
            S.pool(lambda e, on=on: e.tensor_tensor(out=on[:], in0=on[:], in1=gnw[:], op=ALU.mult), r=[on, gnw], w=[on])
            S.dve(lambda e, yc=yc, on=on, sz=sz: e.tensor_tensor(out=yc[:], in0=on[:], in1=sz[:], op=ALU.mult), r=[on, sz], w=[yc])
            ys = yst[tb % 2]
            for jx in range(2):
                S.pe(lambda e, jx=jx, yc=yc: e.transpose(out=p_to[:, jx, :], in_=yc[:, jx * 128:(jx + 1) * 128], identity=identb[:]),
                     r=[yc, identb], w=[p_to])
            S.act(lambda e, ys=ys, off=off: e.copy(out=ys[:, :, off:off + 128], in_=p_to[:, :, :]), r=[p_to], w=[ys])
            if q4 == 3 or c == nchunk - 1:
                S.dma(lambda e, ys=ys, tb=tb: e.dma_start(out=yT_v[:, 4:6, tb * 512:(tb + 1) * 512], in_=ys[:]), r=[ys], w=[("yT_d", "c", tb)],
                      q=OUTQ)
          return (stageA, stageB)

        q_ = []
        for c in range(nchunk):
            sA, sB = chunk_stages(c)
            sA()
            q_.append(sB)
            if len(q_) > 1:
                q_.pop(0)()
        while q_:
            q_.pop(0)()


def ret_params(c, ret_norm):
    g = 1.0 - 2.0 ** (-5.0 - c)
    l = np.arange(128, dtype=np.float64)
    retp = np.zeros((128, 8), np.float64)
    retp[:, 0] = g ** l
    retp[:, 1] = g ** (-l) * 256.0 ** -0.5
    retp[:, 2] = g ** 128
    retp[:, 3] = g ** 127
    retp[:, 4] = g
    gnw = np.tile(ret_norm[256 * c:256 * c + 256][None, :], (128, 1))
    return retp.astype(np.float32), np.ascontiguousarray(gnw.astype(np.float32))


def rot_table():
    inv = 10000.0 ** (-np.arange(0, 256, 2, dtype=np.float64) / 256.0)
    ang = (np.arange(T, dtype=np.float32)[:, None] * inv.astype(np.float32)[None, :]).astype(np.float32).astype(np.float64)
    return np.concatenate([np.cos(ang), np.sin(ang)], axis=1).astype(np.float32)


def nsa_consts():
    c = {}
    n = np.arange(128)
    l = np.arange(128)
    cm = np.zeros((128, 16, 128), np.float32)
    for m in range(16):
        vis = (16 * n[:, None]) <= (128 * m + l[None, :] - 31)
        cm[:, m, :] = np.where(vis, 0.0, NEG)
    c["cmpneg"] = cm.astype(NPBF)
    c["causneg"] = np.where(n[:, None] > l[None, :], NEG, 0.0).astype(np.float32).astype(NPBF)
    c["winneg"] = np.where(n[:, None] <= l[None, :], NEG, 0.0).astype(np.float32).astype(NPBF)
    s = np.arange(T)
    c["Esel"] = (s[None, :] // 64 == n[:, None]).astype(np.float32).astype(NPBF)
    nn = np.arange(512)
    jj = np.arange(128)
    ov = ((nn[:, None] * 16 < (jj[None, :] + 1) * 64) & (nn[:, None] * 16 + 32 > jj[None, :] * 64)).astype(np.float32)
    ov[511, :] = 0.0
    ovx = np.concatenate([ov, np.ones((512, 1), np.float32)], axis=1)
    c["ovx"] = np.ascontiguousarray(ovx.reshape(4, 128, 129).transpose(1, 0, 2)).astype(NPBF)
    x = np.arange(256)
    delta = x[None, :] - 126
    hb = (l[:, None] >= 64).astype(np.int64)
    elig = delta <= hb
    forced = (hb - delta >= 0) & (hb - delta < 2)
    c["selbase"] = np.where(forced, 100.0, np.where(elig, 0.0, -100.0)).astype(np.float32)
    return c


def emit_nsa(S, FM_d, TM_d, SM_d, CD, yT_d, nchunk=64, persist=None):
    FM_v = FM_d.rearrange("(f p) t -> p f t", p=128)
    yT_v = yT_d.rearrange("(f p) t -> p f t", p=128)
    if persist is None:
        with S.scope():
            kcmpT = S.sb("kcmpT", [128, 512], BF16, persist=True)
            CV = S.sb("CV", [128, 4, 257], BF16, persist=True)
            emit_nsa(S, FM_d, TM_d, SM_d, CD, yT_d, nchunk=nchunk, persist=(kcmpT, CV))
        return
    kcmpT, CV = persist
    with S.phase():
        identb = load_const(S, "identb", CD["identb"], [128, 128], BF16)
        identf = load_const(S, "identf", CD["identf"], [128, 128], F32)
        kvT = [S.sb("kcT", [128, T], BF16), S.sb("vcT", [128, T], BF16)]
        for kv in range(2):
            for hf in range(2):
                S.dma(lambda e, kv=kv, hf=hf: e.dma_start(out=kvT[kv][:, hf * 4096:(hf + 1) * 4096], in_=FM_v[:, 8 + kv, hf * 4096:(hf + 1) * 4096]),
                      w=[kvT[kv].k(hf)])
        S.dma(lambda e: e.dma_start(out=CV[:, :, 128:257], in_=CD["ovx"]), w=[CV.k("ov")])
        W1 = S.sb("W1", [128, 32, 256], BF16)
        w1s = [S.sb("w1s%d" % i, [128, 8, 256], F32) for i in range(2)]
        W2 = S.sb("W2", [128, 2, 128], BF16)
        w2s = S.sb("w2s", [128, 2, 128], F32)
        petok = S.sb("petok", [32, 128], F32)
        peT = S.sb("peT", [128, 32], BF16)
        cst = S.sb("cst", [128, 2], F32)
        hidT = S.sb("hidT", [128, 2, 512], BF16)
        pp = [S.ps("np_ps%d" % i, [128, 512], F32) for i in range(4)]
        npp = 0
        nst = 0
        for kv in range(2):
            w1v = CD["cmp_w1"][kv].rearrange("(l d) h -> d l h", d=128)
            for l0 in range(0, 32, 8):
                st = w1s[nst % 2]; nst += 1
                S.dma(lambda e, st=st, l0=l0, w1v=w1v: e.dma_start(out=st[:], in_=w1v[:, l0:l0 + 8, :]), w=[st])
                if nst % 2:
                    S.dve(lambda e, st=st, l0=l0: e.tensor_copy(out=W1[:, l0:l0 + 8, :], in_=st[:]), r=[st], w=[W1.k(l0 // 8)])
                else:
                    S.pool(lambda e, st=st, l0=l0: e.tensor_copy(out=W1[:, l0:l0 + 8, :], in_=st[:]), r=[st], w=[W1.k(l0 // 8)])
            S.dma(lambda e, kv=kv: e.dma_start(out=w2s[:], in_=CD["cmp_w2"][kv].rearrange("(hc p) d -> p hc d", p=128)), w=[w2s])
            S.dve(lambda e: e.tensor_copy(out=W2[:], in_=w2s[:]), r=[w2s], w=[W2])
            S.dma(lambda e, kv=kv: e.dma_start(out=petok[:], in_=CD["cmp_pe"][kv]), w=[petok])
            p0 = pp[npp % 4]; npp += 1
            S.pe(lambda e, p0=p0: e.transpose(out=p0[:, 0:32], in_=petok[:, :], identity=identf[0:32, 0:32]), r=[petok, identf], w=[p0])
            S.act(lambda e, p0=p0: e.copy(out=peT[:], in_=p0[:, 0:32]), r=[p0], w=[peT])
            p1 = pp[npp % 4]; npp += 1
            for hc in range(2):
                for l in range(32):
                    S.pe(lambda e, p1=p1, hc=hc, l=l: e.matmul(out=p1[:, hc:hc + 1], lhsT=W1[:, l, hc * 128:(hc + 1) * 128], rhs=peT[:, l:l + 1],
                                                               start=(l == 0), stop=(l == 31)), r=[W1.k(l // 8), peT], w=[p1])
            S.act(lambda e, p1=p1: e.copy(out=cst[:], in_=p1[:, 0:2]), r=[p1], w=[cst])
            S.dve(lambda e: e.memset(hidT[:, :, 511:512], 0.0), w=[hidT.k("z")])
            for hc in range(2):
                p2 = pp[npp % 4]; npp += 1
                for l in range(32):
                    S.pe(lambda e, p2=p2, hc=hc, l=l, kv=kv: e.matmul(out=p2[:, 0:511], lhsT=W1[:, l, hc * 128:(hc + 1) * 128],
                                                                      rhs=kvT[kv][:, l:l + 16 * 510 + 1:16], start=(l == 0), stop=(l == 31)),
                         r=[W1.k(l // 8), kvT[kv]], w=[p2])
                S.act(lambda e, p2=p2, hc=hc: e.activation(out=hidT[:, hc, 0:511], in_=p2[:, 0:511], func=AF.Silu, bias=cst[:, hc:hc + 1]),
                      r=[p2, cst], w=[hidT.k(hc)])
            if kv == 0:
                p3 = pp[npp % 4]; npp += 1
                for hc in range(2):
                    S.pe(lambda e, p3=p3, hc=hc: e.matmul(out=p3[:, 0:512], lhsT=W2[:, hc, :], rhs=hidT[:, hc, :], start=(hc == 0), stop=(hc == 1)),
                         r=[W2, hidT], w=[p3])
                S.act(lambda e, p3=p3: e.copy(out=kcmpT[:], in_=p3[:, 0:512]), r=[p3], w=[kcmpT])
            else:
                for nk in range(4):
                    p3 = pp[npp % 4]; npp += 1
                    for hc in range(2):
                        S.pe(lambda e, p3=p3, hc=hc, nk=nk: e.matmul(out=p3[:, 0:128], lhsT=hidT[:, hc, nk * 128:(nk + 1) * 128], rhs=W2[:, hc, :],
                                                                     start=(hc == 0), stop=(hc == 1)), r=[W2, hidT], w=[p3])
                    S.act(lambda e, p3=p3, nk=nk: e.copy(out=CV[:, nk, 0:128], in_=p3[:, 0:128]), r=[p3], w=[CV.k("v", nk)])
    with S.phase():
        identb = load_const(S, "identb", CD["identb"], [128, 128], BF16)
        identf = load_const(S, "identf", CD["identf"], [128, 128], F32)
        cmpneg = load_const(S, "cmpneg", CD["cmpneg"], [128, 16, 128], BF16)
        causneg = load_const(S, "causneg", CD["causneg"], [128, 128], BF16)
        winneg = load_const(S, "winneg", CD["winneg"], [128, 128], BF16)
        selbase = load_const(S, "selbase", CD["selbase"], [128, 256], F32)
        sm = load_const(S, "small", SM_d, [128, 64, 16], F32)
        Esel = S.sb("Esel", [128, T], BF16)
        ksT = S.sb("ksT", [128, T], BF16)
        kwT = S.sb("kwT", [128, T], BF16)
        for hf in range(2):
            sl = slice(hf * 4096, (hf + 1) * 4096)
            S.dma(lambda e, sl=sl: e.dma_start(out=Esel[:, sl], in_=CD["Esel"][:, sl]), w=[Esel.k(hf)])
            S.dma(lambda e, sl=sl: e.dma_start(out=ksT[:, sl], in_=FM_v[:, 10, sl]), w=[ksT.k(hf)])
            S.dma(lambda e, sl=sl: e.dma_start(out=kwT[:, sl], in_=FM_v[:, 11, sl]), w=[kwT.k(hf)])
        VS = S.sb("VS", [128, 64, 129], BF16)
        VW = S.sb("VW", [128, 64, 129], BF16)
        for (V, c0) in ((VS, 1536), (VW, 1664)):
            S.dve(lambda e, V=V: e.memset(V[:, :, 128:129], 1.0), w=[V.k("one")])
            for g8 in range(8):
                S.dma(lambda e, V=V, c0=c0, g8=g8: e.dma_start(
                    out=V[:, g8 * 8:(g8 + 1) * 8, 0:128],
                    in_=TM_d[g8 * 1024:(g8 + 1) * 1024, c0:c0 + 128].rearrange("(kt p) d -> p kt d", p=128)), w=[V.k("v", g8)])
        gsig = S.sb("gsig", [128, 64, 6], F32)
        S.act(lambda e: e.activation(out=gsig[:], in_=sm[:, :, 4:10], func=AF.Sigmoid), r=[sm], w=[gsig])
        qTbs = [S.sb("qTb%d" % i, [128, 4, 128], BF16) for i in range(2)]
        zts = [S.sb("nzt%d" % i, [128, 256], BF16) for i in range(2)]
        Es = [S.sb("E%d" % i, [128, 512], BF16) for i in range(4)]
        Ps = [S.sb("P%d" % i, [128, 512], BF16) for i in range(4)]
        rscA = S.sb("rscA", [128, 2], F32)
        rscB = S.sb("rscB", [128, 2], F32)
        imp = S.sb("imp", [128, 128], F32)
        imp2 = S.sb("imp2", [128, 128], F32)
        m8a = S.sb("m8a", [128, 8], F32)
        m8b = S.sb("m8b", [128, 8], F32)
        sel = S.sb("sel", [128, 128], F32)
        biasT = S.sb("biasT", [128, 128], BF16)
        coefc = S.sb("coefc", [128, 2], F32)
        coef = S.sb("coef", [128, 4], F32)
        os_ = [S.sb("o_acc%d" % i, [128, 2, 128], F32) for i in range(2)]
        sz = S.sb("nsz", [128, 256], F32)
        yb = S.sb("yb", [128, 256], BF16)
        yst = [S.sb("nyst%d" % i, [128, 2, 512], BF16) for i in range(2)]
        sbk = [S.ps("n_s%d" % i, [128, 512], F32) for i in range(3)]
        ocA = S.ps("n_ocA", [128, 512], F32)
        ocB = S.ps("n_ocB", [128, 512], F32)
        osb = S.ps("n_os", [128, 512], F32)
        owb = S.ps("n_ow", [128, 512], F32)
        p_t = S.ps("n_pt", [128, 512], F32)
        p_t2v = p_t[:, 256:384].bitcast(BF16).rearrange("p (j t) -> p j t", j=2)
        cnt = {"ns": 0, "ne": 0, "np": 0}

        def pipelined(iters, depth=2):
            q_ = []
            for (s1, s2) in iters:
                s1()
                q_.append(s2)
                if len(q_) > depth:
                    q_.pop(0)()
            while q_:
                q_.pop(0)()

        for i in range(nchunk):
            tb, q4 = i // 4, i % 4
            off = q4 * 128
            qTb = qTbs[i % 2]; zt = zts[i % 2]; o = os_[i % 2]
            S.dma(lambda e, qTb=qTb, i=i: e.dma_start(out=qTb[:], in_=FM_v[:, 4:8, i * 128:(i + 1) * 128]), w=[qTb])
            S.dma(lambda e, zt=zt, i=i: e.dma_start(out=zt[:], in_=TM_d[i * 128:(i + 1) * 128, 256:512]), w=[zt])
            nkc = (8 * i + 6) // 128 + 1

            def comp_iter(kc, i=i, qTb=qTb, nkc=nkc):
                st = {}

                def s1():
                    m = i - 16 * kc
                    sb_ = sbk[cnt["ns"] % 3]; cnt["ns"] += 1
                    S.pe(lambda e: e.matmul(out=sb_[:, 0:512], lhsT=kcmpT[:, kc * 128:(kc + 1) * 128],
                                            rhs=qTb[:].rearrange("p r t -> p (r t)"), start=True, stop=(m >= 16)),
                         r=[kcmpT, qTb], w=[sb_])
                    if m < 16:
                        S.pe(lambda e: e.matmul(out=sb_[:, 0:512].rearrange("p (r t) -> p r t", r=4), lhsT=identb[:],
                                                rhs=cmpneg[:, m, :].unsqueeze(1).to_broadcast([128, 4, 128]), start=False, stop=True),
                             r=[identb, cmpneg], w=[sb_])
                    E = Es[cnt["ne"] % 4]; cnt["ne"] += 1
                    S.act(lambda e: e.activation(out=E[:], in_=sb_[:, 0:512], func=AF.Exp, scale=SCALE_B), r=[sb_], w=[E])
                    st["E"] = E

                def s2():
                    E = st["E"]
                    for r in range(4):
                        bank = ocA if r in (0, 2) else ocB
                        c0, w_ = (0, 257) if r < 2 else (257, 129)
                        rhs_lo = 0 if r < 2 else 128
                        S.pe(lambda e, bank=bank, c0=c0, w_=w_, r=r, rhs_lo=rhs_lo: e.matmul(
                            out=bank[:, c0:c0 + w_], lhsT=E[:, r * 128:(r + 1) * 128], rhs=CV[:, kc, rhs_lo:257],
                            start=(kc == 0 and r < 2), stop=(kc == nkc - 1), skip_group_check=True), r=[E, CV], w=[bank])
                return (s1, s2)

            def kt_iter(kT, V, ob, lo, issel, kts, i=i, qTb=qTb):
                st = {}

                def s1():
                    sb_ = sbk[cnt["ns"] % 3]; cnt["ns"] += 1
                    for h_, kt in enumerate(kts):
                        cs = slice(h_ * 256, (h_ + 1) * 256)
                        nmask = (1 if issel else 0) + (1 if kt == i else 0) + (1 if (not issel and kt == i - 4) else 0)
                        S.pe(lambda e, kt=kt, cs=cs, nmask=nmask: e.matmul(out=sb_[:, cs], lhsT=kT[:, kt * 128:(kt + 1) * 128],
                                                                           rhs=qTb[:, 0:2, :].rearrange("p r t -> p (r t)"), start=True, stop=(nmask == 0)),
                             r=[kT.k(kt // 32), qTb], w=[sb_])
                        done = 0
                        if issel:
                            done += 1
                            S.pe(lambda e, kt=kt, cs=cs, last=(done == nmask): e.matmul(
                                out=sb_[:, cs].rearrange("p (r t) -> p r t", r=2), lhsT=Esel[:, kt * 128:(kt + 1) * 128],
                                rhs=biasT[:].unsqueeze(1).to_broadcast([128, 2, 128]), start=False, stop=last), r=[Esel.k(kt // 32), biasT], w=[sb_])
                        if kt == i:
                            done += 1
                            S.pe(lambda e, cs=cs, last=(done == nmask): e.matmul(
                                out=sb_[:, cs].rearrange("p (r t) -> p r t", r=2), lhsT=identb[:],
                                rhs=causneg[:].unsqueeze(1).to_broadcast([128, 2, 128]), start=False, stop=last), r=[identb, causneg], w=[sb_])
                        if (not issel) and kt == i - 4:
                            done += 1
                            S.pe(lambda e, cs=cs, last=(done == nmask): e.matmul(
                                out=sb_[:, cs].rearrange("p (r t) -> p r t", r=2), lhsT=identb[:],
                                rhs=winneg[:].unsqueeze(1).to_broadcast([128, 2, 128]), start=False, stop=last), r=[identb, winneg], w=[sb_])
                    P = Ps[cnt["np"] % 4]; cnt["np"] += 1
                    w_ = 256 * len(kts)
                    S.act(lambda e: e.activation(out=P[:, 0:w_], in_=sb_[:, 0:w_], func=AF.Exp, scale=SCALE_B), r=[sb_], w=[P])
                    st["P"] = P

                def s2():
                    P = st["P"]
                    for h_, kt in enumerate(kts):
                        for r in range(2):
                            S.pe(lambda e, r=r, kt=kt, h_=h_: e.matmul(out=ob[:, r * 129:(r + 1) * 129],
                                                                       lhsT=P[:, h_ * 256 + r * 128:h_ * 256 + (r + 1) * 128], rhs=V[:, kt, :],
                                                                       start=(kt == lo and r == 0), stop=(kt == i), skip_group_check=True),
                                 r=[P, V], w=[ob])
                return (s1, s2)

            def pairs(lo_, hi_):
                ks = list(range(lo_, hi_))
                return [ks[j:j + 2] for j in range(0, len(ks), 2)]

            pipelined([comp_iter(kc) for kc in range(nkc)])
            for (bank, rsc) in ((ocA, rscA), (ocB, rscB)):
                S.dve(lambda e, bank=bank, rsc=rsc: e.tensor_scalar(out=rsc[:], in0=bank[:, 256:386:129], scalar1=1e-30, scalar2=None, op0=ALU.max),
                      r=[bank], w=[rsc])
                S.dve(lambda e, rsc=rsc: e.reciprocal(out=rsc[:], in_=rsc[:]), r=[rsc], w=[rsc])
            so = 126 - 2 * i
            S.dve(lambda e, so=so: e.scalar_tensor_tensor(out=imp[:], in0=ocA[:, 128:256], scalar=rscA[:, 0:1], in1=selbase[:, so:so + 128],
                                                          op0=ALU.mult, op1=ALU.add), r=[ocA, rscA, selbase], w=[imp])
            S.dve(lambda e: e.scalar_tensor_tensor(out=imp[:], in0=ocB[:, 128:256], scalar=rscB[:, 0:1], in1=imp[:], op0=ALU.mult, op1=ALU.add),
                  r=[ocB, rscB, imp], w=[imp])
            S.dve(lambda e: e.scalar_tensor_tensor(out=imp[:], in0=ocA[:, 257:385], scalar=rscA[:, 1:2], in1=imp[:], op0=ALU.mult, op1=ALU.add),
                  r=[ocA, rscA, imp], w=[imp])
            S.dve(lambda e: e.scalar_tensor_tensor(out=imp[:], in0=ocB[:, 257:385], scalar=rscB[:, 1:2], in1=imp[:], op0=ALU.mult, op1=ALU.add),
                  r=[ocB, rscB, imp], w=[imp])
            S.dve(lambda e, i=i: e.tensor_tensor(out=coefc[:], in0=gsig[:, i, 0:6:3], in1=rscA[:, 0:1].to_broadcast([128, 2]), op=ALU.mult),
                  r=[gsig, rscA], w=[coefc])
            S.dve(lambda e, i=i: e.tensor_tensor(out=coefc[:, 1:2], in0=gsig[:, i, 3:4], in1=rscB[:, 0:1], op=ALU.mult), r=[gsig, rscB, coefc], w=[coefc])
            for r in range(2):
                bank = ocA if r == 0 else ocB
                S.dve(lambda e, r=r, bank=bank, o=o: e.tensor_scalar(out=o[:, r, :], in0=bank[:, 0:128], scalar1=coefc[:, r:r + 1], scalar2=None,
                                                                     op0=ALU.mult), r=[bank, coefc], w=[o.k(r)])
            S.dve(lambda e: e.memset(imp[:, 0:1], 100.0), r=[imp], w=[imp])
            S.dve(lambda e: e.max(out=m8a[:], in_=imp[:]), r=[imp], w=[m8a])
            S.dve(lambda e: e.match_replace(out=imp2[:], in_to_replace=m8a[:], in_values=imp[:], imm_value=-50.0), r=[imp, m8a], w=[imp2])
            S.dve(lambda e: e.max(out=m8b[:], in_=imp2[:]), r=[imp2], w=[m8b])
            S.dve(lambda e: e.tensor_scalar(out=sel[:], in0=imp[:], scalar1=m8b[:, 7:8], scalar2=None, op0=ALU.is_ge), r=[imp, m8b], w=[sel])
            S.dve(lambda e: e.scalar_tensor_tensor(out=sel[:], in0=imp[:], scalar=-50.0, in1=sel[:], op0=ALU.is_gt, op1=ALU.mult),
                  r=[imp, sel], w=[sel])
            lo_w = max(0, i - 4)
            pipelined([kt_iter(kwT, VW, owb, lo_w, False, kts) for kts in pairs(lo_w, i + 1)])
            S.pe(lambda e: e.transpose(out=p_t[:, 0:128], in_=sel[:], identity=identf[:]), r=[sel, identf], w=[p_t])
            S.dve(lambda e: e.tensor_scalar(out=biasT[:], in0=p_t[:, 0:128], scalar1=-1.0, scalar2=-NEG, op0=ALU.add, op1=ALU.mult),
                  r=[p_t], w=[biasT])
            pipelined([kt_iter(ksT, VS, osb, 0, True, kts) for kts in pairs(0, i + 1)])
            S.dve(lambda e: e.tensor_scalar(out=coef[:, 0:2], in0=osb[:, 128:258:129], scalar1=1e-30, scalar2=None, op0=ALU.max), r=[osb], w=[coef])
            S.dve(lambda e: e.tensor_scalar(out=coef[:, 2:4], in0=owb[:, 128:258:129], scalar1=1e-30, scalar2=None, op0=ALU.max), r=[owb], w=[coef])
            S.dve(lambda e: e.reciprocal(out=coef[:], in_=coef[:]), r=[coef], w=[coef])
            S.dve(lambda e, i=i: e.tensor_tensor(out=coef[:].rearrange("p (b r) -> p b r", r=2), in0=coef[:].rearrange("p (b r) -> p b r", r=2),
                                                 in1=gsig[:, i, :].rearrange("p (r b) -> p b r", r=2)[:, 1:3, :], op=ALU.mult), r=[coef, gsig], w=[coef])
            for r in range(2):
                S.dve(lambda e, r=r, o=o: e.scalar_tensor_tensor(out=o[:, r, :], in0=osb[:, r * 129:r * 129 + 128], scalar=coef[:, r:r + 1],
                                                                 in1=o[:, r, :], op0=ALU.mult, op1=ALU.add), r=[osb, coef, o.k(r)], w=[o.k(r)])
                S.dve(lambda e, r=r, o=o: e.scalar_tensor_tensor(out=o[:, r, :], in0=owb[:, r * 129:r * 129 + 128], scalar=coef[:, 2 + r:3 + r],
                                                                 in1=o[:, r, :], op0=ALU.mult, op1=ALU.add), r=[owb, coef, o.k(r)], w=[o.k(r)])
            S.act(lambda e, zt=zt: e.activation(out=sz[:], in_=zt[:], func=AF.Silu), r=[zt], w=[sz])
            S.dve(lambda e, o=o: e.tensor_tensor(out=yb[:], in0=o[:].rearrange("p r d -> p (r d)"), in1=sz[:], op=ALU.mult), r=[o, sz], w=[yb])
            ys = yst[tb % 2]
            for jx in range(2):
                S.pe(lambda e, jx=jx: e.transpose(out=p_t2v[:, jx, :], in_=yb[:, jx * 128:(jx + 1) * 128], identity=identb[:]),
                     r=[yb, identb], w=[p_t])
            S.act(lambda e, ys=ys, off=off: e.copy(out=ys[:, :, off:off + 128], in_=p_t2v), r=[p_t], w=[ys])
            if q4 == 3 or i == nchunk - 1:
                S.dma(lambda e, ys=ys, tb=tb: e.dma_start(out=yT_v[:, 2:4, tb * 512:(tb + 1) * 512], in_=ys[:]), r=[ys], w=[("yT_d", "b", tb)],
                      q=OUTQ)


def wtiles(W):
    K, N = W.shape
    return np.ascontiguousarray(W.reshape(K // 128, 128, N // 128, 128).transpose(2, 1, 0, 3))


def vec16(v):
    return np.ascontiguousarray(v.reshape(-1, 128).T.astype(np.float32))


class TokCtx:
    def __init__(self, S, nst=3, nbf=8):
        self.S = S
        self.wst = [S.sb("wst%d" % i, [128, 16, 128], F32) for i in range(nst)]
        self.wbf = [S.sb("wbf%d" % i, [128, 16, 128], BF16) for i in range(nbf)]
        self.sq = [S.sb("sqk%d" % i, [128, 512], BF16) for i in range(2)]
        self.ones = S.sb("ones", [128, 128], BF16)
        S.dve(lambda e: e.memset(self.ones[:], 1.0), w=[self.ones])
        self.nst = self.nbf = self.nsq = 0

    def wtile(self, tile_ap, nk=16):
        S = self.S
        st = self.wst[self.nst % len(self.wst)]; self.nst += 1
        wb = self.wbf[self.nbf % len(self.wbf)]; self.nbf += 1
        S.dma(lambda e: e.dma_start(out=st[:, 0:nk, :], in_=tile_ap), w=[st], q=("sp" if self.nst % 2 else "pool"))
        S.act(lambda e: e.copy(out=wb[:, 0:nk, :], in_=st[:, 0:nk, :]), r=[st], w=[wb])
        return wb

    def sumsq_add(self, ps_ss, src_ap, rkeys, first, last):
        S = self.S
        sq = self.sq[self.nsq % 2]; self.nsq += 1
        S.act(lambda e: e.activation(out=sq[:], in_=src_ap, func=AF.Square), r=rkeys, w=[sq])
        S.pe(lambda e: e.matmul(out=ps_ss[:, 0:512], lhsT=self.ones[:], rhs=sq[:], start=first, stop=last), r=[self.ones, sq], w=[ps_ss])

    def rstd_from(self, ps_ss, out_bc):
        S = self.S
        S.dve(lambda e: e.tensor_scalar(out=out_bc[:], in0=ps_ss[:, 0:512], scalar1=1.0 / D, scalar2=EPS, op0=ALU.mult, op1=ALU.add),
              r=[ps_ss], w=[out_bc])
        S.act(lambda e: e.activation(out=out_bc[:], in_=out_bc[:], func=AF.Sqrt), r=[out_bc], w=[out_bc])
        S.dve(lambda e: e.reciprocal(out=out_bc[:], in_=out_bc[:]), r=[out_bc], w=[out_bc])


def emit_p1(S, xT_d, gpre_d, hT_d):
    xT_v = xT_d.rearrange("(k p) t -> p k t", p=128)
    hT_v = hT_d.rearrange("(k p) t -> p k t", p=128)
    for half in range(TS // 512):
        t0 = half * 512
        with S.phase():
            tc = TokCtx(S, nst=1, nbf=1)
            gpre = load_const(S, "gpre", gpre_d, [128, 16], F32)
            xT = S.sb("p1_xT", [128, 16, 512], F32)
            hT = S.sb("p1_hT", [128, 16, 512], BF16)
            rstd = S.sb("p1_rstd", [128, 512], F32)
            ps_ss = S.ps("p1_ss", [128, 512], F32)
            for q in range(4):
                S.dma(lambda e, q=q: e.dma_start(out=xT[:, q * 4:(q + 1) * 4, :], in_=xT_v[:, q * 4:(q + 1) * 4, t0:t0 + 512]), w=[xT.k(q)])
            for k in range(16):
                tc.sumsq_add(ps_ss, xT[:, k, :], [xT.k(k // 4)], k == 0, k == 15)
            tc.rstd_from(ps_ss, rstd)
            for k in range(16):
                S.dve(lambda e, k=k: e.scalar_tensor_tensor(out=hT[:, k, :], in0=xT[:, k, :], scalar=gpre[:, k:k + 1], in1=rstd[:],
                                                            op0=ALU.mult, op1=ALU.mult), r=[xT.k(k // 4), gpre, rstd], w=[hT.k(k // 4)])
            for q in range(4):
                S.dma(lambda e, q=q: e.dma_start(out=hT_v[:, q * 4:(q + 1) * 4, t0:t0 + 512], in_=hT[:, q * 4:(q + 1) * 4, :]),
                      r=[hT.k(q)], w=[("hT_d", half, q)], q=OUTQ)


def emit_p4(S, xT_d, Yg_d, pT_d, WD, VD, xo_d, pers, nhalf=2, halves=None):
    if pers is None:
        for half in range(nhalf):
            with S.scope():
                xT = S.sb("xT_res", [128, 16, 512], F32, persist=True)
                mT = S.sb("mT", [128, 16, 512], BF16, persist=True)
                emit_p4(S, xT_d, Yg_d, pT_d, WD, VD, xo_d, (xT, mT), nhalf=nhalf, halves=[half])
        return
    xT, mT = pers
    xT_v = xT_d.rearrange("(k p) t -> p k t", p=128)
    xo_v = xo_d.rearrange("(k p) t -> p k t", p=128)
    Y_ind = isinstance(Yg_d, tuple)
    if not Y_ind:
        Y_v = Yg_d.rearrange("(k p) t -> p k t", p=128)
    pT_v = pT_d.rearrange("(k p) t -> p k t", p=128)
    for half in (halves if halves is not None else range(nhalf)):
        t0 = half * 512
        with S.phase():
            tc = TokCtx(S, nst=4, nbf=8)
            gpre = load_const(S, "gpre", VD["gpre"], [128, 16], F32)
            gssm = load_const(S, "gssm", VD["gssm"], [128, 16], F32)
            hT = S.sb("hT", [128, 16, 512], BF16)
            yT = S.sb("yT", [128, 48, 512], BF16)
            rstd = S.sb("rstd", [128, 512], F32)
            rstd_a = S.sb("rstd_a", [128, 512], F32)
            ps_ss = S.ps("ps_ss", [128, 512], F32)
            ps_g = [S.ps("ps_g%d" % i, [128, 512], F32) for i in range(2)]
            ps_u = [S.ps("ps_u%d" % i, [128, 512], F32) for i in range(2)]
            sg = [S.sb("sg%d" % i, [128, 512], F32) for i in range(2)]
            tj = [S.sb("tj%d" % i, [128, 512], F32) for i in range(3)]
            for q in range(4):
                S.dma(lambda e, q=q: e.dma_start(out=xT[:, q * 4:(q + 1) * 4, :], in_=xT_v[:, q * 4:(q + 1) * 4, t0:t0 + 512]), w=[xT.k(q)])
            if Y_ind:
                Grows, yidx_d = Yg_d
                yidx = S.sb("yidx", [128, 96], mybir.dt.int32)
                S.dma(lambda e: e.dma_start(out=yidx[:], in_=yidx_d), w=[yidx])
                for k in range(48):
                    S.dma(lambda e, k=k: e.indirect_dma_start(
                        out=yT[:, k, :], out_offset=None, in_=Grows[:, :],
                        in_offset=bass.IndirectOffsetOnAxis(ap=yidx[:, half * 48 + k:half * 48 + k + 1], axis=0)),
                        r=[yidx], w=[yT.k(k // 4)], q="pool")
            else:
                for q in range(12):
                    S.dma(lambda e, q=q: e.dma_start(out=yT[:, q * 4:(q + 1) * 4, :], in_=Y_v[:, q * 4:(q + 1) * 4, t0:t0 + 512]), w=[yT.k(q)])
            for k in range(16):
                tc.sumsq_add(ps_ss, xT[:, k, :], [xT.k(k // 4)], k == 0, k == 15)
            tc.rstd_from(ps_ss, rstd)
            for k in range(16):
                S.dve(lambda e, k=k: e.scalar_tensor_tensor(out=hT[:, k, :], in0=xT[:, k, :], scalar=gpre[:, k:k + 1], in1=rstd[:],
                                                            op0=ALU.mult, op1=ALU.mult), r=[xT.k(k // 4), gpre, rstd], w=[hT.k(k)])
            for k in range(16):
                tc.sumsq_add(ps_ss, yT[:, k, :], [yT.k(k // 4)], k == 0, k == 15)
            tc.rstd_from(ps_ss, rstd_a)
            for k in range(16):
                S.pool(lambda e, k=k: e.tensor_scalar(out=yT[:, k, :], in0=yT[:, k, :], scalar1=gssm[:, k:k + 1], scalar2=0.0,
                                                      op0=ALU.mult, op1=ALU.add), r=[yT.k(k // 4), gssm], w=[yT.k(k // 4)])
            ng = 0
            for dt_ in range(16):
                for j in range(3):
                    wg_ = tc.wtile(WD["gm"][j * 16 + dt_])
                    wu_ = tc.wtile(WD["wb"][j * 16 + dt_])
                    pg = ps_g[ng % 2]; pu = ps_u[ng % 2]; sgb = sg[ng % 2]; ng += 1
                    for k in range(16):
                        S.pe(lambda e, pg=pg, wg_=wg_, k=k: e.matmul(out=pg[:, :], lhsT=wg_[:, k, :], rhs=hT[:, k, :], start=(k == 0), stop=(k == 15)),
                             r=[wg_, hT.k(k)], w=[pg])
                    for k in range(16):
                        S.pe(lambda e, pu=pu, wu_=wu_, k=k, j=j: e.matmul(out=pu[:, :], lhsT=wu_[:, k, :], rhs=yT[:, 16 * j + k, :],
                                                                        start=(k == 0), stop=(k == 15)), r=[wu_, yT.k((16 * j + k) // 4)], w=[pu])
                    S.act(lambda e, pg=pg, sgb=sgb: e.activation(out=sgb[:], in_=pg[:, :], func=AF.Sigmoid), r=[pg], w=[sgb])
                    S.dve(lambda e, pu=pu, sgb=sgb, j=j: e.tensor_tensor(out=tj[j][:], in0=pu[:, :], in1=sgb[:], op=ALU.mult), r=[pu, sgb], w=[tj[j]])
                S.pool(lambda e: e.tensor_tensor(out=tj[0][:], in0=tj[0][:], in1=rstd_a[:], op=ALU.mult), r=[tj[0], rstd_a], w=[tj[0]])
                S.pool(lambda e: e.tensor_tensor(out=tj[1][:], in0=tj[1][:], in1=tj[2][:], op=ALU.add), r=[tj[1], tj[2]], w=[tj[1]])
                S.pool(lambda e, dt_=dt_: e.tensor_tensor(out=mT[:, dt_, :], in0=tj[0][:], in1=tj[1][:], op=ALU.add), r=[tj[0], tj[1]], w=[mT.k(dt_)])
        with S.phase():
            tc = TokCtx(S, nst=3, nbf=4)
            gpost = load_const(S, "gpost", VD["gpost"], [128, 16], F32)
            oT = S.sb("oT", [128, 16, 512], F32)
            rstd = S.sb("rstd", [128, 512], F32)
            tmp = [S.sb("tmp%d" % i, [128, 512], F32) for i in range(2)]
            ps_ss = S.ps("ps_ss", [128, 512], F32)
            ps_o = [S.ps("ps_o%d" % i, [128, 512], F32) for i in range(2)]
            for d2 in range(16):
                wo_ = tc.wtile(WD["wo"][d2])
                po = ps_o[d2 % 2]
                for k in range(16):
                    S.pe(lambda e, po=po, wo_=wo_, k=k: e.matmul(out=po[:, :], lhsT=wo_[:, k, :], rhs=mT[:, k, :], start=(k == 0), stop=(k == 15)),
                         r=[wo_, mT.k(k)], w=[po])
                S.act(lambda e, po=po, d2=d2: e.copy(out=oT[:, d2, :], in_=po[:, :]), r=[po], w=[oT.k(d2)])
                tc.sumsq_add(ps_ss, oT[:, d2, :], [oT.k(d2)], d2 == 0, d2 == 15)
            tc.rstd_from(ps_ss, rstd)
            for k in range(16):
                tm_ = tmp[k % 2]
                S.dve(lambda e, k=k, tm_=tm_: e.scalar_tensor_tensor(out=tm_[:], in0=oT[:, k, :], scalar=gpost[:, k:k + 1], in1=rstd[:],
                                                                     op0=ALU.mult, op1=ALU.mult), r=[oT.k(k), gpost, rstd], w=[tm_])
                S.pool(lambda e, k=k, tm_=tm_: e.tensor_tensor(out=xT[:, k, :], in0=xT[:, k, :], in1=tm_[:], op=ALU.add), r=[xT.k(k // 4), tm_],
                       w=[xT.k(k // 4)])
        with S.phase():
            tc = TokCtx(S, nst=3, nbf=4)
            gple = load_const(S, "gple", VD["gple"], [128, 16], F32)
            xn = S.sb("xn", [128, 16, 512], BF16)
            geT = S.sb("geT", [128, 16, 512], F32)
            rstd = S.sb("rstd", [128, 512], F32)
            pTf = S.sb("pTf", [128, 2, 512], F32)
            pTb = S.sb("pTb", [128, 2, 512], BF16)
            tmp = [S.sb("tmp%d" % i, [128, 512], F32) for i in range(2)]
            sgc = [S.sb("sgc%d" % i, [128, 512], F32) for i in range(2)]
            ps_ss = S.ps("ps_ss", [128, 512], F32)
            ps_g = [S.ps("ps_g%d" % i, [128, 512], F32) for i in range(2)]
            ps_e = [S.ps("ps_e%d" % i, [128, 512], F32) for i in range(2)]
            S.dma(lambda e: e.dma_start(out=pTf[:], in_=pT_v[:, :, t0:t0 + 512]), w=[pTf])
            S.pool(lambda e: e.tensor_copy(out=pTb[:], in_=pTf[:]), r=[pTf], w=[pTb])
            for k in range(16):
                tc.sumsq_add(ps_ss, xT[:, k, :], [xT.k(k // 4)], k == 0, k == 15)
            tc.rstd_from(ps_ss, rstd)
            for k in range(16):
                S.dve(lambda e, k=k: e.tensor_tensor(out=xn[:, k, :], in0=xT[:, k, :], in1=rstd[:], op=ALU.mult), r=[xT.k(k // 4), rstd], w=[xn.k(k)])
            for d2 in range(16):
                wg_ = tc.wtile(WD["wg"][d2])
                wp_ = tc.wtile(WD["wp"][d2], nk=2)
                pg = ps_g[d2 % 2]; pe_ = ps_e[d2 % 2]; sgb = sgc[d2 % 2]
                for k in range(16):
                    S.pe(lambda e, pg=pg, wg_=wg_, k=k: e.matmul(out=pg[:, :], lhsT=wg_[:, k, :], rhs=xn[:, k, :], start=(k == 0), stop=(k == 15)),
                         r=[wg_, xn.k(k)], w=[pg])
                for k in range(2):
                    S.pe(lambda e, pe_=pe_, wp_=wp_, k=k: e.matmul(out=pe_[:, :], lhsT=wp_[:, k, :], rhs=pTb[:, k, :], start=(k == 0), stop=(k == 1)),
                         r=[wp_, pTb], w=[pe_])
                S.act(lambda e, pg=pg, sgb=sgb: e.activation(out=sgb[:], in_=pg[:, :], func=AF.Sigmoid), r=[pg], w=[sgb])
                S.dve(lambda e, pe_=pe_, sgb=sgb, d2=d2: e.tensor_tensor(out=geT[:, d2, :], in0=pe_[:, :], in1=sgb[:], op=ALU.mult),
                      r=[pe_, sgb], w=[geT.k(d2)])
                tc.sumsq_add(ps_ss, geT[:, d2, :], [geT.k(d2)], d2 == 0, d2 == 15)
            tc.rstd_from(ps_ss, rstd)
            for k in range(16):
                tm_ = tmp[k % 2]
                S.dve(lambda e, k=k, tm_=tm_: e.scalar_tensor_tensor(out=tm_[:], in0=geT[:, k, :], scalar=gple[:, k:k + 1], in1=rstd[:],
                                                                     op0=ALU.mult, op1=ALU.mult), r=[geT.k(k), gple, rstd], w=[tm_])
                S.pool(lambda e, k=k, tm_=tm_: e.tensor_tensor(out=xT[:, k, :], in0=xT[:, k, :], in1=tm_[:], op=ALU.add), r=[xT.k(k // 4), tm_],
                       w=[xT.k(k // 4)])
            for q in range(4):
                S.dma(lambda e, q=q: e.dma_start(out=xo_v[:, q * 4:(q + 1) * 4, t0:t0 + 512], in_=xT[:, q * 4:(q + 1) * 4, :]),
                      r=[xT.k(q)], w=[("xo_d", half, q)], q=OUTQ)


def p4_inputs(inp, L):
    w_in = inp["w_in"][L]
    d = {}
    d["w_gm"] = wtiles(w_in[:, O_GM:O_GM + 3 * D])
    d["w_wb"] = np.concatenate([wtiles(inp["w_branch"][L, j]) for j in range(3)], axis=0)
    d["w_wo"] = wtiles(inp["w_out"][L])
    d["w_wg"] = wtiles(inp["ple_gate"][L])
    d["w_wp"] = wtiles(inp["ple_proj"][L])
    d["v_gpre"] = vec16(inp["norm_pre"][L])
    d["v_gpost"] = vec16(inp["norm_post"][L])
    d["v_gssm"] = vec16(inp["ssm_norm"][L])
    d["v_gple"] = vec16(inp["ple_norm"][L])
    return d


_CONST_CACHE = {}


def shared_consts():
    if "c" not in _CONST_CACHE:
        _CONST_CACHE["c"] = make_consts()
    return _CONST_CACHE["c"]


def _dt_of(v):
    return BF16 if v.dtype == NPBF else F32


def build_LA():
    nc = bass.Bass("TRN2", target_bir_lowering=False)
    xT = nc.dram_tensor("xT", [D, TS], F32, kind="ExternalInput").ap()
    g = nc.dram_tensor("v_gpre", [128, 16], F32, kind="ExternalInput").ap()
    hT = nc.dram_tensor("hT_out", [D, TS], BF16, kind="ExternalOutput").ap()
    S = Sched(nc)
    emit_p1(S, xT, g, hT)
    S.finish()
    return nc


def mixer_const_specs():
    sc = shared_consts()
    specs = {k: (list(v.shape), _dt_of(v)) for k, v in sc.items()}
    specs.update({"ssdp": ([128, 16], F32), "convp": ([128, 4, 5], F32), "retp": ([128, 8], F32), "gnw": ([128, 256], F32),
                  "cmp_w1": ([2, 4096, 256], F32), "cmp_w2": ([2, 256, 128], F32), "cmp_pe": ([2, 32, 128], F32)})
    return specs


def build_LB():
    nc = bass.Bass("TRN2", target_bir_lowering=False)
    hT = nc.dram_tensor("hT", [D, T], BF16, kind="ExternalInput").ap()
    wfm = nc.dram_tensor("wfm", [D, NFM], F32, kind="ExternalInput").ap()
    wtm = nc.dram_tensor("wtm", [D, NTM + 16], F32, kind="ExternalInput").ap()
    CD = {k: nc.dram_tensor(k, shp, dt, kind="ExternalInput").ap() for k, (shp, dt) in mixer_const_specs().items()}
    FM = nc.dram_tensor("FM_s", [NFM, T], BF16, kind="Internal").ap()
    TM = nc.dram_tensor("TM_s", [T, NTM], BF16, kind="Internal").ap()
    SM = nc.dram_tensor("SM_s", [128, 64, 16], F32, kind="Internal").ap()
    yT = nc.dram_tensor("yT", [768, T], BF16, kind="ExternalOutput").ap()
    S = Sched(nc)
    kcmpT = S.sb("kcmpT", [128, 512], BF16, persist=True)
    CV = S.sb("CV", [128, 4, 257], BF16, persist=True)
    emit_p2(S, hT, wfm, wtm, FM, TM, SM)
    emit_ssd(S, FM, TM, SM, CD, yT)
    emit_ret(S, FM, TM, SM, CD, yT)
    emit_nsa(S, FM, TM, SM, CD, yT, persist=(kcmpT, CV))
    S.finish()
    return nc


def build_LC():
    nc = bass.Bass("TRN2", target_bir_lowering=False)
    xT = nc.dram_tensor("xT", [D, TS], F32, kind="ExternalInput").ap()
    Yg = nc.dram_tensor("Yg", [3 * D, TS], BF16, kind="ExternalInput").ap()
    pT = nc.dram_tensor("pT", [P_DIM, TS], F32, kind="ExternalInput").ap()
    WD = {"gm": nc.dram_tensor("w_gm", [48, 128, 16, 128], F32, kind="ExternalInput").ap(),
          "wb": nc.dram_tensor("w_wb", [48, 128, 16, 128], F32, kind="ExternalInput").ap(),
          "wo": nc.dram_tensor("w_wo", [16, 128, 16, 128], F32, kind="ExternalInput").ap(),
          "wg": nc.dram_tensor("w_wg", [16, 128, 16, 128], F32, kind="ExternalInput").ap(),
          "wp": nc.dram_tensor("w_wp", [16, 128, 2, 128], F32, kind="ExternalInput").ap()}
    VD = {k: nc.dram_tensor("v_" + k, [128, 16], F32, kind="ExternalInput").ap() for k in ("gpre", "gpost", "gssm", "gple")}
    xo = nc.dram_tensor("xo", [D, TS], F32, kind="ExternalOutput").ap()
    S = Sched(nc)
    xTs = S.sb("xT_res", [128, 16, 512], F32, persist=True)
    mT = S.sb("mT", [128, 16, 512], BF16, persist=True)
    emit_p4(S, xT, Yg, pT, WD, VD, xo, (xTs, mT))
    S.finish()
    return nc


def lb_inputs(c, L, inp, hT_full):
    w_in = inp["w_in"][L]
    fm, tm, sm = core_cols(c)
    wtm = np.zeros((D, NTM + 16), np.float32)
    wtm[:, :NTM] = w_in[:, tm]
    wtm[:, NTM:NTM + NSM] = w_in[:, sm]
    im = {"hT": hT_full, "wfm": np.ascontiguousarray(w_in[:, fm]), "wtm": wtm}
    im.update(shared_consts())
    im.update(core_consts(c, L, inp))
    im["cmp_w1"] = inp["cmp_w1"][L]
    im["cmp_w2"] = inp["cmp_w2"][L]
    im["cmp_pe"] = inp["cmp_pe"][L]
    return im


def kernel_multi(**inputs):
    inp = {k: np.asarray(v) for k, v in inputs.items()}
    cores = list(range(NCORE))
    xT = np.ascontiguousarray(inp["x"][0].T)
    ncA, ncB, ncC = build_LA(), build_LB(), build_LC()
    for L in range(DEPTH):
        shared4 = p4_inputs(inp, L)
        ims = [{"xT": np.ascontiguousarray(xT[:, c * TS:(c + 1) * TS]), "v_gpre": shared4["v_gpre"]} for c in cores]
        res = run_bass_kernel_spmd(ncA, ims, core_ids=cores)
        hT_full = np.ascontiguousarray(np.concatenate([np.asarray(r["hT_out"]) for r in res.results], axis=1))
        ims = [lb_inputs(c, L, inp, hT_full) for c in cores]
        res = run_bass_kernel_spmd(ncB, ims, core_ids=cores)
        yTs = [np.asarray(r["yT"]) for r in res.results]
        Yfull = np.concatenate([np.concatenate([yTs[c][256 * j:256 * (j + 1)] for c in cores], axis=0) for j in range(3)], axis=0)
        pT = np.ascontiguousarray(inp["p"][L, 0].T)
        ims = []
        for c in cores:
            im = {"xT": np.ascontiguousarray(xT[:, c * TS:(c + 1) * TS]), "Yg": np.ascontiguousarray(Yfull[:, c * TS:(c + 1) * TS]),
                  "pT": np.ascontiguousarray(pT[:, c * TS:(c + 1) * TS])}
            im.update(shared4)
            ims.append(im)
        res = run_bass_kernel_spmd(ncC, ims, core_ids=cores)
        xT = np.ascontiguousarray(np.concatenate([np.asarray(r["xo"]) for r in res.results], axis=1))
    return np.ascontiguousarray(xT.T)[None].astype(np.float32)


PER_LAYER_MIX = {"ssdp": ([128, 16], F32), "convp": ([128, 4, 5], F32), "gnw": ([128, 256], F32),
                 "cmp_w1": ([2, 4096, 256], F32), "cmp_w2": ([2, 256, 128], F32), "cmp_pe": ([2, 32, 128], F32)}
P4_SPECS = {"w_gm": [48, 128, 16, 128], "w_wb": [48, 128, 16, 128], "w_wo": [16, 128, 16, 128], "w_wg": [16, 128, 16, 128],
            "w_wp": [16, 128, 2, 128], "v_gpre": [128, 16], "v_gpost": [128, 16], "v_gssm": [128, 16], "v_gple": [128, 16]}


def build_fused(stop=99):
    nc = bass.Bass("TRN2", target_bir_lowering=False)
    I32 = mybir.dt.int32
    ext = lambda name, shp, dt: nc.dram_tensor(name, list(shp), dt, kind="ExternalInput").ap()
    xT_in = ext("xT", [D, TS], F32)
    xo = nc.dram_tensor("xo", [D, TS], F32, kind="ExternalOutput").ap()
    xres = nc.dram_tensor("xres", [D, TS], F32, kind="Internal").ap()
    hT_loc = nc.dram_tensor("hT_loc", [D, TS], BF16, kind="Internal").ap()
    hT_all = nc.dram_tensor("hT_all", [NCORE * D, TS], BF16, kind="Internal", addr_space="Shared").ap()
    FM = nc.dram_tensor("FM_s", [NFM, T], BF16, kind="Internal").ap()
    TM = nc.dram_tensor("TM_s", [T, NTM], BF16, kind="Internal").ap()
    SM = nc.dram_tensor("SM_s", [128, 64, 16], F32, kind="Internal").ap()
    yT_loc = nc.dram_tensor("yT_loc", [768, T], BF16, kind="Internal").ap()
    Y_all = nc.dram_tensor("Y_all", [NCORE * 768, T], BF16, kind="Internal", addr_space="Shared").ap()
    Y_cp = nc.dram_tensor("Y_cp", [NCORE * 768, T], BF16, kind="Internal").ap()
    yidx_d = ext("yidx", [128, 96], I32)
    sc = shared_consts()
    CDs = {k: ext(k, v.shape, _dt_of(v)) for k, v in sc.items()}
    CDs["retp"] = ext("retp", [128, 8], F32)
    S = Sched(nc)
    pad_ = S.sb("lowpad", [128, 1024], F32, persist=True)
    kcmpT_p = S.sb("kcmpT", [128, 512], BF16, persist=True)
    CV_p = S.sb("CV", [128, 4, 257], BF16, persist=True)
    hv = hT_all.rearrange("(r k p) t -> p r k t", r=NCORE, p=128)
    hsrc = lambda tb, q: hv[:, tb // 2, q * 4:(q + 1) * 4, (tb % 2) * 512:(tb % 2) * 512 + 512]
    Grows = Y_cp.rearrange("r (tb t) -> (r tb) t", t=512)
    groups = [list(range(NCORE))]
    for L in range(DEPTH):
        sfx = "_L%d" % L
        CD = dict(CDs)
        for k, (shp, dt) in PER_LAYER_MIX.items():
            CD[k] = ext(k + sfx, shp, dt)
        wfm = ext("wfm" + sfx, [D, NFM], F32)
        wtm = ext("wtm" + sfx, [D, NTM + 16], F32)
        pT = ext("pT" + sfx, [P_DIM, TS], F32)
        P4 = {k: ext(k + sfx, shp, F32) for k, shp in P4_SPECS.items()}
        WD = {"gm": P4["w_gm"], "wb": P4["w_wb"], "wo": P4["w_wo"], "wg": P4["w_wg"], "wp": P4["w_wp"]}
        VD = {"gpre": P4["v_gpre"], "gpost": P4["v_gpost"], "gssm": P4["v_gssm"], "gple": P4["v_gple"]}
        x_src = xT_in if L == 0 else xres
        x_dst = xo if L == DEPTH - 1 else xres
        emit_p1(S, x_src, VD["gpre"], hT_loc)
        if stop == 0:
            break
        with S.phase():
            S.cc(lambda e: e.collective_compute("AllGather", ALU.bypass, replica_groups=groups, ins=[hT_loc.opt()], outs=[hT_all.opt()]),
                 r=["hT_loc"], w=["hT_all"])
        if stop == 1:
            break
        emit_p2(S, hsrc, wfm, wtm, FM, TM, SM)
        if stop == 2:
            break
        import os as _os
        _skip = _os.environ.get("FUSED_SKIP", "")
        if "ssd" not in _skip:
            emit_ssd(S, FM, TM, SM, CD, yT_loc)
        if "ret" not in _skip:
            emit_ret(S, FM, TM, SM, CD, yT_loc)
        if "nsa" not in _skip:
            emit_nsa(S, FM, TM, SM, CD, yT_loc, persist=(kcmpT_p, CV_p))
        if stop == 3:
            break
        with S.phase():
            S.cc(lambda e: e.collective_compute("AllGather", ALU.bypass, replica_groups=groups, ins=[yT_loc.opt()], outs=[Y_all.opt()]),
                 r=["yT_loc"], w=["Y_all"])
        with S.phase():
            for q in range(48):
                S.dma(lambda e, q=q: e.dma_start(out=Y_cp[q * 128:(q + 1) * 128, :], in_=Y_all[q * 128:(q + 1) * 128, :]),
                      r=["Y_all"], w=[("Y_cp", q)])
        if stop == 4:
            break
        emit_p4(S, x_src, (Grows, yidx_d), pT, WD, VD, x_dst, None)
        if stop == 5:
            break
    if stop < 99:
        with S.phase():
            t = S.sb("dbg_t", [128, 512], F32)
            S.dve(lambda e: e.memset(t[:], 1.0), w=[t])
            S.dma(lambda e: e.dma_start(out=xo[0:128, 0:512], in_=t[:]), r=[t], w=["xo_dbg"])
    S.finish()
    return nc


def y_index(c):
    idx = np.zeros((128, 96), np.int32)
    p = np.arange(128)
    for h in range(2):
        for j in range(3):
            for cs in range(NCORE):
                for rr in range(2):
                    k = 16 * j + 2 * cs + rr
                    idx[:, h * 48 + k] = (cs * 768 + 256 * j + 128 * rr + p) * 16 + (2 * c + h)
    return idx


def fused_inputs(c, inp):
    im = {"xT": np.ascontiguousarray(inp["x"][0][c * TS:(c + 1) * TS].T), "yidx": y_index(c)}
    im.update(shared_consts())
    for L in range(DEPTH):
        sfx = "_L%d" % L
        w_in = inp["w_in"][L]
        fm, tm, sm = core_cols(c)
        wtm = np.zeros((D, NTM + 16), np.float32)
        wtm[:, :NTM] = w_in[:, tm]
        wtm[:, NTM:NTM + NSM] = w_in[:, sm]
        im["wfm" + sfx] = np.ascontiguousarray(w_in[:, fm])
        im["wtm" + sfx] = wtm
        cc_ = core_consts(c, L, inp)
        im["retp"] = cc_.pop("retp")
        for k, v in cc_.items():
            im[k + sfx] = v
        im["cmp_w1" + sfx] = inp["cmp_w1"][L]
        im["cmp_w2" + sfx] = inp["cmp_w2"][L]
        im["cmp_pe" + sfx] = inp["cmp_pe"][L]
        im["pT" + sfx] = np.ascontiguousarray(inp["p"][L, 0][c * TS:(c + 1) * TS].T)
    return im


def kernel(**inputs):
    inp = {k: np.asarray(v) for k, v in inputs.items()}
    cores = list(range(NCORE))
    nc = build_fused()
    shared4 = []
    for L in range(DEPTH):
        d = p4_inputs(inp, L)
        shared4.append({k + "_L%d" % L: v for k, v in d.items()})
    ims = []
    for c in cores:
        im = fused_inputs(c, inp)
        for d in shared4:
            im.update(d)
        ims.append(im)
    res = run_bass_kernel_spmd(nc, ims, core_ids=cores)
    xT = np.concatenate([np.asarray(r["xo"]) for r in res.results], axis=1)
    return np.ascontiguousarray(xT.T)[None].astype(np.float32)
```

```python
import numpy as np
import ml_dtypes
from contextlib import ExitStack, contextmanager
import concourse.bass as bass
import concourse.mybir as mybir
from concourse.bass_utils import run_bass_kernel_spmd

F32 = mybir.dt.float32
BF16 = mybir.dt.bfloat16
AF = mybir.ActivationFunctionType
ALU = mybir.AluOpType
NPBF = ml_dtypes.bfloat16

NDMA_SEM = 8
OUTQ = "sp"
SAME_ENGINE_SYNC = True

D = 2048
T = 8192
NCORE = 8
TS = T // NCORE
DEPTH = 2
EPS = 1e-6
P_DIM = 256
IN_SIZES = (2048, 2048, 1024, 1024, 32, 2048, 512, 512, 512, 512, 512, 512, 48, 2048, 2048, 2048, 2048, 2048, 6144)
OFF = np.concatenate([[0], np.cumsum(IN_SIZES)]).tolist()
(O_XA, O_ZA, O_BA, O_CA, O_DT, O_QB, O_KCB, O_VCB, O_KSB, O_VSB, O_KWB, O_VWB, O_GB, O_ZB, O_QC, O_KC, O_VC, O_ZC,
 O_GM) = OFF[:19]
NFM = 1536
NTM = 1792
NSM = 10
NEG = -30000.0
SCALE_B = 128 ** -0.5


class Buf:
    def __init__(self, name, t, psum=False):
        self.name = name
        self.t = t
        self.psum = psum

    def __getitem__(self, k):
        return self.t[k]

    def k(self, *idx):
        if self.psum:
            return (self.name,)
        return (self.name,) + tuple(idx)


def _key(x):
    if isinstance(x, Buf):
        return (x.name,)
    if isinstance(x, str):
        return (x,)
    return tuple(x)


class Sched:
    ENGS = ("pe", "act", "dve", "pool", "sp")
    CENGS = ("pe", "act", "dve", "pool")

    def __init__(self, nc):
        self.nc = nc
        self.ops = []
        self.es = ExitStack()
        self.pes = None
        self.sems = {e: self.es.enter_context(nc.semaphore("s_" + e)) for e in self.ENGS}
        self.dsems = {e: [self.es.enter_context(nc.semaphore("d_%s%d" % (e, i))) for i in range(NDMA_SEM)]
                      for e in ("sp", "pool", "act")}
        self.cc_sem = self.es.enter_context(nc.semaphore("s_cc"))
        self.cc_cnt = 0
        self.scopes = []
        self.cnt = {e: 0 for e in self.ENGS}
        self.dcnt = {e: 0 for e in self.dsems}
        self.waited = {e: {} for e in self.ENGS}
        self.last_w = {}
        self.readers = {}
        self.nphase = 0
        self.total_ops = 0
        self.eng = {"pe": nc.tensor, "act": nc.scalar, "dve": nc.vector, "pool": nc.gpsimd, "sp": nc.sync}

    def _stack(self, persist):
        if persist or self.pes is None:
            return self.scopes[-1] if self.scopes else self.es
        return self.pes

    @contextmanager
    def scope(self):
        assert self.pes is None
        st = ExitStack()
        self.scopes.append(st)
        try:
            yield
        finally:
            self.scopes.pop()
            st.close()

    def sb(self, name, shape, dt, persist=False):
        self.nalloc = getattr(self, "nalloc", 0) + 1
        name = "sb%d_%s" % (self.nalloc, name)
        t = self._stack(persist).enter_context(self.nc.sbuf_tensor(name, list(shape), dt))
        return Buf(name, t)

    def ps(self, name, shape, dt, persist=False):
        self.nalloc = getattr(self, "nalloc", 0) + 1
        name = "ps%d_%s" % (self.nalloc, name)
        t = self._stack(persist).enter_context(self.nc.psum_tensor(name, list(shape), dt))
        if not hasattr(self, "psum_names"):
            self.psum_names = set()
        self.psum_names.add(name)
        return Buf(name, t, psum=True)

    @contextmanager
    def phase(self):
        self.pes = ExitStack()
        try:
            yield
            self.flush()
        finally:
            self.pes.close()
            self.pes = None

    def add(self, eng, fn, r=(), w=(), dma=False):
        self.ops.append(dict(eng=eng, fn=fn, r=[_key(x) for x in r], w=[_key(x) for x in w], dma=dma))

    def pe(self, fn, r=(), w=()): self.add("pe", fn, r, w)
    def act(self, fn, r=(), w=()): self.add("act", fn, r, w)
    def dve(self, fn, r=(), w=()): self.add("dve", fn, r, w)
    def pool(self, fn, r=(), w=()): self.add("pool", fn, r, w)
    def dma(self, fn, r=(), w=(), q="sp"): self.add(q, fn, r, w, dma=True)

    def cc(self, fn, r=(), w=()):
        self.add("pool", fn, r, w, dma=False)
        self.ops[-1]["cc"] = True

    @staticmethod
    def _ov(a, b):
        n = min(len(a), len(b))
        return a[:n] == b[:n]

    def flush(self):
        nc = self.nc
        ops = self.ops
        self.ops = []
        base = self.total_ops
        allops = getattr(self, "_all", None)
        if allops is None:
            allops = self._all = {}
        last_w, readers = self.last_w, self.readers
        n = len(ops)
        for li, op in enumerate(ops):
            i = base + li
            allops[i] = op
            deps = set()
            psn = getattr(self, "psum_names", ())
            for k in op["r"]:
                for (k2, j) in last_w.get(k[0], ()):
                    if self._ov(k, k2):
                        deps.add(j)
                if k[0] in psn:
                    for (k2, j) in readers.get(k[0], ()):
                        if j >= base and allops[j]["eng"] != op["eng"]:
                            deps.add(j)
            for k in op["w"]:
                for (k2, j) in last_w.get(k[0], ()):
                    if self._ov(k, k2):
                        deps.add(j)
                for (k2, j) in readers.get(k[0], ()):
                    if self._ov(k, k2):
                        deps.add(j)
            deps.discard(i)
            fdeps = []
            for j in deps:
                if j < base:
                    continue
                oj = allops[j]
                if oj["eng"] == op["eng"] and not oj["dma"] and not op["dma"] and not oj.get("cc") and not op.get("cc"):
                    if op["eng"] == "pe" or not SAME_ENGINE_SYNC:
                        continue
                fdeps.append(j)
            op["deps"] = fdeps
            for k in op["w"]:
                lw = last_w.setdefault(k[0], [])
                lw[:] = [(k2, j) for (k2, j) in lw if not (len(k) <= len(k2) and k2[:len(k)] == k)]
                lw.append((k, i))
                rd = readers.setdefault(k[0], [])
                rd[:] = [(k2, j) for (k2, j) in rd if not (len(k) <= len(k2) and k2[:len(k)] == k)]
            for k in op["r"]:
                rd = readers.setdefault(k[0], [])
                rd[:] = [(k2, j) for (k2, j) in rd
                         if not (k2 == k and j >= base and allops[j]["eng"] == op["eng"]
                                 and not allops[j]["dma"] and not op["dma"])]
                rd.append((k, i))
        needed = set()
        for op in ops:
            for j in op["deps"]:
                needed.add(j)
        lastc = {}
        for li, op in enumerate(ops):
            if not op["dma"] and not op.get("cc"):
                lastc[op["eng"]] = base + li
        for e, j in lastc.items():
            needed.add(j)
        bar = []
        if self.nphase > 0:
            for e in self.CENGS:
                if self.cnt[e] > 0:
                    bar.append((self.sems[e], self.cnt[e]))
            if self.cc_cnt > 0:
                bar.append((self.cc_sem, self.cc_cnt))
            for e, lst in self.dsems.items():
                m = self.dcnt[e]
                for si in range(NDMA_SEM):
                    k = (m - si + NDMA_SEM - 1) // NDMA_SEM if m > si else 0
                    if k > 0:
                        bar.append((lst[si], 16 * k))
        for li, op in enumerate(ops):
            i = base + li
            e = op["eng"]
            if op["dma"]:
                m = self.dcnt[e]; self.dcnt[e] += 1
                op["sig"] = (self.dsems[e][m % NDMA_SEM], 16 * (m // NDMA_SEM + 1))
                op["dprev"] = (self.dsems[e][m % NDMA_SEM], 16 * (m // NDMA_SEM)) if m >= NDMA_SEM else None
            elif op.get("cc"):
                self.cc_cnt += 1
                op["sig"] = (self.cc_sem, self.cc_cnt)
            elif i in needed:
                self.cnt[e] += 1
                op["sig"] = (self.sems[e], self.cnt[e])
            else:
                op["sig"] = None
        self.total_ops += n
        self.nphase += 1
        sems_id = {}

        with nc.Block() as block:
            def emit(e, eng):
                waited = self.waited[e]

                def wait(s, v):
                    if waited.get(id(s), 0) >= v:
                        return
                    waited[id(s)] = v
                    eng.wait_ge(s, v)
                for (s, v) in bar:
                    wait(s, v)
                for op in ops:
                    if op["eng"] != e:
                        continue
                    if op["dma"] and op["dprev"] is not None:
                        wait(*op["dprev"])
                    for j in op["deps"]:
                        wait(*allops[j]["sig"])
                    ins = op["fn"](eng)
                    if op["sig"] is not None:
                        if op.get("cc"):
                            ins.then_inc(op["sig"][0])
                        else:
                            ins.then_inc(op["sig"][0], 16 if op["dma"] else 1)

            @block.tensor
            def _(eng): emit("pe", eng)

            @block.scalar
            def _(eng): emit("act", eng)

            @block.vector
            def _(eng): emit("dve", eng)

            @block.gpsimd
            def _(eng): emit("pool", eng)

            @block.sync
            def _(eng): emit("sp", eng)
        for li in range(n):
            op = allops[base + li]
            op["fn"] = None

    def finish(self):
        nc = self.nc
        assert not self.ops
        with nc.Block() as block:
            @block.sync
            def _(eng):
                waited = self.waited["sp"]
                for e in self.CENGS:
                    if self.cnt[e] > 0 and waited.get(id(self.sems[e]), 0) < self.cnt[e]:
                        eng.wait_ge(self.sems[e], self.cnt[e])
                if self.cc_cnt > 0 and waited.get(id(self.cc_sem), 0) < self.cc_cnt:
                    eng.wait_ge(self.cc_sem, self.cc_cnt)
                for e, lst in self.dsems.items():
                    m = self.dcnt[e]
                    for si in range(NDMA_SEM):
                        k = (m - si + NDMA_SEM - 1) // NDMA_SEM if m > si else 0
                        if k > 0 and waited.get(id(lst[si]), 0) < 16 * k:
                            eng.wait_ge(lst[si], 16 * k)
        self.es.close()


def core_cols(c):
    gk = c // 2
    own = [2 * (c % 2), 2 * (c % 2) + 1]
    oth = [r for r in range(4) if r not in own]
    qorder = own + oth
    fm = []
    fm += list(range(O_XA + 256 * c, O_XA + 256 * c + 256))
    fm += list(range(O_BA + 128 * c, O_BA + 128 * c + 128))
    fm += list(range(O_CA + 128 * c, O_CA + 128 * c + 128))
    for r in qorder:
        hh = 4 * gk + r
        fm += list(range(O_QB + 128 * hh, O_QB + 128 * hh + 128))
    for o in (O_KCB, O_VCB, O_KSB, O_KWB):
        fm += list(range(o + 128 * gk, o + 128 * gk + 128))
    tm = []
    tm += list(range(O_ZA + 256 * c, O_ZA + 256 * c + 256))
    for r in own:
        hh = 4 * gk + r
        tm += list(range(O_ZB + 128 * hh, O_ZB + 128 * hh + 128))
    for o in (O_ZC, O_QC, O_KC, O_VC):
        tm += list(range(o + 256 * c, o + 256 * c + 256))
    for o in (O_VSB, O_VWB):
        tm += list(range(o + 128 * gk, o + 128 * gk + 128))
    sm = list(range(O_DT + 4 * c, O_DT + 4 * c + 4))
    for r in own:
        hh = 4 * gk + r
        sm += [O_GB + 3 * hh + b for b in range(3)]
    assert len(fm) == NFM and len(tm) == NTM and len(sm) == NSM
    return np.array(fm), np.array(tm), np.array(sm)


def emit_p2(S, hT_d, wfm_d, wtm_d, FM_d, TM_d, SM_d):
    NW = NTM + 16
    with S.phase():
        Wfm = S.sb("Wfm", [128, 16, NFM], BF16)
        Wtm = S.sb("Wtm", [128, 16, NW], BF16)
        stg = [S.sb("wstg%d" % i, [128, NW], F32) for i in range(3)]
        ns = 0
        for k in range(16):
            for (wd, Wb, ncol) in ((wfm_d, Wfm, NFM), (wtm_d, Wtm, NW)):
                st = stg[ns % 3]; ns += 1
                S.dma(lambda e, st=st, wd=wd, k=k, ncol=ncol: e.dma_start(out=st[:, :ncol], in_=wd[k * 128:(k + 1) * 128, :]),
                      w=[st])
                if ns % 2:
                    S.pool(lambda e, st=st, Wb=Wb, k=k, ncol=ncol: e.tensor_copy(out=Wb[:, k, :], in_=st[:, :ncol]),
                           r=[st], w=[Wb.k(k)])
                else:
                    S.dve(lambda e, st=st, Wb=Wb, k=k, ncol=ncol: e.tensor_copy(out=Wb[:, k, :], in_=st[:, :ncol]),
                          r=[st], w=[Wb.k(k)])
        hb = [S.sb("p2_h%d" % i, [128, 16, 512], BF16) for i in range(2)]
        fms = [S.sb("p2_fm%d" % i, [128, 12, 512], BF16) for i in range(2)]
        tms = [S.sb("p2_tm%d" % i, [128, NTM], BF16) for i in range(2)]
        sms = [S.sb("p2_sm%d" % i, [128, 16], F32) for i in range(2)]
        banks = [S.ps("p2_ps%d" % i, [128, 512], F32) for i in range(8)]
        nb = 0
        nev = 0
        if callable(hT_d):
            hsrc = hT_d
        else:
            hT_v = hT_d.rearrange("(k p) t -> p k t", p=128)
            hsrc = lambda tb, q: hT_v[:, q * 4:(q + 1) * 4, tb * 512:(tb + 1) * 512]
        FM_v = FM_d.rearrange("(f p) t -> p f t", p=128)
        ntm = 0
        import os as _os
        _ntb = int(_os.environ.get('P2_NTB', '16')); _fm = int(_os.environ.get('P2_FM', '1')); _tm = int(_os.environ.get('P2_TM', '1'))
        for tb in range(_ntb):
            h = hb[tb % 2]
            for q in range(4):
                S.dma(lambda e, h=h, q=q, tb=tb: e.dma_start(out=h[:, q * 4:(q + 1) * 4, :], in_=hsrc(tb, q)),
                      w=[h.k(q)])
            fmst = fms[tb % 2]
            for ft in range(12 if _fm else 0):
                ps = banks[nb % 8]; nb += 1
                for k in range(16):
                    S.pe(lambda e, ps=ps, k=k, ft=ft, h=h: e.matmul(out=ps[:], lhsT=Wfm[:, k, ft * 128:(ft + 1) * 128],
                                                                    rhs=h[:, k, :], start=(k == 0), stop=(k == 15)),
                         r=[Wfm.k(k), h.k(k // 4)], w=[ps])
                if nev % 2:
                    S.act(lambda e, ps=ps, fmst=fmst, ft=ft: e.copy(out=fmst[:, ft, :], in_=ps[:]), r=[ps], w=[fmst.k(ft // 4)])
                else:
                    S.dve(lambda e, ps=ps, fmst=fmst, ft=ft: e.tensor_copy(out=fmst[:, ft, :], in_=ps[:]), r=[ps], w=[fmst.k(ft // 4)])
                nev += 1
                if ft % 4 == 3:
                    f0 = ft - 3
                    S.dma(lambda e, fmst=fmst, f0=f0, tb=tb: e.dma_start(out=FM_v[:, f0:f0 + 4, tb * 512:(tb + 1) * 512],
                                                                         in_=fmst[:, f0:f0 + 4, :]),
                          r=[fmst.k(f0 // 4)], w=[("FM_d", tb, f0)], q=OUTQ)
            for tt in range(4 if _tm else 0):
                tmst = tms[ntm % 2]; smst = sms[ntm % 2]; ntm += 1
                ct = tb * 4 + tt
                for cg in range(int(_os.environ.get('P2_NCG', '4'))):
                    c0 = cg * 512
                    c1 = min(c0 + 512, NTM + 16)
                    ps = banks[nb % 8]; nb += 1
                    for k in range(16):
                        S.pe(lambda e, ps=ps, k=k, h=h, tt=tt, c0=c0, c1=c1: e.matmul(
                            out=ps[:, :c1 - c0], lhsT=h[:, k, tt * 128:(tt + 1) * 128], rhs=Wtm[:, k, c0:c1],
                            start=(k == 0), stop=(k == 15)), r=[Wtm.k(k), h.k(k // 4)], w=[ps])
                    cb = min(c1, NTM)
                    if nev % 2:
                        S.act(lambda e, ps=ps, tmst=tmst, c0=c0, cb=cb: e.copy(out=tmst[:, c0:cb], in_=ps[:, :cb - c0]),
                              r=[ps], w=[tmst.k(cg)])
                    else:
                        S.dve(lambda e, ps=ps, tmst=tmst, c0=c0, cb=cb: e.tensor_copy(out=tmst[:, c0:cb], in_=ps[:, :cb - c0]),
                              r=[ps], w=[tmst.k(cg)])
                    nev += 1
                    if cg == 3 and int(_os.environ.get('P2_SMC', '1')):
                        S.act(lambda e, ps=ps, smst=smst, c0=c0: e.copy(out=smst[:, :16], in_=ps[:, NTM - c0:NTM - c0 + 16]),
                              r=[ps], w=[smst])
                S.dma(lambda e, tmst=tmst, ct=ct: e.dma_start(out=TM_d[ct * 128:(ct + 1) * 128, :], in_=tmst[:]),
                      r=[tmst], w=[("TM_d", ct)], q=OUTQ)
                if int(_os.environ.get('P2_SM', '1')):
                    S.dma(lambda e, smst=smst, ct=ct: e.dma_start(out=SM_d[:, ct, :], in_=smst[:]),
                          r=[smst], w=[("SM_d", ct)], q=OUTQ)


def make_consts():
    c = {}
    c["identb"] = np.eye(128, dtype=np.float32).astype(NPBF)
    c["identf"] = np.eye(128, dtype=np.float32)
    j = np.arange(128)
    c["U"] = (j[:, None] <= j[None, :]).astype(np.float32)
    m = np.where(j[None, :] < j[:, None], NEG, 0.0).astype(np.float32)
    c["mneg4"] = np.tile(m, (1, 4))
    c["tri01"] = (j[None, :] >= j[:, None]).astype(np.float32).astype(NPBF)
    c["rot"] = rot_table()
    c.update(nsa_consts())
    return c


def load_const(S, name, ap_d, shape, dt, persist=False):
    b = S.sb(name, shape, dt, persist=persist)
    S.dma(lambda e: e.dma_start(out=b[:], in_=ap_d), w=[b])
    return b


def emit_ssd(S, FM_d, TM_d, SM_d, CD, yT_d, nchunk=64):
    FM_v = FM_d.rearrange("(f p) t -> p f t", p=128)
    yT_v = yT_d.rearrange("(f p) t -> p f t", p=128)
    with S.phase():
        identb = load_const(S, "identb", CD["identb"], [128, 128], BF16)
        identf = load_const(S, "identf", CD["identf"], [128, 128], F32)
        U = load_const(S, "Utri", CD["U"], [128, 128], F32)
        mneg4 = load_const(S, "mneg4", CD["mneg4"], [128, 512], F32)
        ssdp = load_const(S, "ssdp", CD["ssdp"], [128, 16], F32)
        convp = load_const(S, "convp", CD["convp"], [128, 4, 5], F32)
        sm = load_const(S, "small", SM_d, [128, 64, 16], F32)
        dt = S.sb("dt", [128, 64, 4], F32)
        a = S.sb("a_t", [128, 64, 4], F32)
        negA = S.sb("negA", [128, 4], F32)
        S.dve(lambda e: e.tensor_tensor(out=dt[:], in0=sm[:, :, 0:4], in1=ssdp[:, 0:4].unsqueeze(1).to_broadcast([128, 64, 4]),
                                        op=ALU.add), r=[sm, ssdp], w=[dt])
        S.act(lambda e: e.activation(out=dt[:], in_=dt[:], func=AF.Exp), r=[dt], w=[dt])
        S.act(lambda e: e.activation(out=dt[:], in_=dt[:], func=AF.Ln, bias=1.0), r=[dt], w=[dt])
        S.act(lambda e: e.activation(out=negA[:], in_=ssdp[:, 4:8], func=AF.Exp), r=[ssdp], w=[negA])
        S.dve(lambda e: e.tensor_scalar(out=negA[:], in0=negA[:], scalar1=-1.0, scalar2=None, op0=ALU.mult), r=[negA], w=[negA])
        S.dve(lambda e: e.tensor_tensor(out=a[:], in0=dt[:], in1=negA[:].unsqueeze(1).to_broadcast([128, 64, 4]), op=ALU.mult),
              r=[dt, negA], w=[a])
        Dg = S.sb("Dg", [128, 16, 128], BF16)
        for tl in range(4):
            for tap in range(4):
                S.dve(lambda e, tl=tl, tap=tap: e.tensor_scalar(out=Dg[:, tl * 4 + tap, :], in0=identf[:], scalar1=convp[:, tl, tap:tap + 1],
                                                                scalar2=None, op0=ALU.mult), r=[identf, convp], w=[Dg.k(tl * 4 + tap)])
        prev = S.sb("prev", [128, 256], F32)
        prevb = S.sb("prevb", [128, 256], BF16)
        S.dve(lambda e: e.memset(prev[:], 0.0), w=[prev])
        S.dve(lambda e: e.memset(prevb[:], 0.0), w=[prevb])
        xin = [S.sb("xin%d" % i, [128, 4, 515], BF16) for i in range(2)]
        xcs = [S.sb("xc%d" % i, [128, 4, 512], BF16) for i in range(2)]
        yst = [S.sb("yst%d" % i, [128, 2, 512], BF16) for i in range(2)]
        xdts = [S.sb("xdt%d" % i, [128, 256], BF16) for i in range(2)]
        xdtes = [S.sb("xdte%d" % i, [128, 256], BF16) for i in range(2)]
        xsk = [S.sb("xsk%d" % i, [128, 256], F32) for i in range(2)]
        bmt = [S.sb("bmt%d" % i, [128, 128], BF16) for i in range(2)]
        abcs = [S.sb("abc%d" % i, [128, 4, 128], F32) for i in range(2)]
        negCs = [S.sb("negC%d" % i, [128, 4], F32) for i in range(2)]
        eacss = [S.sb("eacs%d" % i, [128, 4], F32) for i in range(2)]
        cdecs = [S.sb("cdec%d" % i, [128, 4], F32) for i in range(2)]
        decTs = [S.sb("decT%d" % i, [128, 4, 128], F32) for i in range(2)]
        LTs = [S.sb("LT%d" % i, [128, 4, 128], BF16) for i in range(2)]
        yds = [S.sb("yd%d" % i, [128, 256], F32) for i in range(2)]
        y1s = [S.sb("y1_%d" % i, [128, 256], F32) for i in range(2)]
        y2s = [S.sb("y2_%d" % i, [128, 256], F32) for i in range(2)]
        zts = [S.sb("zt%d" % i, [128, 256], BF16) for i in range(2)]
        szs = [S.sb("sz%d" % i, [128, 256], F32) for i in range(2)]
        ygs = [S.sb("yg%d" % i, [128, 256], BF16) for i in range(2)]
        p_tr = [S.ps("ps_tr%d" % i, [128, 512], BF16) for i in range(2)]
        p_seg = [S.ps("ps_seg%d" % i, [128, 512], F32) for i in range(2)]
        p_misc = [S.ps("ps_misc%d" % i, [128, 512], F32) for i in range(2)]
        p_y = S.ps("ps_y", [128, 512], F32)
        p_to = S.ps("ps_to", [128, 2, 128], BF16)
        p_cv = p_seg
        ncv = 0
        for c in range(nchunk):
            tb, q = c // 4, c % 4
            off = q * 128
            par = c % 2
            if q == 0:
                xi = xin[tb % 2]; xc = xcs[tb % 2]
                if tb == 0:
                    S.pool(lambda e, xi=xi: e.memset(xi[:, :, 0:3], 0.0), w=[xi])
                    S.dma(lambda e, xi=xi: e.dma_start(out=xi[:, :, 3:515], in_=FM_v[:, 0:4, 0:512]), w=[xi])
                else:
                    S.dma(lambda e, xi=xi, tb=tb: e.dma_start(out=xi[:, :, :], in_=FM_v[:, 0:4, tb * 512 - 3:tb * 512 + 512]), w=[xi])
                for tl in range(4):
                    pc = p_cv[ncv % 2]; ncv += 1
                    for tap in range(4):
                        S.pe(lambda e, pc=pc, tl=tl, tap=tap, xi=xi: e.matmul(out=pc[:, :], lhsT=Dg[:, tl * 4 + tap, :], rhs=xi[:, tl, tap:tap + 512],
                                                                             start=(tap == 0), stop=(tap == 3)), r=[Dg, xi], w=[pc])
                    S.act(lambda e, pc=pc, tl=tl, xc=xc: e.activation(out=xc[:, tl, :], in_=pc[:, :], func=AF.Silu, bias=convp[:, tl, 4:5]),
                          r=[pc, convp], w=[xc.k(tl)])
            xc = xcs[tb % 2]
            ptr = p_tr[par]; seg = p_seg[par]; misc = p_misc[par]
            xdt = xdts[par]; xdte = xdtes[par]; xs_ = xsk[par]; bm = bmt[par]; abc = abcs[par]
            negC = negCs[par]; eacs = eacss[par]; cdec = cdecs[par]; decT = decTs[par]; LT = LTs[par]
            yd = yds[par]; y1 = y1s[par]; y2 = y2s[par]; zt = zts[par]; sz = szs[par]; yg = ygs[par]
            for jx in range(3):
                S.pe(lambda e, ptr=ptr, jx=jx, xc=xc, off=off: e.transpose(out=ptr[:, jx * 128:(jx + 1) * 128], in_=xc[:, jx, off:off + 128],
                                                                           identity=identb[:]), r=[xc.k(jx), identb], w=[ptr.k(jx)])
            S.dve(lambda e, ptr=ptr, xdt=xdt, c=c: e.tensor_tensor(out=xdt[:].rearrange("p (h e) -> p h e", h=4),
                                                                    in0=ptr[:, 0:256].rearrange("p (h e) -> p h e", h=4),
                                                                    in1=dt[:, c, :].unsqueeze(2).to_broadcast([128, 4, 64]), op=ALU.mult),
                  r=[ptr.k(0), ptr.k(1), dt], w=[xdt])
            S.dve(lambda e, ptr=ptr, xs_=xs_: e.tensor_tensor(out=xs_[:].rearrange("p (h e) -> p h e", h=4),
                                                              in0=ptr[:, 0:256].rearrange("p (h e) -> p h e", h=4),
                                                              in1=ssdp[:, 8:12].unsqueeze(2).to_broadcast([128, 4, 64]), op=ALU.mult),
                  r=[ptr.k(0), ptr.k(1), ssdp], w=[xs_])
            S.act(lambda e, ptr=ptr, bm=bm: e.copy(out=bm[:], in_=ptr[:, 256:384]), r=[ptr.k(2)], w=[bm])
            S.pool(lambda e, abc=abc, c=c: e.tensor_copy(out=abc[:], in_=a[:, c, :].unsqueeze(2).to_broadcast([128, 4, 128])),
                   r=[a], w=[abc])
            S.pe(lambda e, misc=misc, c=c: e.matmul(out=misc[:, 128:132], lhsT=U[:], rhs=a[:, c, :], start=True, stop=True),
                 r=[U, a], w=[misc.k("C")])
            S.pe(lambda e, seg=seg: e.matmul(out=seg[:, :], lhsT=identf[:], rhs=mneg4[:], start=True, stop=False),
                 r=[identf, mneg4], w=[seg])
            for h in range(4):
                S.pe(lambda e, seg=seg, h=h, abc=abc: e.matmul(out=seg[:, h * 128:(h + 1) * 128], lhsT=abc[:, h, :], rhs=U[:],
                                                                start=False, stop=(h == 3)), r=[abc, U], w=[seg])
            S.dve(lambda e, misc=misc, negC=negC: e.tensor_scalar(out=negC[:], in0=misc[:, 128:132], scalar1=-1.0, scalar2=None, op0=ALU.mult),
                  r=[misc.k("C")], w=[negC])
            S.act(lambda e, misc=misc, eacs=eacs: e.activation(out=eacs[:], in_=misc[:, 128:132], func=AF.Exp), r=[misc.k("C")], w=[eacs])
            for h in range(4):
                S.act(lambda e, seg=seg, h=h, decT=decT, negC=negC: e.activation(out=decT[:, h, :], in_=seg[:, h * 128:(h + 1) * 128],
                                                                                 func=AF.Exp, bias=negC[:, h:h + 1]),
                      r=[seg, negC], w=[decT.k(h)])
            S.act(lambda e, seg=seg, cdec=cdec: e.activation(out=cdec[:], in_=seg[:, :].rearrange("p (h e) -> p h e", h=4)[:, :, 127],
                                                             func=AF.Exp), r=[seg], w=[cdec])
            S.pe(lambda e, misc=misc, xc=xc, off=off: e.matmul(out=misc[:, 0:128], lhsT=xc[:, 2, off:off + 128], rhs=xc[:, 3, off:off + 128],
                                                               start=True, stop=True), r=[xc.k(2), xc.k(3)], w=[misc.k("cb")])
            S.dve(lambda e, misc=misc, LT=LT, decT=decT: e.tensor_tensor(out=LT[:], in0=misc[:, 0:128].unsqueeze(1).to_broadcast([128, 4, 128]),
                                                                         in1=decT[:], op=ALU.mult), r=[misc.k("cb"), decT], w=[LT])
            S.dve(lambda e, xdt=xdt, xdte=xdte, decT=decT: e.tensor_tensor(out=xdte[:].rearrange("p (h e) -> p h e", h=4),
                                                                           in0=xdt[:].rearrange("p (h e) -> p h e", h=4),
                                                                           in1=decT[:, :, 127:128].to_broadcast([128, 4, 64]), op=ALU.mult),
                  r=[xdt, decT], w=[xdte])
            for h in range(4):
                S.pe(lambda e, h=h, LT=LT, xdt=xdt: e.matmul(out=p_y[:, h * 64:(h + 1) * 64], lhsT=LT[:, h, :], rhs=xdt[:, h * 64:(h + 1) * 64],
                                                             start=True, stop=True), r=[LT, xdt], w=[p_y.k("d")])
            S.pe(lambda e, xc=xc, off=off: e.matmul(out=p_y[:, 256:512], lhsT=xc[:, 3, off:off + 128], rhs=prevb[:], start=True, stop=True),
                 r=[xc.k(3), prevb], w=[p_y.k("o")])
            S.pe(lambda e, misc=misc, bm=bm, xdte=xdte: e.matmul(out=misc[:, 256:512], lhsT=bm[:], rhs=xdte[:], start=True, stop=True),
                 r=[bm, xdte], w=[misc.k("st")])
            S.act(lambda e, yd=yd: e.copy(out=yd[:], in_=p_y[:, 0:256]), r=[p_y.k("d")], w=[yd])
            S.dve(lambda e, y1=y1, eacs=eacs: e.tensor_tensor(out=y1[:].rearrange("p (h e) -> p h e", h=4),
                                                              in0=p_y[:, 256:512].rearrange("p (h e) -> p h e", h=4),
                                                              in1=eacs[:].unsqueeze(2).to_broadcast([128, 4, 64]), op=ALU.mult),
                  r=[p_y.k("o"), eacs], w=[y1])
            S.pool(lambda e, y2=y2, yd=yd, xs_=xs_: e.tensor_tensor(out=y2[:], in0=yd[:], in1=xs_[:], op=ALU.add), r=[yd, xs_], w=[y2])
            S.pool(lambda e, y2=y2, y1=y1: e.tensor_tensor(out=y2[:], in0=y2[:], in1=y1[:], op=ALU.add), r=[y2, y1], w=[y2])
            S.dma(lambda e, zt=zt, c=c: e.dma_start(out=zt[:], in_=TM_d[c * 128:(c + 1) * 128, 0:256]), w=[zt])
            S.act(lambda e, zt=zt, sz=sz: e.activation(out=sz[:], in_=zt[:], func=AF.Silu), r=[zt], w=[sz])
            S.dve(lambda e, yg=yg, y2=y2, sz=sz: e.tensor_tensor(out=yg[:], in0=y2[:], in1=sz[:], op=ALU.mult), r=[y2, sz], w=[yg])
            ys = yst[tb % 2]
            for jx in range(2):
                S.pe(lambda e, jx=jx, yg=yg: e.transpose(out=p_to[:, jx, :], in_=yg[:, jx * 128:(jx + 1) * 128], identity=identb[:]),
                     r=[yg, identb], w=[p_to])
            S.act(lambda e, ys=ys, off=off: e.copy(out=ys[:, :, off:off + 128], in_=p_to[:, :, :]), r=[p_to], w=[ys])
            if q == 3 or c == nchunk - 1:
                S.dma(lambda e, ys=ys, tb=tb: e.dma_start(out=yT_v[:, 0:2, tb * 512:(tb + 1) * 512], in_=ys[:]), r=[ys], w=[("yT_d", "a", tb)],
                      q=OUTQ)
            S.dve(lambda e, cdec=cdec: e.tensor_tensor(out=prev[:].rearrange("p (h e) -> p h e", h=4),
                                                       in0=prev[:].rearrange("p (h e) -> p h e", h=4),
                                                       in1=cdec[:].unsqueeze(2).to_broadcast([128, 4, 64]), op=ALU.mult),
                  r=[prev, cdec], w=[prev])
            S.dve(lambda e, misc=misc: e.tensor_tensor(out=prev[:], in0=prev[:], in1=misc[:, 256:512], op=ALU.add),
                  r=[prev, misc.k("st")], w=[prev])
            S.pool(lambda e: e.tensor_copy(out=prevb[:], in_=prev[:]), r=[prev], w=[prevb])


def ssd_params(c, conv_w, conv_b, dt_bias, a_log, d_skip):
    ssdp = np.zeros((128, 16), np.float32)
    ssdp[:, 0:4] = dt_bias[4 * c:4 * c + 4][None, :]
    ssdp[:, 4:8] = a_log[4 * c:4 * c + 4][None, :]
    ssdp[:, 8:12] = d_skip[4 * c:4 * c + 4][None, :]
    chans = [np.arange(256 * c, 256 * c + 128), np.arange(256 * c + 128, 256 * c + 256),
             2048 + np.arange(128 * c, 128 * c + 128), 3072 + np.arange(128 * c, 128 * c + 128)]
    convp = np.zeros((128, 4, 5), np.float32)
    for tl, ch in enumerate(chans):
        convp[:, tl, 0:4] = conv_w[:, ch].T
        convp[:, tl, 4] = conv_b[ch]
    return ssdp, convp


def core_consts(c, layer, inp):
    d = {}
    ssdp, convp = ssd_params(c, inp["conv_w"][layer], inp["conv_b"][layer], inp["dt_bias"][layer], inp["a_log"][layer],
                             inp["d_skip"][layer])
    d["ssdp"] = ssdp
    d["convp"] = convp
    d["retp"], d["gnw"] = ret_params(c, inp["ret_norm"][layer])
    return d


def emit_ret(S, FM_d, TM_d, SM_d, CD, yT_d, nchunk=64):
    yT_v = yT_d.rearrange("(f p) t -> p f t", p=128)
    with S.phase():
        identb = load_const(S, "identb", CD["identb"], [128, 128], BF16)
        tri01 = load_const(S, "tri01", CD["tri01"], [128, 128], BF16)
        retp = load_const(S, "retp", CD["retp"], [128, 8], F32)
        gnw = load_const(S, "gnw", CD["gnw"], [128, 256], F32)
        R = S.sb("R", [128, 2, 256], F32)
        Rg = S.sb("Rg", [128, 2, 256], BF16)
        Rt = S.sb("Rt", [128, 2, 256], F32)
        S.dve(lambda e: e.memset(R[:], 0.0), w=[R])
        S.dve(lambda e: e.memset(Rg[:], 0.0), w=[Rg])
        tmcs = [S.sb("tmc%d" % i, [128, 1024], BF16) for i in range(2)]
        rts = [S.sb("rt%d" % i, [128, 256], F32) for i in range(2)]
        m1s = [S.sb("m1_%d" % i, [128, 128, 2], F32) for i in range(2)]
        m2s = [S.sb("m2_%d" % i, [128, 128, 2], F32) for i in range(2)]
        qrs = [S.sb("qr%d" % i, [128, 128, 2], BF16) for i in range(2)]
        krs = [S.sb("kr%d" % i, [128, 128, 2], BF16) for i in range(2)]
        qkTs = [S.sb("qkT%d" % i, [128, 4, 128], BF16) for i in range(2)]
        sTms = [S.sb("sTm%d" % i, [128, 128], BF16) for i in range(2)]
        st6 = [S.sb("st6_%d" % i, [128, 6], F32) for i in range(2)]
        mvs = [S.sb("mv%d" % i, [128, 2], F32) for i in range(2)]
        rss = [S.sb("rs%d" % i, [128, 2], F32) for i in range(2)]
        ons = [S.sb("on%d" % i, [128, 256], F32) for i in range(2)]
        szs = [S.sb("rsz%d" % i, [128, 256], F32) for i in range(2)]
        ycs = [S.sb("yc%d" % i, [128, 256], BF16) for i in range(2)]
        yst = [S.sb("ryst%d" % i, [128, 2, 512], BF16) for i in range(2)]
        p_tr = [S.ps("rp_tr%d" % i, [128, 4, 128], BF16) for i in range(2)]
        p_s = S.ps("rp_s", [128, 512], F32)
        p_o = [S.ps("rp_o%d" % i, [128, 512], F32) for i in range(2)]
        p_kvs = [S.ps("rp_kv%d" % i, [128, 512], F32) for i in range(2)]
        p_to = S.ps("rp_to", [128, 2, 128], BF16)
        def chunk_stages(c):
          st_ = {}

          def stageA():
            tb, q4 = c // 4, c % 4
            off = q4 * 128
            par = c % 2
            tmc = tmcs[par]; rt = rts[par]; qkT = qkTs[par]; sTm = sTms[par]
            ptr = p_tr[par]; po = p_o[par]; p_kv = p_kvs[par]
            S.dma(lambda e, tmc=tmc, c=c: e.dma_start(out=tmc[:], in_=TM_d[c * 128:(c + 1) * 128, 512:1536]), w=[tmc])
            S.dma(lambda e, rt=rt, c=c: e.dma_start(out=rt[:], in_=CD["rot"][c * 128:(c + 1) * 128, :]), w=[rt])
            xr = {}
            for nm, c0, gi, outs in (("q", 256, 0, qrs), ("k", 512, 1, krs)):
                m1 = m1s[0 if nm == "q" else 1]; m2 = m2s[0 if nm == "q" else 1]
                xo = outs[par]
                xr[nm] = xo
                for mm_, t0 in ((m1, 0), (m2, 128)):
                    S.dve(lambda e, mm_=mm_, t0=t0, tmc=tmc, c0=c0, gi=gi, rt=rt: e.scalar_tensor_tensor(
                        out=mm_[:], in0=tmc[:, c0:c0 + 256].rearrange("p (i two) -> p i two", two=2), scalar=retp[:, gi:gi + 1],
                        in1=rt[:, t0:t0 + 128].unsqueeze(2).to_broadcast([128, 128, 2]), op0=ALU.mult, op1=ALU.mult),
                        r=[tmc, retp, rt], w=[mm_])
                S.pool(lambda e, xo=xo, m1=m1, m2=m2: e.tensor_tensor(out=xo[:, :, 0], in0=m1[:, :, 0], in1=m2[:, :, 1], op=ALU.subtract),
                       r=[m1, m2], w=[xo.k(0)])
                S.pool(lambda e, xo=xo, m1=m1, m2=m2: e.tensor_tensor(out=xo[:, :, 1], in0=m2[:, :, 0], in1=m1[:, :, 1], op=ALU.add),
                       r=[m1, m2], w=[xo.k(1)])
            qr, kr = xr["q"], xr["k"]
            for jx in range(2):
                S.pe(lambda e, ptr=ptr, jx=jx, qr=qr: e.transpose(out=ptr[:, jx, :], in_=qr[:].rearrange("p i two -> p (i two)")[:, jx * 128:(jx + 1) * 128],
                                                                  identity=identb[:]), r=[qr, identb], w=[ptr])
            for jx in range(2):
                S.pe(lambda e, ptr=ptr, jx=jx, kr=kr: e.transpose(out=ptr[:, 2 + jx, :], in_=kr[:].rearrange("p i two -> p (i two)")[:, jx * 128:(jx + 1) * 128],
                                                                  identity=identb[:]), r=[kr, identb], w=[ptr])
            S.act(lambda e, ptr=ptr, qkT=qkT: e.copy(out=qkT[:], in_=ptr[:]), r=[ptr], w=[qkT])
            for dc in range(2):
                S.pe(lambda e, dc=dc, qkT=qkT: e.matmul(out=p_s[:, 0:128], lhsT=qkT[:, 2 + dc, :], rhs=qkT[:, dc, :], start=(dc == 0), stop=(dc == 1)),
                     r=[qkT], w=[p_s])
            S.dve(lambda e, sTm=sTm: e.tensor_tensor(out=sTm[:], in0=p_s[:, 0:128], in1=tri01[:], op=ALU.mult), r=[p_s, tri01], w=[sTm])
            for dc in range(2):
                S.pe(lambda e, dc=dc, kr=kr, tmc=tmc, p_kv=p_kv: e.matmul(out=p_kv[:, dc * 256:(dc + 1) * 256],
                                                               lhsT=kr[:].rearrange("p i two -> p (i two)")[:, dc * 128:(dc + 1) * 128],
                                                               rhs=tmc[:, 768:1024], start=True, stop=True), r=[kr, tmc], w=[p_kv])
            sz = szs[par]
            S.act(lambda e, sz=sz, tmc=tmc: e.activation(out=sz[:], in_=tmc[:, 0:256], func=AF.Silu), r=[tmc], w=[sz])
            st_.update(dict(tb=tb, q4=q4, off=off, par=par, tmc=tmc, qkT=qkT, sTm=sTm, po=po, p_kv=p_kv, sz=sz))

          def stageB():
            tb, q4, off, par = st_['tb'], st_['q4'], st_['off'], st_['par']
            tmc, qkT, sTm, po, p_kv = st_['tmc'], st_['qkT'], st_['sTm'], st_['po'], st_['p_kv']
            S.pe(lambda e, po=po, sTm=sTm, tmc=tmc: e.matmul(out=po[:, 0:256], lhsT=sTm[:], rhs=tmc[:, 768:1024], start=True, stop=False),
                 r=[sTm, tmc], w=[po])
            for dc in range(2):
                S.pe(lambda e, po=po, dc=dc, qkT=qkT: e.matmul(out=po[:, 0:256], lhsT=qkT[:, dc, :], rhs=Rg[:, dc, :], start=False, stop=(dc == 1)),
                     r=[qkT, Rg], w=[po])
            S.pool(lambda e: e.tensor_scalar(out=Rt[:], in0=R[:], scalar1=retp[:, 2:3], scalar2=0.0, op0=ALU.mult, op1=ALU.add),
                   r=[R, retp], w=[Rt])
            S.dve(lambda e, p_kv=p_kv: e.scalar_tensor_tensor(out=R[:].rearrange("p a b -> p (a b)"), in0=p_kv[:, :], scalar=retp[:, 3:4],
                                                   in1=Rt[:].rearrange("p a b -> p (a b)"), op0=ALU.mult, op1=ALU.add),
                  r=[p_kv, retp, Rt], w=[R])
            S.pool(lambda e: e.tensor_scalar(out=Rg[:], in0=R[:], scalar1=retp[:, 4:5], scalar2=0.0, op0=ALU.mult, op1=ALU.add),
                   r=[R, retp], w=[Rg])
            s6 = st6[par]; mv = mvs[par]; rs = rss[par]; on = ons[par]; sz = szs[par]; yc = ycs[par]
            S.dve(lambda e, s6=s6, po=po: e.bn_stats(out=s6[:], in_=po[:, 0:256]), r=[po], w=[s6])
            S.dve(lambda e, s6=s6, mv=mv: e.bn_aggr(out=mv[:], in_=s6[:]), r=[s6], w=[mv])
            S.dve(lambda e, mv=mv, rs=rs: e.tensor_scalar(out=rs[:, 0:1], in0=mv[:, 1:2], scalar1=EPS, scalar2=None, op0=ALU.add), r=[mv], w=[rs])
            S.act(lambda e, rs=rs: e.activation(out=rs[:, 0:1], in_=rs[:, 0:1], func=AF.Sqrt), r=[rs], w=[rs])
            S.dve(lambda e, rs=rs: e.reciprocal(out=rs[:, 0:1], in_=rs[:, 0:1]), r=[rs], w=[rs])
            S.dve(lambda e, rs=rs, mv=mv: e.tensor_scalar(out=rs[:, 1:2], in0=mv[:, 0:1], scalar1=rs[:, 0:1], scalar2=-1.0, op0=ALU.mult, op1=ALU.mult),
                  r=[rs, mv], w=[rs])
            S.act(lambda e, on=on, po=po, rs=rs: e.activation(out=on[:], in_=po[:, 0:256], func=AF.Identity, bias=rs[:, 1:2], scale=rs[:, 0:1]),
                  r=[po, rs], w=[on])
            S.pool(lambda e, on=on: e.tensor_tensor(out=on[:], in0=on[:], in1=gnw[:], op=ALU.mult), r=[on, gnw], w=[on])
            S.dve(lambda e, yc=yc, on=on, sz=sz: e.tensor_tensor(out=yc[:], in0=on[:], in1=sz[:], op=ALU.mult), r=[on, sz], w=[yc])
            ys = yst[tb % 2]
            for jx in range(2):
                S.pe(lambda e, jx=jx, yc=yc: e.transpose(out=p_to[:, jx, :], in_=yc[:, jx * 128:(jx + 1) * 128], identity=identb[:]),
                     r=[yc, identb], w=[p_to])
            S.act(lambda e, ys=ys, off=off: e.copy(out=ys[:, :, off:off + 128], in_=p_to[:, :, :]), r=[p_to], w=[ys])
            if q4 == 3 or c == nchunk - 1:
                S.dma(lambda e, ys=ys, tb=tb: e.dma_start(out=yT_v[:, 4:6, tb * 512:(tb + 1) * 512], in_=ys[:]), r=[ys], w=[("yT_d", "c", tb)],
                      q=OUTQ)
          return (stageA, stageB)

        q_ = []
        for c in range(nchunk):
            sA, sB = chunk_stages(c)
            sA()
            q_.append(sB)
            if len(q_) > 1:
                q_.pop(0)()
        while q_:
            q_.pop(0)()


def ret_params(c, ret_norm):
    g = 1.0 - 2.0 ** (-5.0 - c)
    l = np.arange(128, dtype=np.float64)
    retp = np.zeros((128, 8), np.float64)
    retp[:, 0] = g ** l
    retp[:, 1] = g ** (-l) * 256.0 ** -0.5
    retp[:, 2] = g ** 128
    retp[:, 3] = g ** 127
    retp[:, 4] = g
    gnw = np.tile(ret_norm[256 * c:256 * c + 256][None, :], (128, 1))
    return retp.astype(np.float32), np.ascontiguousarray(gnw.astype(np.float32))


def rot_table():
    inv = 10000.0 ** (-np.arange(0, 256, 2, dtype=np.float64) / 256.0)
    ang = (np.arange(T, dtype=np.float32)[:, None] * inv.astype(np.float32)[None, :]).astype(np.float32).astype(np.float64)
    return np.concatenate([np.cos(ang), np.sin(ang)], axis=1).astype(np.float32)


def nsa_consts():
    c = {}
    n = np.arange(128)
    l = np.arange(128)
    cm = np.zeros((128, 16, 128), np.float32)
    for m in range(16):
        vis = (16 * n[:, None]) <= (128 * m + l[None, :] - 31)
        cm[:, m, :] = np.where(vis, 0.0, NEG)
    c["cmpneg"] = cm.astype(NPBF)
    c["causneg"] = np.where(n[:, None] > l[None, :], NEG, 0.0).astype(np.float32).astype(NPBF)
    c["winneg"] = np.where(n[:, None] <= l[None, :], NEG, 0.0).astype(np.float32).astype(NPBF)
    s = np.arange(T)
    c["Esel"] = (s[None, :] // 64 == n[:, None]).astype(np.float32).astype(NPBF)
    nn = np.arange(512)
    jj = np.arange(128)
    ov = ((nn[:, None] * 16 < (jj[None, :] + 1) * 64) & (nn[:, None] * 16 + 32 > jj[None, :] * 64)).astype(np.float32)
    ov[511, :] = 0.0
    ovx = np.concatenate([ov, np.ones((512, 1), np.float32)], axis=1)
    c["ovx"] = np.ascontiguousarray(ovx.reshape(4, 128, 129).transpose(1, 0, 2)).astype(NPBF)
    x = np.arange(256)
    delta = x[None, :] - 126
    hb = (l[:, None] >= 64).astype(np.int64)
    elig = delta <= hb
    forced = (hb - delta >= 0) & (hb - delta < 2)
    c["selbase"] = np.where(forced, 100.0, np.where(elig, 0.0, -100.0)).astype(np.float32)
    return c


def emit_nsa(S, FM_d, TM_d, SM_d, CD, yT_d, nchunk=64, persist=None):
    FM_v = FM_d.rearrange("(f p) t -> p f t", p=128)
    yT_v = yT_d.rearrange("(f p) t -> p f t", p=128)
    if persist is None:
        with S.scope():
            kcmpT = S.sb("kcmpT", [128, 512], BF16, persist=True)
            CV = S.sb("CV", [128, 4, 257], BF16, persist=True)
            emit_nsa(S, FM_d, TM_d, SM_d, CD, yT_d, nchunk=nchunk, persist=(kcmpT, CV))
        return
    kcmpT, CV = persist
    with S.phase():
        identb = load_const(S, "identb", CD["identb"], [128, 128], BF16)
        identf = load_const(S, "identf", CD["identf"], [128, 128], F32)
        kvT = [S.sb("kcT", [128, T], BF16), S.sb("vcT", [128, T], BF16)]
        for kv in range(2):
            for hf in range(2):
                S.dma(lambda e, kv=kv, hf=hf: e.dma_start(out=kvT[kv][:, hf * 4096:(hf + 1) * 4096], in_=FM_v[:, 8 + kv, hf * 4096:(hf + 1) * 4096]),
                      w=[kvT[kv].k(hf)])
        S.dma(lambda e: e.dma_start(out=CV[:, :, 128:257], in_=CD["ovx"]), w=[CV.k("ov")])
        W1 = S.sb("W1", [128, 32, 256], BF16)
        w1s = [S.sb("w1s%d" % i, [128, 8, 256], F32) for i in range(2)]
        W2 = S.sb("W2", [128, 2, 128], BF16)
        w2s = S.sb("w2s", [128, 2, 128], F32)
        petok = S.sb("petok", [32, 128], F32)
        peT = S.sb("peT", [128, 32], BF16)
        cst = S.sb("cst", [128, 2], F32)
        hidT = S.sb("hidT", [128, 2, 512], BF16)
        pp = [S.ps("np_ps%d" % i, [128, 512], F32) for i in range(4)]
        npp = 0
        nst = 0
        for kv in range(2):
            w1v = CD["cmp_w1"][kv].rearrange("(l d) h -> d l h", d=128)
            for l0 in range(0, 32, 8):
                st = w1s[nst % 2]; nst += 1
                S.dma(lambda e, st=st, l0=l0, w1v=w1v: e.dma_start(out=st[:], in_=w1v[:, l0:l0 + 8, :]), w=[st])
                if nst % 2:
                    S.dve(lambda e, st=st, l0=l0: e.tensor_copy(out=W1[:, l0:l0 + 8, :], in_=st[:]), r=[st], w=[W1.k(l0 // 8)])
                else:
                    S.pool(lambda e, st=st, l0=l0: e.tensor_copy(out=W1[:, l0:l0 + 8, :], in_=st[:]), r=[st], w=[W1.k(l0 // 8)])
            S.dma(lambda e, kv=kv: e.dma_start(out=w2s[:], in_=CD["cmp_w2"][kv].rearrange("(hc p) d -> p hc d", p=128)), w=[w2s])
            S.dve(lambda e: e.tensor_copy(out=W2[:], in_=w2s[:]), r=[w2s], w=[W2])
            S.dma(lambda e, kv=kv: e.dma_start(out=petok[:], in_=CD["cmp_pe"][kv]), w=[petok])
            p0 = pp[npp % 4]; npp += 1
            S.pe(lambda e, p0=p0: e.transpose(out=p0[:, 0:32], in_=petok[:, :], identity=identf[0:32, 0:32]), r=[petok, identf], w=[p0])
            S.act(lambda e, p0=p0: e.copy(out=peT[:], in_=p0[:, 0:32]), r=[p0], w=[peT])
            p1 = pp[npp % 4]; npp += 1
            for hc in range(2):
                for l in range(32):
                    S.pe(lambda e, p1=p1, hc=hc, l=l: e.matmul(out=p1[:, hc:hc + 1], lhsT=W1[:, l, hc * 128:(hc + 1) * 128], rhs=peT[:, l:l + 1],
                                                               start=(l == 0), stop=(l == 31)), r=[W1.k(l // 8), peT], w=[p1])
            S.act(lambda e, p1=p1: e.copy(out=cst[:], in_=p1[:, 0:2]), r=[p1], w=[cst])
            S.dve(lambda e: e.memset(hidT[:, :, 511:512], 0.0), w=[hidT.k("z")])
            for hc in range(2):
                p2 = pp[npp % 4]; npp += 1
                for l in range(32):
                    S.pe(lambda e, p2=p2, hc=hc, l=l, kv=kv: e.matmul(out=p2[:, 0:511], lhsT=W1[:, l, hc * 128:(hc + 1) * 128],
                                                                      rhs=kvT[kv][:, l:l + 16 * 510 + 1:16], start=(l == 0), stop=(l == 31)),
                         r=[W1.k(l // 8), kvT[kv]], w=[p2])
                S.act(lambda e, p2=p2, hc=hc: e.activation(out=hidT[:, hc, 0:511], in_=p2[:, 0:511], func=AF.Silu, bias=cst[:, hc:hc + 1]),
                      r=[p2, cst], w=[hidT.k(hc)])
            if kv == 0:
                p3 = pp[npp % 4]; npp += 1
                for hc in range(2):
                    S.pe(lambda e, p3=p3, hc=hc: e.matmul(out=p3[:, 0:512], lhsT=W2[:, hc, :], rhs=hidT[:, hc, :], start=(hc == 0), stop=(hc == 1)),
                         r=[W2, hidT], w=[p3])
                S.act(lambda e, p3=p3: e.copy(out=kcmpT[:], in_=p3[:, 0:512]), r=[p3], w=[kcmpT])
            else:
                for nk in range(4):
                    p3 = pp[npp % 4]; npp += 1
                    for hc in range(2):
                        S.pe(lambda e, p3=p3, hc=hc, nk=nk: e.matmul(out=p3[:, 0:128], lhsT=hidT[:, hc, nk * 128:(nk + 1) * 128], rhs=W2[:, hc, :],
                                                                     start=(hc == 0), stop=(hc == 1)), r=[W2, hidT], w=[p3])
                    S.act(lambda e, p3=p3, nk=nk: e.copy(out=CV[:, nk, 0:128], in_=p3[:, 0:128]), r=[p3], w=[CV.k("v", nk)])
    with S.phase():
        identb = load_const(S, "identb", CD["identb"], [128, 128], BF16)
        identf = load_const(S, "identf", CD["identf"], [128, 128], F32)
        cmpneg = load_const(S, "cmpneg", CD["cmpneg"], [128, 16, 128], BF16)
        causneg = load_const(S, "causneg", CD["causneg"], [128, 128], BF16)
        winneg = load_const(S, "winneg", CD["winneg"], [128, 128], BF16)
        selbase = load_const(S, "selbase", CD["selbase"], [128, 256], F32)
        sm = load_const(S, "small", SM_d, [128, 64, 16], F32)
        Esel = S.sb("Esel", [128, T], BF16)
        ksT = S.sb("ksT", [128, T], BF16)
        kwT = S.sb("kwT", [128, T], BF16)
        for hf in range(2):
            sl = slice(hf * 4096, (hf + 1) * 4096)
            S.dma(lambda e, sl=sl: e.dma_start(out=Esel[:, sl], in_=CD["Esel"][:, sl]), w=[Esel.k(hf)])
            S.dma(lambda e, sl=sl: e.dma_start(out=ksT[:, sl], in_=FM_v[:, 10, sl]), w=[ksT.k(hf)])
            S.dma(lambda e, sl=sl: e.dma_start(out=kwT[:, sl], in_=FM_v[:, 11, sl]), w=[kwT.k(hf)])
        VS = S.sb("VS", [128, 64, 129], BF16)
        VW = S.sb("VW", [128, 64, 129], BF16)
        for (V, c0) in ((VS, 1536), (VW, 1664)):
            S.dve(lambda e, V=V: e.memset(V[:, :, 128:129], 1.0), w=[V.k("one")])
            for g8 in range(8):
                S.dma(lambda e, V=V, c0=c0, g8=g8: e.dma_start(
                    out=V[:, g8 * 8:(g8 + 1) * 8, 0:128],
                    in_=TM_d[g8 * 1024:(g8 + 1) * 1024, c0:c0 + 128].rearrange("(kt p) d -> p kt d", p=128)), w=[V.k("v", g8)])
        gsig = S.sb("gsig", [128, 64, 6], F32)
        S.act(lambda e: e.activation(out=gsig[:], in_=sm[:, :, 4:10], func=AF.Sigmoid), r=[sm], w=[gsig])
        qTbs = [S.sb("qTb%d" % i, [128, 4, 128], BF16) for i in range(2)]
        zts = [S.sb("nzt%d" % i, [128, 256], BF16) for i in range(2)]
        Es = [S.sb("E%d" % i, [128, 512], BF16) for i in range(4)]
        Ps = [S.sb("P%d" % i, [128, 512], BF16) for i in range(4)]
        rscA = S.sb("rscA", [128, 2], F32)
        rscB = S.sb("rscB", [128, 2], F32)
        imp = S.sb("imp", [128, 128], F32)
        imp2 = S.sb("imp2", [128, 128], F32)
        m8a = S.sb("m8a", [128, 8], F32)
        m8b = S.sb("m8b", [128, 8], F32)
        sel = S.sb("sel", [128, 128], F32)
        biasT = S.sb("biasT", [128, 128], BF16)
        coefc = S.sb("coefc", [128, 2], F32)
        coef = S.sb("coef", [128, 4], F32)
        os_ = [S.sb("o_acc%d" % i, [128, 2, 128], F32) for i in range(2)]
        sz = S.sb("nsz", [128, 256], F32)
        yb = S.sb("yb", [128, 256], BF16)
        yst = [S.sb("nyst%d" % i, [128, 2, 512], BF16) for i in range(2)]
        sbk = [S.ps("n_s%d" % i, [128, 512], F32) for i in range(3)]
        ocA = S.ps("n_ocA", [128, 512], F32)
        ocB = S.ps("n_ocB", [128, 512], F32)
        osb = S.ps("n_os", [128, 512], F32)
        owb = S.ps("n_ow", [128, 512], F32)
        p_t = S.ps("n_pt", [128, 512], F32)
        p_t2v = p_t[:, 256:384].bitcast(BF16).rearrange("p (j t) -> p j t", j=2)
        cnt = {"ns": 0, "ne": 0, "np": 0}

        def pipelined(iters, depth=2):
            q_ = []
            for (s1, s2) in iters:
                s1()
                q_.append(s2)
                if len(q_) > depth:
                    q_.pop(0)()
            while q_:
                q_.pop(0)()

        for i in range(nchunk):
            tb, q4 = i // 4, i % 4
            off = q4 * 128
            qTb = qTbs[i % 2]; zt = zts[i % 2]; o = os_[i % 2]
            S.dma(lambda e, qTb=qTb, i=i: e.dma_start(out=qTb[:], in_=FM_v[:, 4:8, i * 128:(i + 1) * 128]), w=[qTb])
            S.dma(lambda e, zt=zt, i=i: e.dma_start(out=zt[:], in_=TM_d[i * 128:(i + 1) * 128, 256:512]), w=[zt])
            nkc = (8 * i + 6) // 128 + 1

            def comp_iter(kc, i=i, qTb=qTb, nkc=nkc):
                st = {}

                def s1():
                    m = i - 16 * kc
                    sb_ = sbk[cnt["ns"] % 3]; cnt["ns"] += 1
                    S.pe(lambda e: e.matmul(out=sb_[:, 0:512], lhsT=kcmpT[:, kc * 128:(kc + 1) * 128],
                                            rhs=qTb[:].rearrange("p r t -> p (r t)"), start=True, stop=(m >= 16)),
                         r=[kcmpT, qTb], w=[sb_])
                    if m < 16:
                        S.pe(lambda e: e.matmul(out=sb_[:, 0:512].rearrange("p (r t) -> p r t", r=4), lhsT=identb[:],
                                                rhs=cmpneg[:, m, :].unsqueeze(1).to_broadcast([128, 4, 128]), start=False, stop=True),
                             r=[identb, cmpneg], w=[sb_])
                    E = Es[cnt["ne"] % 4]; cnt["ne"] += 1
                    S.act(lambda e: e.activation(out=E[:], in_=sb_[:, 0:512], func=AF.Exp, scale=SCALE_B), r=[sb_], w=[E])
                    st["E"] = E

                def s2():
                    E = st["E"]
                    for r in range(4):
                        bank = ocA if r in (0, 2) else ocB
                        c0, w_ = (0, 257) if r < 2 else (257, 129)
                        rhs_lo = 0 if r < 2 else 128
                        S.pe(lambda e, bank=bank, c0=c0, w_=w_, r=r, rhs_lo=rhs_lo: e.matmul(
                            out=bank[:, c0:c0 + w_], lhsT=E[:, r * 128:(r + 1) * 128], rhs=CV[:, kc, rhs_lo:257],
                            start=(kc == 0 and r < 2), stop=(kc == nkc - 1), skip_group_check=True), r=[E, CV], w=[bank])
                return (s1, s2)

            def kt_iter(kT, V, ob, lo, issel, kts, i=i, qTb=qTb):
                st = {}

                def s1():
                    sb_ = sbk[cnt["ns"] % 3]; cnt["ns"] += 1
                    for h_, kt in enumerate(kts):
                        cs = slice(h_ * 256, (h_ + 1) * 256)
                        nmask = (1 if issel else 0) + (1 if kt == i else 0) + (1 if (not issel and kt == i - 4) else 0)
                        S.pe(lambda e, kt=kt, cs=cs, nmask=nmask: e.matmul(out=sb_[:, cs], lhsT=kT[:, kt * 128:(kt + 1) * 128],
                                                                           rhs=qTb[:, 0:2, :].rearrange("p r t -> p (r t)"), start=True, stop=(nmask == 0)),
                             r=[kT.k(kt // 32), qTb], w=[sb_])
                        done = 0
                        if issel:
                            done += 1
                            S.pe(lambda e, kt=kt, cs=cs, last=(done == nmask): e.matmul(
                                out=sb_[:, cs].rearrange("p (r t) -> p r t", r=2), lhsT=Esel[:, kt * 128:(kt + 1) * 128],
                                rhs=biasT[:].unsqueeze(1).to_broadcast([128, 2, 128]), start=False, stop=last), r=[Esel.k(kt // 32), biasT], w=[sb_])
                        if kt == i:
                            done += 1
                            S.pe(lambda e, cs=cs, last=(done == nmask): e.matmul(
                                out=sb_[:, cs].rearrange("p (r t) -> p r t", r=2), lhsT=identb[:],
                                rhs=causneg[:].unsqueeze(1).to_broadcast([128, 2, 128]), start=False, stop=last), r=[identb, causneg], w=[sb_])
                        if (not issel) and kt == i - 4:
                            done += 1
                            S.pe(lambda e, cs=cs, last=(done == nmask): e.matmul(
                                out=sb_[:, cs].rearrange("p (r t) -> p r t", r=2), lhsT=identb[:],
                                rhs=winneg[:].unsqueeze(1).to_broadcast([128, 2, 128]), start=False, stop=last), r=[identb, winneg], w=[sb_])
                    P = Ps[cnt["np"] % 4]; cnt["np"] += 1
                    w_ = 256 * len(kts)
                    S.act(lambda e: e.activation(out=P[:, 0:w_], in_=sb_[:, 0:w_], func=AF.Exp, scale=SCALE_B), r=[sb_], w=[P])
                    st["P"] = P

                def s2():
                    P = st["P"]
                    for h_, kt in enumerate(kts):
                        for r in range(2):
                            S.pe(lambda e, r=r, kt=kt, h_=h_: e.matmul(out=ob[:, r * 129:(r + 1) * 129],
                                                                       lhsT=P[:, h_ * 256 + r * 128:h_ * 256 + (r + 1) * 128], rhs=V[:, kt, :],
                                                                       start=(kt == lo and r == 0), stop=(kt == i), skip_group_check=True),
                                 r=[P, V], w=[ob])
                return (s1, s2)

            def pairs(lo_, hi_):
                ks = list(range(lo_, hi_))
                return [ks[j:j + 2] for j in range(0, len(ks), 2)]

            pipelined([comp_iter(kc) for kc in range(nkc)])
            for (bank, rsc) in ((ocA, rscA), (ocB, rscB)):
                S.dve(lambda e, bank=bank, rsc=rsc: e.tensor_scalar(out=rsc[:], in0=bank[:, 256:386:129], scalar1=1e-30, scalar2=None, op0=ALU.max),
                      r=[bank], w=[rsc])
                S.dve(lambda e, rsc=rsc: e.reciprocal(out=rsc[:], in_=rsc[:]), r=[rsc], w=[rsc])
            so = 126 - 2 * i
            S.dve(lambda e, so=so: e.scalar_tensor_tensor(out=imp[:], in0=ocA[:, 128:256], scalar=rscA[:, 0:1], in1=selbase[:, so:so + 128],
                                                          op0=ALU.mult, op1=ALU.add), r=[ocA, rscA, selbase], w=[imp])
            S.dve(lambda e: e.scalar_tensor_tensor(out=imp[:], in0=ocB[:, 128:256], scalar=rscB[:, 0:1], in1=imp[:], op0=ALU.mult, op1=ALU.add),
                  r=[ocB, rscB, imp], w=[imp])
            S.dve(lambda e: e.scalar_tensor_tensor(out=imp[:], in0=ocA[:, 257:385], scalar=rscA[:, 1:2], in1=imp[:], op0=ALU.mult, op1=ALU.add),
                  r=[ocA, rscA, imp], w=[imp])
            S.dve(lambda e: e.scalar_tensor_tensor(out=imp[:], in0=ocB[:, 257:385], scalar=rscB[:, 1:2], in1=imp[:], op0=ALU.mult, op1=ALU.add),
                  r=[ocB, rscB, imp], w=[imp])
            S.dve(lambda e, i=i: e.tensor_tensor(out=coefc[:], in0=gsig[:, i, 0:6:3], in1=rscA[:, 0:1].to_broadcast([128, 2]), op=ALU.mult),
                  r=[gsig, rscA], w=[coefc])
            S.dve(lambda e, i=i: e.tensor_tensor(out=coefc[:, 1:2], in0=gsig[:, i, 3:4], in1=rscB[:, 0:1], op=ALU.mult), r=[gsig, rscB, coefc], w=[coefc])
            for r in range(2):
                bank = ocA if r == 0 else ocB
                S.dve(lambda e, r=r, bank=bank, o=o: e.tensor_scalar(out=o[:, r, :], in0=bank[:, 0:128], scalar1=coefc[:, r:r + 1], scalar2=None,
                                                                     op0=ALU.mult), r=[bank, coefc], w=[o.k(r)])
            S.dve(lambda e: e.memset(imp[:, 0:1], 100.0), r=[imp], w=[imp])
            S.dve(lambda e: e.max(out=m8a[:], in_=imp[:]), r=[imp], w=[m8a])
            S.dve(lambda e: e.match_replace(out=imp2[:], in_to_replace=m8a[:], in_values=imp[:], imm_value=-50.0), r=[imp, m8a], w=[imp2])
            S.dve(lambda e: e.max(out=m8b[:], in_=imp2[:]), r=[imp2], w=[m8b])
            S.dve(lambda e: e.tensor_scalar(out=sel[:], in0=imp[:], scalar1=m8b[:, 7:8], scalar2=None, op0=ALU.is_ge), r=[imp, m8b], w=[sel])
            S.dve(lambda e: e.scalar_tensor_tensor(out=sel[:], in0=imp[:], scalar=-50.0, in1=sel[:], op0=ALU.is_gt, op1=ALU.mult),
                  r=[imp, sel], w=[sel])
            lo_w = max(0, i - 4)
            pipelined([kt_iter(kwT, VW, owb, lo_w, False, kts) for kts in pairs(lo_w, i + 1)])
            S.pe(lambda e: e.transpose(out=p_t[:, 0:128], in_=sel[:], identity=identf[:]), r=[sel, identf], w=[p_t])
            S.dve(lambda e: e.tensor_scalar(out=biasT[:], in0=p_t[:, 0:128], scalar1=-1.0, scalar2=-NEG, op0=ALU.add, op1=ALU.mult),
                  r=[p_t], w=[biasT])
            pipelined([kt_iter(ksT, VS, osb, 0, True, kts) for kts in pairs(0, i + 1)])
            S.dve(lambda e: e.tensor_scalar(out=coef[:, 0:2], in0=osb[:, 128:258:129], scalar1=1e-30, scalar2=None, op0=ALU.max), r=[osb], w=[coef])
            S.dve(lambda e: e.tensor_scalar(out=coef[:, 2:4], in0=owb[:, 128:258:129], scalar1=1e-30, scalar2=None, op0=ALU.max), r=[owb], w=[coef])
            S.dve(lambda e: e.reciprocal(out=coef[:], in_=coef[:]), r=[coef], w=[coef])
            S.dve(lambda e, i=i: e.tensor_tensor(out=coef[:].rearrange("p (b r) -> p b r", r=2), in0=coef[:].rearrange("p (b r) -> p b r", r=2),
                                                 in1=gsig[:, i, :].rearrange("p (r b) -> p b r", r=2)[:, 1:3, :], op=ALU.mult), r=[coef, gsig], w=[coef])
            for r in range(2):
                S.dve(lambda e, r=r, o=o: e.scalar_tensor_tensor(out=o[:, r, :], in0=osb[:, r * 129:r * 129 + 128], scalar=coef[:, r:r + 1],
                                                                 in1=o[:, r, :], op0=ALU.mult, op1=ALU.add), r=[osb, coef, o.k(r)], w=[o.k(r)])
                S.dve(lambda e, r=r, o=o: e.scalar_tensor_tensor(out=o[:, r, :], in0=owb[:, r * 129:r * 129 + 128], scalar=coef[:, 2 + r:3 + r],
                                                                 in1=o[:, r, :], op0=ALU.mult, op1=ALU.add), r=[owb, coef, o.k(r)], w=[o.k(r)])
            S.act(lambda e, zt=zt: e.activation(out=sz[:], in_=zt[:], func=AF.Silu), r=[zt], w=[sz])
            S.dve(lambda e, o=o: e.tensor_tensor(out=yb[:], in0=o[:].rearrange("p r d -> p (r d)"), in1=sz[:], op=ALU.mult), r=[o, sz], w=[yb])
            ys = yst[tb % 2]
            for jx in range(2):
                S.pe(lambda e, jx=jx: e.transpose(out=p_t2v[:, jx, :], in_=yb[:, jx * 128:(jx + 1) * 128], identity=identb[:]),
                     r=[yb, identb], w=[p_t])
            S.act(lambda e, ys=ys, off=off: e.copy(out=ys[:, :, off:off + 128], in_=p_t2v), r=[p_t], w=[ys])
            if q4 == 3 or i == nchunk - 1:
                S.dma(lambda e, ys=ys, tb=tb: e.dma_start(out=yT_v[:, 2:4, tb * 512:(tb + 1) * 512], in_=ys[:]), r=[ys], w=[("yT_d", "b", tb)],
                      q=OUTQ)


def wtiles(W):
    K, N = W.shape
    return np.ascontiguousarray(W.reshape(K // 128, 128, N // 128, 128).transpose(2, 1, 0, 3))


def vec16(v):
    return np.ascontiguousarray(v.reshape(-1, 128).T.astype(np.float32))


class TokCtx:
    def __init__(self, S, nst=3, nbf=8):
        self.S = S
        self.wst = [S.sb("wst%d" % i, [128, 16, 128], F32) for i in range(nst)]
        self.wbf = [S.sb("wbf%d" % i, [128, 16, 128], BF16) for i in range(nbf)]
        self.sq = [S.sb("sqk%d" % i, [128, 512], BF16) for i in range(2)]
        self.ones = S.sb("ones", [128, 128], BF16)
        S.dve(lambda e: e.memset(self.ones[:], 1.0), w=[self.ones])
        self.nst = self.nbf = self.nsq = 0

    def wtile(self, tile_ap, nk=16):
        S = self.S
        st = self.wst[self.nst % len(self.wst)]; self.nst += 1
        wb = self.wbf[self.nbf % len(self.wbf)]; self.nbf += 1
        S.dma(lambda e: e.dma_start(out=st[:, 0:nk, :], in_=tile_ap), w=[st])
        S.act(lambda e: e.copy(out=wb[:, 0:nk, :], in_=st[:, 0:nk, :]), r=[st], w=[wb])
        return wb

    def sumsq_add(self, ps_ss, src_ap, rkeys, first, last):
        S = self.S
        sq = self.sq[self.nsq % 2]; self.nsq += 1
        S.act(lambda e: e.activation(out=sq[:], in_=src_ap, func=AF.Square), r=rkeys, w=[sq])
        S.pe(lambda e: e.matmul(out=ps_ss[:, 0:512], lhsT=self.ones[:], rhs=sq[:], start=first, stop=last), r=[self.ones, sq], w=[ps_ss])

    def rstd_from(self, ps_ss, out_bc):
        S = self.S
        S.dve(lambda e: e.tensor_scalar(out=out_bc[:], in0=ps_ss[:, 0:512], scalar1=1.0 / D, scalar2=EPS, op0=ALU.mult, op1=ALU.add),
              r=[ps_ss], w=[out_bc])
        S.act(lambda e: e.activation(out=out_bc[:], in_=out_bc[:], func=AF.Sqrt), r=[out_bc], w=[out_bc])
        S.dve(lambda e: e.reciprocal(out=out_bc[:], in_=out_bc[:]), r=[out_bc], w=[out_bc])


def emit_p1(S, xT_d, gpre_d, hT_d):
    xT_v = xT_d.rearrange("(k p) t -> p k t", p=128)
    hT_v = hT_d.rearrange("(k p) t -> p k t", p=128)
    for half in range(TS // 512):
        t0 = half * 512
        with S.phase():
            tc = TokCtx(S, nst=1, nbf=1)
            gpre = load_const(S, "gpre", gpre_d, [128, 16], F32)
            xT = S.sb("p1_xT", [128, 16, 512], F32)
            hT = S.sb("p1_hT", [128, 16, 512], BF16)
            rstd = S.sb("p1_rstd", [128, 512], F32)
            ps_ss = S.ps("p1_ss", [128, 512], F32)
            for q in range(4):
                S.dma(lambda e, q=q: e.dma_start(out=xT[:, q * 4:(q + 1) * 4, :], in_=xT_v[:, q * 4:(q + 1) * 4, t0:t0 + 512]), w=[xT.k(q)])
            for k in range(16):
                tc.sumsq_add(ps_ss, xT[:, k, :], [xT.k(k // 4)], k == 0, k == 15)
            tc.rstd_from(ps_ss, rstd)
            for k in range(16):
                S.dve(lambda e, k=k: e.scalar_tensor_tensor(out=hT[:, k, :], in0=xT[:, k, :], scalar=gpre[:, k:k + 1], in1=rstd[:],
                                                            op0=ALU.mult, op1=ALU.mult), r=[xT.k(k // 4), gpre, rstd], w=[hT.k(k // 4)])
            for q in range(4):
                S.dma(lambda e, q=q: e.dma_start(out=hT_v[:, q * 4:(q + 1) * 4, t0:t0 + 512], in_=hT[:, q * 4:(q + 1) * 4, :]),
                      r=[hT.k(q)], w=[("hT_d", half, q)], q=OUTQ)


def emit_p4(S, xT_d, Yg_d, pT_d, WD, VD, xo_d, pers, nhalf=2, halves=None):
    if pers is None:
        for half in range(nhalf):
            with S.scope():
                xT = S.sb("xT_res", [128, 16, 512], F32, persist=True)
                mT = S.sb("mT", [128, 16, 512], BF16, persist=True)
                emit_p4(S, xT_d, Yg_d, pT_d, WD, VD, xo_d, (xT, mT), nhalf=nhalf, halves=[half])
        return
    xT, mT = pers
    xT_v = xT_d.rearrange("(k p) t -> p k t", p=128)
    xo_v = xo_d.rearrange("(k p) t -> p k t", p=128)
    Y_ind = isinstance(Yg_d, tuple)
    if not Y_ind:
        Y_v = Yg_d.rearrange("(k p) t -> p k t", p=128)
    pT_v = pT_d.rearrange("(k p) t -> p k t", p=128)
    for half in (halves if halves is not None else range(nhalf)):
        t0 = half * 512
        with S.phase():
            tc = TokCtx(S, nst=4, nbf=8)
            gpre = load_const(S, "gpre", VD["gpre"], [128, 16], F32)
            gssm = load_const(S, "gssm", VD["gssm"], [128, 16], F32)
            hT = S.sb("hT", [128, 16, 512], BF16)
            yT = S.sb("yT", [128, 48, 512], BF16)
            rstd = S.sb("rstd", [128, 512], F32)
            rstd_a = S.sb("rstd_a", [128, 512], F32)
            ps_ss = S.ps("ps_ss", [128, 512], F32)
            ps_g = [S.ps("ps_g%d" % i, [128, 512], F32) for i in range(2)]
            ps_u = [S.ps("ps_u%d" % i, [128, 512], F32) for i in range(2)]
            sg = [S.sb("sg%d" % i, [128, 512], F32) for i in range(2)]
            tj = [S.sb("tj%d" % i, [128, 512], F32) for i in range(3)]
            for q in range(4):
                S.dma(lambda e, q=q: e.dma_start(out=xT[:, q * 4:(q + 1) * 4, :], in_=xT_v[:, q * 4:(q + 1) * 4, t0:t0 + 512]), w=[xT.k(q)])
            if Y_ind:
                Grows, yidx_d = Yg_d
                yidx = S.sb("yidx", [128, 96], mybir.dt.int32)
                S.dma(lambda e: e.dma_start(out=yidx[:], in_=yidx_d), w=[yidx])
                for k in range(48):
                    S.dma(lambda e, k=k: e.indirect_dma_start(
                        out=yT[:, k, :], out_offset=None, in_=Grows[:, :],
                        in_offset=bass.IndirectOffsetOnAxis(ap=yidx[:, half * 48 + k:half * 48 + k + 1], axis=0)),
                        r=[yidx], w=[yT.k(k // 4)], q="pool")
            else:
                for q in range(12):
                    S.dma(lambda e, q=q: e.dma_start(out=yT[:, q * 4:(q + 1) * 4, :], in_=Y_v[:, q * 4:(q + 1) * 4, t0:t0 + 512]), w=[yT.k(q)])
            for k in range(16):
                tc.sumsq_add(ps_ss, xT[:, k, :], [xT.k(k // 4)], k == 0, k == 15)
            tc.rstd_from(ps_ss, rstd)
            for k in range(16):
                S.dve(lambda e, k=k: e.scalar_tensor_tensor(out=hT[:, k, :], in0=xT[:, k, :], scalar=gpre[:, k:k + 1], in1=rstd[:],
                                                            op0=ALU.mult, op1=ALU.mult), r=[xT.k(k // 4), gpre, rstd], w=[hT.k(k)])
            for k in range(16):
                tc.sumsq_add(ps_ss, yT[:, k, :], [yT.k(k // 4)], k == 0, k == 15)
            tc.rstd_from(ps_ss, rstd_a)
            for k in range(16):
                S.pool(lambda e, k=k: e.tensor_scalar(out=yT[:, k, :], in0=yT[:, k, :], scalar1=gssm[:, k:k + 1], scalar2=0.0,
                                                      op0=ALU.mult, op1=ALU.add), r=[yT.k(k // 4), gssm], w=[yT.k(k // 4)])
            ng = 0
            for dt_ in range(16):
                for j in range(3):
                    wg_ = tc.wtile(WD["gm"][j * 16 + dt_])
                    wu_ = tc.wtile(WD["wb"][j * 16 + dt_])
                    pg = ps_g[ng % 2]; pu = ps_u[ng % 2]; sgb = sg[ng % 2]; ng += 1
                    for k in range(16):
                        S.pe(lambda e, pg=pg, wg_=wg_, k=k: e.matmul(out=pg[:, :], lhsT=wg_[:, k, :], rhs=hT[:, k, :], start=(k == 0), stop=(k == 15)),
                             r=[wg_, hT.k(k)], w=[pg])
                    for k in range(16):
                        S.pe(lambda e, pu=pu, wu_=wu_, k=k, j=j: e.matmul(out=pu[:, :], lhsT=wu_[:, k, :], rhs=yT[:, 16 * j + k, :],
                                                                        start=(k == 0), stop=(k == 15)), r=[wu_, yT.k((16 * j + k) // 4)], w=[pu])
                    S.act(lambda e, pg=pg, sgb=sgb: e.activation(out=sgb[:], in_=pg[:, :], func=AF.Sigmoid), r=[pg], w=[sgb])
                    S.dve(lambda e, pu=pu, sgb=sgb, j=j: e.tensor_tensor(out=tj[j][:], in0=pu[:, :], in1=sgb[:], op=ALU.mult), r=[pu, sgb], w=[tj[j]])
                S.pool(lambda e: e.tensor_tensor(out=tj[0][:], in0=tj[0][:], in1=rstd_a[:], op=ALU.mult), r=[tj[0], rstd_a], w=[tj[0]])
                S.pool(lambda e: e.tensor_tensor(out=tj[1][:], in0=tj[1][:], in1=tj[2][:], op=ALU.add), r=[tj[1], tj[2]], w=[tj[1]])
                S.pool(lambda e, dt_=dt_: e.tensor_tensor(out=mT[:, dt_, :], in0=tj[0][:], in1=tj[1][:], op=ALU.add), r=[tj[0], tj[1]], w=[mT.k(dt_)])
        with S.phase():
            tc = TokCtx(S, nst=6, nbf=6)
            gpost = load_const(S, "gpost", VD["gpost"], [128, 16], F32)
            oT = S.sb("oT", [128, 16, 512], F32)
            rstd = S.sb("rstd", [128, 512], F32)
            tmp = [S.sb("tmp%d" % i, [128, 512], F32) for i in range(2)]
            ps_ss = S.ps("ps_ss", [128, 512], F32)
            ps_o = [S.ps("ps_o%d" % i, [128, 512], F32) for i in range(2)]
            for d2 in range(16):
                wo_ = tc.wtile(WD["wo"][d2])
                po = ps_o[d2 % 2]
                for k in range(16):
                    S.pe(lambda e, po=po, wo_=wo_, k=k: e.matmul(out=po[:, :], lhsT=wo_[:, k, :], rhs=mT[:, k, :], start=(k == 0), stop=(k == 15)),
                         r=[wo_, mT.k(k)], w=[po])
                S.act(lambda e, po=po, d2=d2: e.copy(out=oT[:, d2, :], in_=po[:, :]), r=[po], w=[oT.k(d2)])
                tc.sumsq_add(ps_ss, oT[:, d2, :], [oT.k(d2)], d2 == 0, d2 == 15)
            tc.rstd_from(ps_ss, rstd)
            for k in range(16):
                tm_ = tmp[k % 2]
                S.dve(lambda e, k=k, tm_=tm_: e.scalar_tensor_tensor(out=tm_[:], in0=oT[:, k, :], scalar=gpost[:, k:k + 1], in1=rstd[:],
                                                                     op0=ALU.mult, op1=ALU.mult), r=[oT.k(k), gpost, rstd], w=[tm_])
                S.pool(lambda e, k=k, tm_=tm_: e.tensor_tensor(out=xT[:, k, :], in0=xT[:, k, :], in1=tm_[:], op=ALU.add), r=[xT.k(k // 4), tm_],
                       w=[xT.k(k // 4)])
        with S.phase():
            tc = TokCtx(S, nst=6, nbf=6)
            gple = load_const(S, "gple", VD["gple"], [128, 16], F32)
            xn = S.sb("xn", [128, 16, 512], BF16)
            geT = S.sb("geT", [128, 16, 512], F32)
            rstd = S.sb("rstd", [128, 512], F32)
            pTf = S.sb("pTf", [128, 2, 512], F32)
            pTb = S.sb("pTb", [128, 2, 512], BF16)
            tmp = [S.sb("tmp%d" % i, [128, 512], F32) for i in range(2)]
            sgc = [S.sb("sgc%d" % i, [128, 512], F32) for i in range(2)]
            ps_ss = S.ps("ps_ss", [128, 512], F32)
            ps_g = [S.ps("ps_g%d" % i, [128, 512], F32) for i in range(2)]
            ps_e = [S.ps("ps_e%d" % i, [128, 512], F32) for i in range(2)]
            S.dma(lambda e: e.dma_start(out=pTf[:], in_=pT_v[:, :, t0:t0 + 512]), w=[pTf])
            S.pool(lambda e: e.tensor_copy(out=pTb[:], in_=pTf[:]), r=[pTf], w=[pTb])
            for k in range(16):
                tc.sumsq_add(ps_ss, xT[:, k, :], [xT.k(k // 4)], k == 0, k == 15)
            tc.rstd_from(ps_ss, rstd)
            for k in range(16):
                S.dve(lambda e, k=k: e.tensor_tensor(out=xn[:, k, :], in0=xT[:, k, :], in1=rstd[:], op=ALU.mult), r=[xT.k(k // 4), rstd], w=[xn.k(k)])
            for d2 in range(16):
                wg_ = tc.wtile(WD["wg"][d2])
                wp_ = tc.wtile(WD["wp"][d2], nk=2)
                pg = ps_g[d2 % 2]; pe_ = ps_e[d2 % 2]; sgb = sgc[d2 % 2]
                for k in range(16):
                    S.pe(lambda e, pg=pg, wg_=wg_, k=k: e.matmul(out=pg[:, :], lhsT=wg_[:, k, :], rhs=xn[:, k, :], start=(k == 0), stop=(k == 15)),
                         r=[wg_, xn.k(k)], w=[pg])
                for k in range(2):
                    S.pe(lambda e, pe_=pe_, wp_=wp_, k=k: e.matmul(out=pe_[:, :], lhsT=wp_[:, k, :], rhs=pTb[:, k, :], start=(k == 0), stop=(k == 1)),
                         r=[wp_, pTb], w=[pe_])
                S.act(lambda e, pg=pg, sgb=sgb: e.activation(out=sgb[:], in_=pg[:, :], func=AF.Sigmoid), r=[pg], w=[sgb])
                S.dve(lambda e, pe_=pe_, sgb=sgb, d2=d2: e.tensor_tensor(out=geT[:, d2, :], in0=pe_[:, :], in1=sgb[:], op=ALU.mult),
                      r=[pe_, sgb], w=[geT.k(d2)])
                tc.sumsq_add(ps_ss, geT[:, d2, :], [geT.k(d2)], d2 == 0, d2 == 15)
            tc.rstd_from(ps_ss, rstd)
            for k in range(16):
                tm_ = tmp[k % 2]
                S.dve(lambda e, k=k, tm_=tm_: e.scalar_tensor_tensor(out=tm_[:], in0=geT[:, k, :], scalar=gple[:, k:k + 1], in1=rstd[:],
                                                                     op0=ALU.mult, op1=ALU.mult), r=[geT.k(k), gple, rstd], w=[tm_])
                S.pool(lambda e, k=k, tm_=tm_: e.tensor_tensor(out=xT[:, k, :], in0=xT[:, k, :], in1=tm_[:], op=ALU.add), r=[xT.k(k // 4), tm_],
                       w=[xT.k(k // 4)])
            for q in range(4):
                S.dma(lambda e, q=q: e.dma_start(out=xo_v[:, q * 4:(q + 1) * 4, t0:t0 + 512], in_=xT[:, q * 4:(q + 1) * 4, :]),
                      r=[xT.k(q)], w=[("xo_d", half, q)], q=OUTQ)


def p4_inputs(inp, L):
    w_in = inp["w_in"][L]
    d = {}
    d["w_gm"] = wtiles(w_in[:, O_GM:O_GM + 3 * D])
    d["w_wb"] = np.concatenate([wtiles(inp["w_branch"][L, j]) for j in range(3)], axis=0)
    d["w_wo"] = wtiles(inp["w_out"][L])
    d["w_wg"] = wtiles(inp["ple_gate"][L])
    d["w_wp"] = wtiles(inp["ple_proj"][L])
    d["v_gpre"] = vec16(inp["norm_pre"][L])
    d["v_gpost"] = vec16(inp["norm_post"][L])
    d["v_gssm"] = vec16(inp["ssm_norm"][L])
    d["v_gple"] = vec16(inp["ple_norm"][L])
    return d


_CONST_CACHE = {}


def shared_consts():
    if "c" not in _CONST_CACHE:
        _CONST_CACHE["c"] = make_consts()
    return _CONST_CACHE["c"]


def _dt_of(v):
    return BF16 if v.dtype == NPBF else F32


def build_LA():
    nc = bass.Bass("TRN2", target_bir_lowering=False)
    xT = nc.dram_tensor("xT", [D, TS], F32, kind="ExternalInput").ap()
    g = nc.dram_tensor("v_gpre", [128, 16], F32, kind="ExternalInput").ap()
    hT = nc.dram_tensor("hT_out", [D, TS], BF16, kind="ExternalOutput").ap()
    S = Sched(nc)
    emit_p1(S, xT, g, hT)
    S.finish()
    return nc


def mixer_const_specs():
    sc = shared_consts()
    specs = {k: (list(v.shape), _dt_of(v)) for k, v in sc.items()}
    specs.update({"ssdp": ([128, 16], F32), "convp": ([128, 4, 5], F32), "retp": ([128, 8], F32), "gnw": ([128, 256], F32),
                  "cmp_w1": ([2, 4096, 256], F32), "cmp_w2": ([2, 256, 128], F32), "cmp_pe": ([2, 32, 128], F32)})
    return specs


def build_LB():
    nc = bass.Bass("TRN2", target_bir_lowering=False)
    hT = nc.dram_tensor("hT", [D, T], BF16, kind="ExternalInput").ap()
    wfm = nc.dram_tensor("wfm", [D, NFM], F32, kind="ExternalInput").ap()
    wtm = nc.dram_tensor("wtm", [D, NTM + 16], F32, kind="ExternalInput").ap()
    CD = {k: nc.dram_tensor(k, shp, dt, kind="ExternalInput").ap() for k, (shp, dt) in mixer_const_specs().items()}
    FM = nc.dram_tensor("FM_s", [NFM, T], BF16, kind="Internal").ap()
    TM = nc.dram_tensor("TM_s", [T, NTM], BF16, kind="Internal").ap()
    SM = nc.dram_tensor("SM_s", [128, 64, 16], F32, kind="Internal").ap()
    yT = nc.dram_tensor("yT", [768, T], BF16, kind="ExternalOutput").ap()
    S = Sched(nc)
    kcmpT = S.sb("kcmpT", [128, 512], BF16, persist=True)
    CV = S.sb("CV", [128, 4, 257], BF16, persist=True)
    emit_p2(S, hT, wfm, wtm, FM, TM, SM)
    emit_ssd(S, FM, TM, SM, CD, yT)
    emit_ret(S, FM, TM, SM, CD, yT)
    emit_nsa(S, FM, TM, SM, CD, yT, persist=(kcmpT, CV))
    S.finish()
    return nc


def build_LC():
    nc = bass.Bass("TRN2", target_bir_lowering=False)
    xT = nc.dram_tensor("xT", [D, TS], F32, kind="ExternalInput").ap()
    Yg = nc.dram_tensor("Yg", [3 * D, TS], BF16, kind="ExternalInput").ap()
    pT = nc.dram_tensor("pT", [P_DIM, TS], F32, kind="ExternalInput").ap()
    WD = {"gm": nc.dram_tensor("w_gm", [48, 128, 16, 128], F32, kind="ExternalInput").ap(),
          "wb": nc.dram_tensor("w_wb", [48, 128, 16, 128], F32, kind="ExternalInput").ap(),
          "wo": nc.dram_tensor("w_wo", [16, 128, 16, 128], F32, kind="ExternalInput").ap(),
          "wg": nc.dram_tensor("w_wg", [16, 128, 16, 128], F32, kind="ExternalInput").ap(),
          "wp": nc.dram_tensor("w_wp", [16, 128, 2, 128], F32, kind="ExternalInput").ap()}
    VD = {k: nc.dram_tensor("v_" + k, [128, 16], F32, kind="ExternalInput").ap() for k in ("gpre", "gpost", "gssm", "gple")}
    xo = nc.dram_tensor("xo", [D, TS], F32, kind="ExternalOutput").ap()
    S = Sched(nc)
    xTs = S.sb("xT_res", [128, 16, 512], F32, persist=True)
    mT = S.sb("mT", [128, 16, 512], BF16, persist=True)
    emit_p4(S, xT, Yg, pT, WD, VD, xo, (xTs, mT))
    S.finish()
    return nc


def lb_inputs(c, L, inp, hT_full):
    w_in = inp["w_in"][L]
    fm, tm, sm = core_cols(c)
    wtm = np.zeros((D, NTM + 16), np.float32)
    wtm[:, :NTM] = w_in[:, tm]
    wtm[:, NTM:NTM + NSM] = w_in[:, sm]
    im = {"hT": hT_full, "wfm": np.ascontiguousarray(w_in[:, fm]), "wtm": wtm}
    im.update(shared_consts())
    im.update(core_consts(c, L, inp))
    im["cmp_w1"] = inp["cmp_w1"][L]
    im["cmp_w2"] = inp["cmp_w2"][L]
    im["cmp_pe"] = inp["cmp_pe"][L]
    return im


def kernel_multi(**inputs):
    inp = {k: np.asarray(v) for k, v in inputs.items()}
    cores = list(range(NCORE))
    xT = np.ascontiguousarray(inp["x"][0].T)
    ncA, ncB, ncC = build_LA(), build_LB(), build_LC()
    for L in range(DEPTH):
        shared4 = p4_inputs(inp, L)
        ims = [{"xT": np.ascontiguousarray(xT[:, c * TS:(c + 1) * TS]), "v_gpre": shared4["v_gpre"]} for c in cores]
        res = run_bass_kernel_spmd(ncA, ims, core_ids=cores)
        hT_full = np.ascontiguousarray(np.concatenate([np.asarray(r["hT_out"]) for r in res.results], axis=1))
        ims = [lb_inputs(c, L, inp, hT_full) for c in cores]
        res = run_bass_kernel_spmd(ncB, ims, core_ids=cores)
        yTs = [np.asarray(r["yT"]) for r in res.results]
        Yfull = np.concatenate([np.concatenate([yTs[c][256 * j:256 * (j + 1)] for c in cores], axis=0) for j in range(3)], axis=0)
        pT = np.ascontiguousarray(inp["p"][L, 0].T)
        ims = []
        for c in cores:
            im = {"xT": np.ascontiguousarray(xT[:, c * TS:(c + 1) * TS]), "Yg": np.ascontiguousarray(Yfull[:, c * TS:(c + 1) * TS]),
                  "pT": np.ascontiguousarray(pT[:, c * TS:(c + 1) * TS])}
            im.update(shared4)
            ims.append(im)
        res = run_bass_kernel_spmd(ncC, ims, core_ids=cores)
        xT = np.ascontiguousarray(np.concatenate([np.asarray(r["xo"]) for r in res.results], axis=1))
    return np.ascontiguousarray(xT.T)[None].astype(np.float32)


PER_LAYER_MIX = {"ssdp": ([128, 16], F32), "convp": ([128, 4, 5], F32), "gnw": ([128, 256], F32),
                 "cmp_w1": ([2, 4096, 256], F32), "cmp_w2": ([2, 256, 128], F32), "cmp_pe": ([2, 32, 128], F32)}
P4_SPECS = {"w_gm": [48, 128, 16, 128], "w_wb": [48, 128, 16, 128], "w_wo": [16, 128, 16, 128], "w_wg": [16, 128, 16, 128],
            "w_wp": [16, 128, 2, 128], "v_gpre": [128, 16], "v_gpost": [128, 16], "v_gssm": [128, 16], "v_gple": [128, 16]}


def build_fused(stop=99):
    nc = bass.Bass("TRN2", target_bir_lowering=False)
    I32 = mybir.dt.int32
    ext = lambda name, shp, dt: nc.dram_tensor(name, list(shp), dt, kind="ExternalInput").ap()
    xT_in = ext("xT", [D, TS], F32)
    xo = nc.dram_tensor("xo", [D, TS], F32, kind="ExternalOutput").ap()
    xres = nc.dram_tensor("xres", [D, TS], F32, kind="Internal").ap()
    hT_loc = nc.dram_tensor("hT_loc", [D, TS], BF16, kind="Internal").ap()
    hT_all = nc.dram_tensor("hT_all", [NCORE * D, TS], BF16, kind="Internal", addr_space="Shared").ap()
    FM = nc.dram_tensor("FM_s", [NFM, T], BF16, kind="Internal").ap()
    TM = nc.dram_tensor("TM_s", [T, NTM], BF16, kind="Internal").ap()
    SM = nc.dram_tensor("SM_s", [128, 64, 16], F32, kind="Internal").ap()
    yT_loc = nc.dram_tensor("yT_loc", [768, T], BF16, kind="Internal").ap()
    Y_all = nc.dram_tensor("Y_all", [NCORE * 768, T], BF16, kind="Internal", addr_space="Shared").ap()
    Y_cp = nc.dram_tensor("Y_cp", [NCORE * 768, T], BF16, kind="Internal").ap()
    yidx_d = ext("yidx", [128, 96], I32)
    sc = shared_consts()
    CDs = {k: ext(k, v.shape, _dt_of(v)) for k, v in sc.items()}
    CDs["retp"] = ext("retp", [128, 8], F32)
    S = Sched(nc)
    pad_ = S.sb("lowpad", [128, 1024], F32, persist=True)
    kcmpT_p = S.sb("kcmpT", [128, 512], BF16, persist=True)
    CV_p = S.sb("CV", [128, 4, 257], BF16, persist=True)
    hv = hT_all.rearrange("(r k p) t -> p r k t", r=NCORE, p=128)
    hsrc = lambda tb, q: hv[:, tb // 2, q * 4:(q + 1) * 4, (tb % 2) * 512:(tb % 2) * 512 + 512]
    Grows = Y_cp.rearrange("r (tb t) -> (r tb) t", t=512)
    groups = [list(range(NCORE))]
    for L in range(DEPTH):
        sfx = "_L%d" % L
        CD = dict(CDs)
        for k, (shp, dt) in PER_LAYER_MIX.items():
            CD[k] = ext(k + sfx, shp, dt)
        wfm = ext("wfm" + sfx, [D, NFM], F32)
        wtm = ext("wtm" + sfx, [D, NTM + 16], F32)
        pT = ext("pT" + sfx, [P_DIM, TS], F32)
        P4 = {k: ext(k + sfx, shp, F32) for k, shp in P4_SPECS.items()}
        WD = {"gm": P4["w_gm"], "wb": P4["w_wb"], "wo": P4["w_wo"], "wg": P4["w_wg"], "wp": P4["w_wp"]}
        VD = {"gpre": P4["v_gpre"], "gpost": P4["v_gpost"], "gssm": P4["v_gssm"], "gple": P4["v_gple"]}
        x_src = xT_in if L == 0 else xres
        x_dst = xo if L == DEPTH - 1 else xres
        emit_p1(S, x_src, VD["gpre"], hT_loc)
        if stop == 0:
            break
        with S.phase():
            S.cc(lambda e: e.collective_compute("AllGather", ALU.bypass, replica_groups=groups, ins=[hT_loc.opt()], outs=[hT_all.opt()]),
                 r=["hT_loc"], w=["hT_all"])
        if stop == 1:
            break
        emit_p2(S, hsrc, wfm, wtm, FM, TM, SM)
        if stop == 2:
            break
        import os as _os
        _skip = _os.environ.get("FUSED_SKIP", "")
        if "ssd" not in _skip:
            emit_ssd(S, FM, TM, SM, CD, yT_loc)
        if "ret" not in _skip:
            emit_ret(S, FM, TM, SM, CD, yT_loc)
        if "nsa" not in _skip:
            emit_nsa(S, FM, TM, SM, CD, yT_loc, persist=(kcmpT_p, CV_p))
        if stop == 3:
            break
        with S.phase():
            S.cc(lambda e: e.collective_compute("AllGather", ALU.bypass, replica_groups=groups, ins=[yT_loc.opt()], outs=[Y_all.opt()]),
                 r=["yT_loc"], w=["Y_all"])
        with S.phase():
            for q in range(48):
                S.dma(lambda e, q=q: e.dma_start(out=Y_cp[q * 128:(q + 1) * 128, :], in_=Y_all[q * 128:(q + 1) * 128, :]),
                      r=["Y_all"], w=[("Y_cp", q)])
        if stop == 4:
            break
        emit_p4(S, x_src, (Grows, yidx_d), pT, WD, VD, x_dst, None)
        if stop == 5:
            break
    if stop < 99:
        with S.phase():
            t = S.sb("dbg_t", [128, 512], F32)
            S.dve(lambda e: e.memset(t[:], 1.0), w=[t])
            S.dma(lambda e: e.dma_start(out=xo[0:128, 0:512], in_=t[:]), r=[t], w=["xo_dbg"])
    S.finish()
    return nc


def y_index(c):
    idx = np.zeros((128, 96), np.int32)
    p = np.arange(128)
    for h in range(2):
        for j in range(3):
            for cs in range(NCORE):
                for rr in range(2):
                    k = 16 * j + 2 * cs + rr
                    idx[:, h * 48 + k] = (cs * 768 + 256 * j + 128 * rr + p) * 16 + (2 * c + h)
    return idx


def fused_inputs(c, inp):
    im = {"xT": np.ascontiguousarray(inp["x"][0][c * TS:(c + 1) * TS].T), "yidx": y_index(c)}
    im.update(shared_consts())
    for L in range(DEPTH):
        sfx = "_L%d" % L
        w_in = inp["w_in"][L]
        fm, tm, sm = core_cols(c)
        wtm = np.zeros((D, NTM + 16), np.float32)
        wtm[:, :NTM] = w_in[:, tm]
        wtm[:, NTM:NTM + NSM] = w_in[:, sm]
        im["wfm" + sfx] = np.ascontiguousarray(w_in[:, fm])
        im["wtm" + sfx] = wtm
        cc_ = core_consts(c, L, inp)
        im["retp"] = cc_.pop("retp")
        for k, v in cc_.items():
            im[k + sfx] = v
        im["cmp_w1" + sfx] = inp["cmp_w1"][L]
        im["cmp_w2" + sfx] = inp["cmp_w2"][L]
        im["cmp_pe" + sfx] = inp["cmp_pe"][L]
        im["pT" + sfx] = np.ascontiguousarray(inp["p"][L, 0][c * TS:(c + 1) * TS].T)
    return im


def kernel(**inputs):
    inp = {k: np.asarray(v) for k, v in inputs.items()}
    cores = list(range(NCORE))
    nc = build_fused()
    shared4 = []
    for L in range(DEPTH):
        d = p4_inputs(inp, L)
        shared4.append({k + "_L%d" % L: v for k, v in d.items()})
    ims = []
    for c in cores:
        im = fused_inputs(c, inp)
        for d in shared4:
            im.update(d)
        ims.append(im)
    res = run_bass_kernel_spmd(nc, ims, core_ids=cores)
    xT = np.concatenate([np.asarray(r["xo"]) for r in res.results], axis=1)
    return np.ascontiguousarray(xT.T)[None].astype(np.float32)
```
